# Optimizing a Trainium2 kernel written in Bass

```python
import math
import numpy as np
import jax, jax.numpy as jnp
from jax import lax

D_MODEL = 2048
BATCH = 2
SEQ = 4096
DEPTH = 2

HEAD_DIM = 128
ROPE_THETA = 500000.0
ROPE_FRACTION_DIV = 4
Q_BLK = 128
NEG = -1e30
FORCE = 1e6

A_HEADS = 8
A_LATENT = 512
IDX_HEADS = 16
IDX_DIM = 64
DSA_TOPK = 256

B_HEADS = 8
B_KV_GROUPS = 2
B_HPG = B_HEADS // B_KV_GROUPS
CMP_LEN = 32
CMP_STRIDE = 16
SLC_LEN = 64
SLC_TOPN = 16
WIN_LEN = 512

C_HEADS = 4
C_DIM = 128

BRANCH_W = A_HEADS * HEAD_DIM
N_BRANCH = 3

D_FF = int(math.ceil(8 * D_MODEL / 3 / 256)) * 256

ALPHA = (2 * DEPTH) ** 0.25
BETA = (8 * DEPTH) ** -0.25

A_Q_W = A_HEADS * HEAD_DIM
IDX_Q_W = IDX_HEADS * IDX_DIM
B_Q_W = B_HEADS * HEAD_DIM
B_KV_W = 3 * 2 * B_KV_GROUPS * HEAD_DIM
B_GATE_W = 3 * B_HEADS
C_QK_W = C_HEADS * 2 * C_DIM
C_V_W = C_HEADS * 2 * C_DIM
MERGE_GATE_W = N_BRANCH * D_MODEL
IN_SPLITS = (A_Q_W, A_LATENT, IDX_Q_W, IDX_DIM, IDX_HEADS,
             B_Q_W, B_KV_W, B_GATE_W,
             C_QK_W, C_QK_W, C_V_W,
             MERGE_GATE_W)
N_IN = sum(IN_SPLITS)
IN_OFFSETS = tuple(int(o) for o in np.cumsum(IN_SPLITS)[:-1])

kernel_name = 'hybrid_dsa_nsa_diffattn_deepnorm_adaln'

f32 = jnp.float32


def layer_norm(x, g=None, b=None, eps=1e-5):
    xf = x.astype(f32)
    mu = xf.mean(-1, keepdims=True)
    var = jnp.mean(jnp.square(xf - mu), -1, keepdims=True)
    y = (xf - mu) * lax.rsqrt(var + eps)
    if g is not None:
        y = y * g.astype(f32) + b.astype(f32)
    return y.astype(x.dtype)


def rms_norm(x, g, eps=1e-6):
    xf = x.astype(f32)
    y = xf * lax.rsqrt(jnp.mean(jnp.square(xf), -1, keepdims=True) + eps)
    return (y * g.astype(f32)).astype(x.dtype)


def partial_rope(x, pos):
    d = x.shape[-1]
    r = d // ROPE_FRACTION_DIV
    half = r // 2
    inv = ROPE_THETA ** (-(jnp.arange(half, dtype=f32) * 2.0) / r)
    ang = pos.astype(f32)[:, None] * inv[None, :]
    cos = jnp.cos(ang)[:, None, :]
    sin = jnp.sin(ang)[:, None, :]
    x1 = x[..., :half].astype(f32)
    x2 = x[..., half:r].astype(f32)
    rot = jnp.concatenate([x1 * cos - x2 * sin, x2 * cos + x1 * sin], -1).astype(x.dtype)
    return jnp.concatenate([rot, x[..., r:]], -1)


def masked_softmax(s, mask):
    s = jnp.where(mask, s.astype(f32), NEG)
    s = s - s.max(-1, keepdims=True)
    p = jnp.exp(s) * mask
    return p / jnp.maximum(p.sum(-1, keepdims=True), 1e-30)


def sweep_query_blocks(block_fn, L):
    out = lax.map(block_fn, jnp.arange(L // Q_BLK))
    n, B, q, w = out.shape
    return out.transpose(1, 0, 2, 3).reshape(B, n * q, w)


def dsa_attention(q, k, v, iq, ik, iw):
    B, L = q.shape[:2]
    k_sel = min(DSA_TOPK, L // 4)
    scale = HEAD_DIM ** -0.5
    idx_scale = IDX_DIM ** -0.5
    key_pos = jnp.arange(L)
    gather = jax.vmap(lambda arr, ii: arr[ii])

    def block(j):
        t = j * Q_BLK + jnp.arange(Q_BLK)
        qb = lax.dynamic_slice_in_dim(q, j * Q_BLK, Q_BLK, 1)
        iqb = lax.dynamic_slice_in_dim(iq, j * Q_BLK, Q_BLK, 1)
        iwb = lax.dynamic_slice_in_dim(iw, j * Q_BLK, Q_BLK, 1)
        rel = jax.nn.relu(jnp.einsum('bqhd,bsd->bqhs', iqb, ik) * idx_scale)
        score = jnp.einsum('bqh,bqhs->bqs', iwb, rel).astype(f32)
        causal = key_pos[None, :] <= t[:, None]
        score = jnp.where(causal, score, NEG)
        _, idx = lax.top_k(score, k_sel)
        valid = idx <= t[None, :, None]
        kg = gather(k, idx)
        vg = gather(v, idx)
        s = jnp.einsum('bqhd,bqkhd->bhqk', qb, kg) * scale
        p = masked_softmax(s, valid[:, None])
        o = jnp.einsum('bhqk,bqkhd->bqhd', p, vg)
        return o.reshape(B, Q_BLK, A_HEADS * HEAD_DIM)

    return sweep_query_blocks(block, L)


def nsa_attention(q, k, v, gate_logits, cmp_w1, cmp_w2, cmp_pe):
    B, L = q.shape[:2]
    G, D = B_KV_GROUPS, HEAD_DIM
    scale = D ** -0.5
    n_cmp = (L - CMP_LEN) // CMP_STRIDE + 1
    starts = np.arange(n_cmp) * CMP_STRIDE
    tok = starts[:, None] + np.arange(CMP_LEN)[None, :]
    cmp_end = jnp.asarray(starts + CMP_LEN - 1)

    def compress(xs, w1, w2, pe):
        blk = xs[:, tok] + pe[None, None, :, None, :]
        blk = blk.transpose(0, 1, 3, 2, 4).reshape(B, n_cmp, G, CMP_LEN * D)
        return jax.nn.silu(blk @ w1) @ w2

    kc = compress(k[:, :, 0], cmp_w1[0], cmp_w2[0], cmp_pe[0])
    vc = compress(v[:, :, 0], cmp_w1[1], cmp_w2[1], cmp_pe[1])
    n_slc = L // SLC_LEN
    n_sel = min(SLC_TOPN, n_slc)
    slc_start = np.arange(n_slc) * SLC_LEN
    cover = jnp.asarray(((starts[:, None] < slc_start[None, :] + SLC_LEN)
                         & (starts[:, None] + CMP_LEN > slc_start[None, :])).astype(np.float32))
    ks_blk = k[:, :, 1].reshape(B, n_slc, SLC_LEN, G, D).transpose(0, 3, 1, 2, 4)
    vs_blk = v[:, :, 1].reshape(B, n_slc, SLC_LEN, G, D).transpose(0, 3, 1, 2, 4)
    gather2 = jax.vmap(jax.vmap(lambda blocks, ii: blocks[ii]))
    blk_id = jnp.arange(n_slc)
    pad = ((0, 0), (WIN_LEN, 0), (0, 0), (0, 0))
    kw_pad = jnp.pad(k[:, :, 2], pad)
    vw_pad = jnp.pad(v[:, :, 2], pad)
    gates = jax.nn.sigmoid(gate_logits.astype(f32))

    def block(j):
        t = j * Q_BLK + jnp.arange(Q_BLK)
        qb = lax.dynamic_slice_in_dim(q, j * Q_BLK, Q_BLK, 1).reshape(B, Q_BLK, G, B_HPG, D)
        gb = lax.dynamic_slice_in_dim(gates, j * Q_BLK, Q_BLK, 1).reshape(B, Q_BLK, G, B_HPG, 3)
        s = jnp.einsum('bqghd,bngd->bghqn', qb, kc) * scale
        p_cmp = masked_softmax(s, cmp_end[None, :] <= t[:, None])
        o_cmp = jnp.einsum('bghqn,bngd->bqghd', p_cmp, vc)
        imp = jnp.einsum('bghqn,nm->bgqm', p_cmp, cover)
        cur = t // SLC_LEN
        forced = ((blk_id[None, :] == 0) | (blk_id[None, :] == cur[:, None])
                  | (blk_id[None, :] == cur[:, None] - 1))
        admissible = blk_id[None, :] <= cur[:, None]
        imp = jnp.where(forced, FORCE, jnp.where(admissible, imp, NEG))
        _, sel = lax.top_k(imp, n_sel)
        kg = gather2(ks_blk, sel).reshape(B, G, Q_BLK, n_sel * SLC_LEN, D)
        vg = gather2(vs_blk, sel).reshape(B, G, Q_BLK, n_sel * SLC_LEN, D)
        tok_pos = sel[..., None] * SLC_LEN + jnp.arange(SLC_LEN)
        m_slc = (tok_pos <= t[None, None, :, None, None]).reshape(B, G, 1, Q_BLK, n_sel * SLC_LEN)
        s = jnp.einsum('bqghd,bgqxd->bghqx', qb, kg) * scale
        p = masked_softmax(s, m_slc)
        o_slc = jnp.einsum('bghqx,bgqxd->bqghd', p, vg)
        kw = lax.dynamic_slice_in_dim(kw_pad, j * Q_BLK, WIN_LEN + Q_BLK, 1)
        vw = lax.dynamic_slice_in_dim(vw_pad, j * Q_BLK, WIN_LEN + Q_BLK, 1)
        s_pos = j * Q_BLK - WIN_LEN + jnp.arange(WIN_LEN + Q_BLK)
        diff = t[:, None] - s_pos[None, :]
        m_win = (s_pos[None, :] >= 0) & (diff >= 0) & (diff < WIN_LEN)
        s = jnp.einsum('bqghd,bkgd->bghqk', qb, kw) * scale
        p = masked_softmax(s, m_win)
        o_win = jnp.einsum('bghqk,bkgd->bqghd', p, vw)
        o = gb[..., 0:1] * o_cmp + gb[..., 1:2] * o_slc + gb[..., 2:3] * o_win
        return o.reshape(B, Q_BLK, B_HEADS * D)

    return sweep_query_blocks(block, L)


def diff_attention(q, k, v, lam, subln_g, lam_init):
    B, L = q.shape[:2]
    scale = C_DIM ** -0.5
    lamf = lam.astype(f32)
    lam_val = (jnp.exp(jnp.sum(lamf[0] * lamf[1])) - jnp.exp(jnp.sum(lamf[2] * lamf[3])) + lam_init)
    key_pos = jnp.arange(L)

    def block(j):
        t = j * Q_BLK + jnp.arange(Q_BLK)
        qb = lax.dynamic_slice_in_dim(q, j * Q_BLK, Q_BLK, 1)
        s = jnp.einsum('bqhmd,bkhmd->bmhqk', qb, k) * scale
        p = masked_softmax(s, key_pos[None, :] <= t[:, None])
        attn = p[:, 0] - lam_val * p[:, 1]
        o = jnp.einsum('bhqk,bkhe->bqhe', attn, v)
        o = rms_norm(o, subln_g) * (1.0 - lam_init)
        return o.reshape(B, Q_BLK, C_HEADS * 2 * C_DIM)

    return sweep_query_blocks(block, L)


def token_mixing(u, w_in, a_lat_g, a_up, cmp_w1, cmp_w2, cmp_pe, lam, c_subln_g, w_br, w_o, lam_init):
    B, L, _ = u.shape
    pos = jnp.arange(L)
    h = u @ w_in
    aq, alat, iq, ik, iw, bq, bkv, bg, cq, ck, cv, gl = jnp.split(h, list(IN_OFFSETS), axis=-1)
    aq = partial_rope(aq.reshape(B, L, A_HEADS, HEAD_DIM), pos)
    akv = (rms_norm(alat, a_lat_g) @ a_up).reshape(B, L, 2, A_HEADS, HEAD_DIM)
    ak = partial_rope(akv[:, :, 0], pos)
    av = akv[:, :, 1]
    iq = partial_rope(iq.reshape(B, L, IDX_HEADS, IDX_DIM), pos)
    ik = partial_rope(layer_norm(ik)[:, :, None, :], pos)[:, :, 0]
    iw = iw * (IDX_HEADS ** -0.5)
    ya = dsa_attention(aq, ak, av, iq, ik, iw)
    bq = partial_rope(bq.reshape(B, L, B_HEADS, HEAD_DIM), pos)
    bkv = bkv.reshape(B, L, 3, 2, B_KV_GROUPS, HEAD_DIM)
    bk = partial_rope(bkv[:, :, :, 0].reshape(B, L, 3 * B_KV_GROUPS, HEAD_DIM), pos)
    bk = bk.reshape(B, L, 3, B_KV_GROUPS, HEAD_DIM)
    bv = bkv[:, :, :, 1]
    yb = nsa_attention(bq, bk, bv, bg.reshape(B, L, B_HEADS, 3), cmp_w1, cmp_w2, cmp_pe)
    cq = partial_rope(cq.reshape(B, L, C_HEADS * 2, C_DIM), pos).reshape(B, L, C_HEADS, 2, C_DIM)
    ck = partial_rope(ck.reshape(B, L, C_HEADS * 2, C_DIM), pos).reshape(B, L, C_HEADS, 2, C_DIM)
    cv = cv.reshape(B, L, C_HEADS, 2 * C_DIM)
    yc = diff_attention(cq, ck, cv, lam, c_subln_g, lam_init)
    y = jnp.stack([ya, yb, yc])
    br = jnp.einsum('rbse,red->rbsd', y, w_br)
    g = jax.nn.sigmoid(gl.astype(f32)).reshape(B, L, N_BRANCH, D_MODEL)
    merged = jnp.einsum('bsrd,rbsd->bsd', g, br)
    return merged @ w_o


def setup_inputs(seed: int = 0) -> dict:
    key = jax.random.key(seed)
    ks = jax.random.split(key, 18)

    def nrm(k, shape, s):
        return jax.random.normal(k, shape, f32) * s

    return {
        'x': nrm(ks[0], (BATCH, SEQ, D_MODEL), 1.0),
        'c': nrm(ks[1], (BATCH, D_MODEL), 1.0),
        'w_ada': nrm(ks[2], (DEPTH, D_MODEL, 6 * D_MODEL), 0.5 * D_MODEL ** -0.5),
        'b_ada': nrm(ks[3], (DEPTH, 6 * D_MODEL), 0.02),
        'w_in': nrm(ks[4], (DEPTH, D_MODEL, N_IN), D_MODEL ** -0.5),
        'a_lat_g': 1.0 + nrm(ks[5], (DEPTH, A_LATENT), 0.02),
        'a_up': nrm(ks[6], (DEPTH, A_LATENT, 2 * A_HEADS * HEAD_DIM), A_LATENT ** -0.5),
        'cmp_w1': nrm(ks[7], (DEPTH, 2, CMP_LEN * HEAD_DIM, HEAD_DIM), (CMP_LEN * HEAD_DIM) ** -0.5),
        'cmp_w2': nrm(ks[8], (DEPTH, 2, HEAD_DIM, HEAD_DIM), HEAD_DIM ** -0.5),
        'cmp_pe': nrm(ks[9], (DEPTH, 2, CMP_LEN, HEAD_DIM), 0.1),
        'lam': nrm(ks[10], (DEPTH, 4, C_DIM), 0.1),
        'c_subln_g': 1.0 + nrm(ks[11], (DEPTH, 2 * C_DIM), 0.02),
        'w_br': nrm(ks[12], (DEPTH, N_BRANCH, BRANCH_W, D_MODEL), BETA * BRANCH_W ** -0.5),
        'w_o': nrm(ks[13], (DEPTH, D_MODEL, D_MODEL), BETA * D_MODEL ** -0.5),
        'w_ffn_in': nrm(ks[14], (DEPTH, D_MODEL, 2 * D_FF), D_MODEL ** -0.5),
        'w_ffn_out': nrm(ks[15], (DEPTH, D_FF, D_MODEL), BETA * D_FF ** -0.5),
        'ln_g': 1.0 + nrm(ks[16], (DEPTH, 2, D_MODEL), 0.02),
        'ln_b': nrm(ks[17], (DEPTH, 2, D_MODEL), 0.02),
    }


def reference(x, c, w_ada, b_ada, w_in, a_lat_g, a_up, cmp_w1, cmp_w2, cmp_pe, lam, c_subln_g,
              w_br, w_o, w_ffn_in, w_ffn_out, ln_g, ln_b):
    cs = jax.nn.silu(c)
    for l in range(DEPTH):
        lam_init = 0.8 - 0.6 * math.exp(-0.3 * l)
        mod = cs @ w_ada[l] + b_ada[l]
        sh_a, sc_a, g_a, sh_f, sc_f, g_f = jnp.split(mod, 6, axis=-1)
        u = layer_norm(x) * (1.0 + sc_a[:, None]) + sh_a[:, None]
        y = token_mixing(u, w_in[l], a_lat_g[l], a_up[l], cmp_w1[l], cmp_w2[l], cmp_pe[l],
                         lam[l], c_subln_g[l], w_br[l], w_o[l], lam_init)
        x = layer_norm(ALPHA * x + g_a[:, None] * y, ln_g[l, 0], ln_b[l, 0])
        u = layer_norm(x) * (1.0 + sc_f[:, None]) + sh_f[:, None]
        gate, up = jnp.split(u @ w_ffn_in[l], 2, axis=-1)
        f = (jax.nn.silu(gate) * up) @ w_ffn_out[l]
        x = layer_norm(ALPHA * x + g_f[:, None] * f, ln_g[l, 1], ln_b[l, 1])
    return x
```

```python
import math
from collections import defaultdict
from contextlib import ExitStack

import numpy as np
import concourse.bass as bass
import concourse.mybir as mybir
from concourse.bass_utils import run_bass_kernel_spmd

F32 = mybir.dt.float32
BF16 = mybir.dt.bfloat16
I32 = mybir.dt.int32
ALU = mybir.AluOpType
AF = mybir.ActivationFunctionType

D = 2048
L = 4096
NOWN = 1024
DFF = 5632
NIN = 14440
ROPE_THETA = 500000.0
ALPHA = 4 ** 0.25
NEG = -1e30

O_AQ, O_ALAT, O_IQ, O_IK, O_IW, O_BQ, O_BKV, O_BG, O_CQ, O_CK, O_CV, O_GL = (
    0, 1024, 1536, 2560, 2624, 2640, 3664, 5200, 5224, 6248, 7272, 8296)

EPOCH = 30000
DEBUG_TB = False
NDMASEM = 24


class Buf:
    __slots__ = ("w", "r")

    def __init__(self):
        self.w = None
        self.r = {}


class FW:
    def __init__(self, nc, stack):
        self.nc = nc
        self.stack = stack
        self.engs = {"pe": nc.tensor, "act": nc.scalar, "dve": nc.vector, "pool": nc.gpsimd, "sp": nc.sync}
        self.ops = {k: [] for k in self.engs}
        self.cnt = {k: 0 for k in self.engs}
        self.sems = {}
        self.waited = {k: {} for k in self.engs}
        self.dma_i = 0
        self.dma_last = [None] * NDMASEM
        self.dma_val = [0] * NDMASEM
        for i in range(NDMASEM):
            self.sems[("dma", i)] = stack.enter_context(nc.semaphore(f"dsem{i}"))
        self.n_inst = 0
        self.bufs = defaultdict(Buf)

    def B(self, *key):
        return self.bufs[key]

    def _engsem(self, eng, epoch):
        key = (eng, epoch)
        if key not in self.sems:
            self.sems[key] = self.stack.enter_context(self.nc.semaphore(f"s_{eng}_{epoch}"))
        return key

    def _emit_waits(self, eng, toks):
        need = {}
        for (k, v) in toks:
            if need.get(k, 0) < v:
                need[k] = v
        for k, v in need.items():
            if self.waited[eng].get(k, 0) >= v:
                continue
            self.waited[eng][k] = v
            h = self.sems[k]
            self.ops[eng].append(lambda e, h=h, v=v: e.wait_ge(h, v))
            self.n_inst += 1

    def _deps(self, eng, reads, writes):
        toks = []
        for b in reads:
            if b.w is not None:
                toks.append(b.w)
        for b in writes:
            if b.w is not None:
                toks.append(b.w)
            for k, v in b.r.items():
                toks.append((k, v))
        if eng == "pe":
            toks = [t for t in toks if t[0][0] != "pe"]
        return toks

    def _mark(self, tok, reads, writes):
        key, val = tok
        for b in reads:
            if b.r.get(key, 0) < val:
                b.r[key] = val
        for b in writes:
            b.w = tok
            b.r = {}

    def _last_tok(self, eng):
        c = self.cnt[eng]
        if not c:
            return None
        return ((eng, (c - 1) // EPOCH), (c - 1) % EPOCH + 1)

    def op(self, eng, fn, reads=(), writes=()):
        self._emit_waits(eng, self._deps(eng, reads, writes))
        self.cnt[eng] += 1
        c = self.cnt[eng]
        epoch, val = (c - 1) // EPOCH, (c - 1) % EPOCH + 1
        key = self._engsem(eng, epoch)
        if val == 1 and epoch > 0:
            pass
        h = self.sems[key]
        if DEBUG_TB:
            import traceback
            tb = traceback.extract_stack(limit=6)

            def run(e, fn=fn, h=h, tb=tb):
                try:
                    return fn(e).then_inc(h, 1)
                except Exception:
                    print("".join(traceback.format_list(tb)))
                    raise
            self.ops[eng].append(run)
        else:
            self.ops[eng].append(lambda e, fn=fn, h=h: fn(e).then_inc(h, 1))
        self.n_inst += 1
        tok = (key, val)
        self._mark(tok, reads, writes)
        return tok

    def dma(self, eng, out, in_, reads=(), writes=(), **kw):
        i = self.dma_i % NDMASEM
        self.dma_i += 1
        toks = self._deps(eng, reads, writes)
        if self.dma_last[i] is not None:
            toks.append(self.dma_last[i])
        self._emit_waits(eng, toks)
        self.dma_val[i] += 16
        key = ("dma", i)
        val = self.dma_val[i]
        h = self.sems[key]
        self.ops[eng].append(lambda e, h=h: e.dma_start(out=out, in_=in_, **kw).then_inc(h, 16))
        self.n_inst += 1
        tok = (key, val)
        self.dma_last[i] = tok
        self._mark(tok, reads, writes)
        return tok

    def coll(self, eng, fn, reads=(), writes=()):
        key = ("cc", 0)
        if key not in self.sems:
            self.sems[key] = self.stack.enter_context(self.nc.semaphore("ccsem"))
            self.cc_val = 0
        self._emit_waits(eng, self._deps(eng, reads, writes))
        self.cc_val += 1
        val = self.cc_val
        h = self.sems[key]
        self.ops[eng].append(lambda e, h=h: fn(e).then_inc(h, 1))
        self.n_inst += 1
        tok = (key, val)
        self._mark(tok, reads, writes)
        self.cc_last = tok
        return tok

    def dma_like(self, eng, fn, reads=(), writes=()):
        i = self.dma_i % NDMASEM
        self.dma_i += 1
        toks = self._deps(eng, reads, writes)
        if self.dma_last[i] is not None:
            toks.append(self.dma_last[i])
        self._emit_waits(eng, toks)
        self.dma_val[i] += 16
        key = ("dma", i)
        val = self.dma_val[i]
        h = self.sems[key]
        self.ops[eng].append(lambda e, h=h: fn(e).then_inc(h, 16))
        self.n_inst += 1
        tok = (key, val)
        self.dma_last[i] = tok
        self._mark(tok, reads, writes)
        return tok

    def fence(self, include_cc=False):
        toks = []
        for eng in self.engs:
            t = self._last_tok(eng)
            if t:
                toks.append(t)
        for i in range(NDMASEM):
            if self.dma_last[i] is not None:
                toks.append(self.dma_last[i])
        if include_cc and getattr(self, "cc_last", None) is not None:
            toks.append(self.cc_last)
        for eng in self.engs:
            self._emit_waits(eng, [t for t in toks if not (t[0][0] == eng)])

    def finish(self):
        toks = []
        for eng in self.engs:
            t = self._last_tok(eng)
            if t:
                toks.append(t)
        for i in range(NDMASEM):
            if self.dma_last[i] is not None:
                toks.append(self.dma_last[i])
        if getattr(self, "cc_last", None) is not None:
            toks.append(self.cc_last)
        self._emit_waits("sp", toks)

    def replay(self):
        with self.nc.Block() as block:
            @block.sync
            def _(e):
                for f in self.ops["sp"]:
                    f(e)

            @block.tensor
            def _(e):
                for f in self.ops["pe"]:
                    f(e)

            @block.scalar
            def _(e):
                for f in self.ops["act"]:
                    f(e)

            @block.vector
            def _(e):
                for f in self.ops["dve"]:
                    f(e)

            @block.gpsimd
            def _(e):
                for f in self.ops["pool"]:
                    f(e)


class Prog:
    def __init__(self, debug=False, phases=None, layers=(0, 1)):
        self.debug = debug
        self.layers = tuple(layers)
        self.phases = phases
        self.nc = bass.Bass("TRN2", target_bir_lowering=False)
        self.st = ExitStack()
        self.fw = FW(self.nc, self.st)
        self.uid = 0

    def din(self, name, shape, dt=F32):
        return self.nc.dram_tensor(name, list(shape), dt, kind="ExternalInput").ap()

    def dout(self, name, shape, dt=F32):
        return self.nc.dram_tensor(name, list(shape), dt, kind="ExternalOutput").ap()

    def dscr(self, name, shape, dt=BF16):
        kind = "ExternalOutput" if self.debug else "Internal"
        return self.nc.dram_tensor(name, list(shape), dt, kind=kind).ap()

    def sb(self, stack, name, shape, dt):
        self.uid += 1
        return stack.enter_context(self.nc.sbuf_tensor(f"{name}_{self.uid}", list(shape), dt))

    def mm(self, out, lhsT, rhs, start, stop, r, w):
        self.fw.op("pe", lambda e: e.matmul(out, lhsT, rhs, start=start, stop=stop), r, w)

    def tr(self, out, in_, r, w):
        ident = self.ident[0:in_.shape[0], 0:in_.shape[0]]
        self.fw.op("pe", lambda e: e.transpose(out, in_, ident), list(r) + [self.fw.B("ident")], w)

    def act(self, out, in_, func, r, w, bias=0.0, scale=1.0, accum=None):
        if accum is None:
            self.fw.op("act", lambda e: e.activation(out, in_, func, bias=bias, scale=scale), r, w)
        else:
            self.fw.op("act", lambda e: e.activation(out, in_, func, bias=bias, scale=scale, accum_out=accum), r, w)

    def ts(self, eng, out, in0, s1, s2, op0, op1, r, w, accum=None):
        if accum is None:
            if op1 is None:
                self.fw.op(eng, lambda e: e.tensor_scalar(out, in0, s1, None, op0), r, w)
            else:
                self.fw.op(eng, lambda e: e.tensor_scalar(out, in0, s1, s2, op0, op1), r, w)
        else:
            self.fw.op(eng, lambda e: e.tensor_scalar(out, in0, s1, s2, op0, op1, accum_out=accum), r, w)

    def tt(self, eng, out, in0, in1, op, r, w):
        self.fw.op(eng, lambda e: e.tensor_tensor(out, in0, in1, op), r, w)

    def stt(self, out, in0, scalar, in1, op0, op1, r, w, accum=None):
        if accum is None:
            self.fw.op("dve", lambda e: e.scalar_tensor_tensor(out, in0, scalar, in1, op0, op1), r, w)
        else:
            self.fw.op("dve", lambda e: e.scalar_tensor_tensor(out, in0, scalar, in1, op0, op1, accum_out=accum), r, w)

    def cp(self, eng, out, in_, r, w):
        if eng == "act":
            self.fw.op("act", lambda e: e.copy(out, in_), r, w)
        else:
            self.fw.op(eng, lambda e: e.tensor_copy(out, in_), r, w)

    def ms(self, eng, ap, val, w):
        self.fw.op(eng, lambda e: e.memset(ap, val), (), w)

    def recip(self, out, in_, r, w):
        self.fw.op("dve", lambda e: e.reciprocal(out, in_), r, w)

    UNTRACKED = (("scr",), ("s_y",), ("s_uT",), ("s_x1",), ("xout",))

    def dma(self, q, out, in_, r, w, **kw):
        un = [self.fw.bufs[k] for k in self.UNTRACKED]
        r = [b for b in r if not any(b is u for u in un)]
        w = [b for b in w if not any(b is u for u in un)]
        self.fw.dma(q, out, in_, r, w, **kw)

    def psum(self, i):
        return self.ps[i], self.fw.B("ps", i)

    def build(self):
        nc, fw, st = self.nc, self.fw, self.st
        B = fw.B
        C = {}
        C["xf"] = self.din("xf", [L, D])
        C["xo"] = self.din("xo", [NOWN, D])
        C["qpos"] = self.din("qpos", [NOWN])
        C["qposc"] = self.din("qposc", [128, 8])
        C["ct"] = self.din("ct", [128, 16])
        C["inv16"] = self.din("inv16", [16])
        C["inv8"] = self.din("inv8", [8])
        self.Il = {}
        for l in self.layers:
            W = dict(C)
            sfx = str(l)
            W["laminit"] = self.din("laminit" + sfx, [1])
            W["w_ada"] = self.din("w_ada" + sfx, [D, 3072])
            W["b_ada"] = self.din("b_ada" + sfx, [3072])
            W["w_in"] = self.din("w_in" + sfx, [D, NIN])
            W["a_lat_g"] = self.din("a_lat_g" + sfx, [128, 4])
            W["a_up"] = self.din("a_up" + sfx, [512, 512])
            W["w_kv"] = self.din("w_kv" + sfx, [D, 1280])
            W["cmp_w1"] = self.din("cmp_w1" + sfx, [2, 4096, 128])
            W["cmp_w2"] = self.din("cmp_w2" + sfx, [2, 128, 128])
            W["cmp_peT"] = self.din("cmp_peT" + sfx, [2, 128, 32])
            W["lam"] = self.din("lam" + sfx, [4, 128])
            W["c_subln_g"] = self.din("c_subln_g" + sfx, [256])
            W["w_br"] = self.din("w_br" + sfx, [3, 1024, D])
            W["w_o"] = self.din("w_o" + sfx, [D, D])
            W["w_ffn_in"] = self.din("w_ffn_in" + sfx, [D, 2 * DFF])
            W["w_ffn_out"] = self.din("w_ffn_out" + sfx, [DFF, D])
            W["ln_g"] = self.din("ln_g" + sfx, [2, D])
            W["ln_b"] = self.din("ln_b" + sfx, [2, D])
            self.Il[l] = W
        I = self.Il[self.layers[0]]
        self.I = I
        xout = self.dout("xout", [NOWN, D])

        S = {}
        S["modq"] = [self.nc.dram_tensor(f"s_modq{l}", [1, 3072], F32, kind="Internal").ap() for l in range(2)]
        S["modall"] = [self.nc.dram_tensor(f"s_modall{l}", [4, 3072], F32, kind="Internal").ap() for l in range(2)]
        S["lamd"] = self.dscr("s_lamd", [2], F32)
        idr = lambda n, shp: self.nc.dram_tensor(n, list(shp), BF16, kind="Internal").ap()
        S["L_akT"] = [idr(f"l_akT{i}", [128, L]) for i in range(2)]
        S["G_akT"] = [idr(f"g_akT{i}", [512, L]) for i in range(2)]
        S["L_av"] = [idr(f"l_av{i}", [2048, 256]) for i in range(2)]
        S["G_av"] = [idr(f"g_av{i}", [4 * 2048, 256]) for i in range(2)]
        S["L_bK"] = idr("l_bK", [128, L])
        S["G_bK"] = idr("g_bK", [512, L])
        S["L_bV"] = idr("l_bV", [L, 128])
        S["G_bV"] = idr("g_bV", [4 * L, 128])
        S["L_ckT"] = [idr(f"l_ckT{i}", [128, L]) for i in range(2)]
        S["G_ckT"] = [idr(f"g_ckT{i}", [512, L]) for i in range(2)]
        S["L_cv"] = [idr(f"l_cv{i}", [2048, 256]) for i in range(2)]
        S["G_cv"] = [idr(f"g_cv{i}", [4 * 2048, 256]) for i in range(2)]
        S["ikT"] = self.dscr("s_ikT", [128, L])
        S["aqT"] = self.dscr("s_aqT", [8, 128, NOWN])
        S["iqT"] = self.dscr("s_iqT", [8, 128, NOWN])
        S["iw"] = self.dscr("s_iw", [NOWN, 16], F32)
        S["bqT"] = self.dscr("s_bqT", [8, 128, NOWN])
        S["bg"] = self.dscr("s_bg", [NOWN, 24], F32)
        S["cqT"] = self.dscr("s_cqT", [8, 128, NOWN])
        S["kwT"] = self.dscr("s_kwT", [2, 128, L])
        S["vw"] = self.dscr("s_vw", [L, 256])
        S["uT"] = self.dscr("s_uT", [16, 128, NOWN])
        S["y"] = self.dscr("s_y", [NOWN, 3072])
        S["x1"] = self.dscr("s_x1", [NOWN, D], F32)
        S["xown1"] = self.nc.dram_tensor("s_xown1", [NOWN, D], F32, kind="Internal").ap()
        S["G"] = self.nc.dram_tensor("s_G", [8, 512, D], F32, kind="Internal").ap()
        self.S = S

        self.ps = [st.enter_context(nc.psum_tensor(f"ps{i}", [128, 512], F32)) for i in range(8)]
        g = st
        self.ident = self.sb(g, "ident", [128, 128], BF16)
        self.ones_bf = self.sb(g, "ones_bf", [128, 128], BF16)
        self.modT = self.sb(g, "modT", [128, 4, 16], F32)
        self.POSq = self.sb(g, "POSq", [128, 16], F32)
        self.qposB = self.sb(g, "qposB", [128, NOWN], F32)
        self.cmpR = self.sb(g, "cmpR", [128, 2, 2, 193], BF16)
        self.kcT = self.sb(g, "kcT", [128, 2, 256], BF16)
        self.lamv = self.sb(g, "lamv", [128, 4], F32)
        self.epsc = self.sb(g, "epsc", [128, 2], F32)
        self.ms("dve", self.epsc[:, 0:1], 1e-5, [B("epsc")])
        self.ms("dve", self.epsc[:, 1:2], 1e-6, [B("epsc")])

        ident, ones_bf = self.ident, self.ones_bf
        self.ms("pool", ident[:], 1.0, [B("ident")])
        fw.op("pool", lambda e: e.affine_select(ident[:], ident[:], pattern=[[-1, 128]], compare_op=ALU.is_equal,
                                                fill=0.0, base=0, channel_multiplier=1),
              [B("ident")], [B("ident")])
        self.ms("pool", ones_bf[:], 1.0, [B("ones")])
        self.dma("sp", self.POSq[:, 0:8], I["qposc"][:, :], [], [B("POSq")])
        self.dma("sp", self.qposB[:], I["qpos"].partition_broadcast(128), [], [B("qposB")])

        ph = self.phases
        if ph is None or "mod" in ph:
            for l in self.layers:
                self.layer = l
                self.I = self.Il[l]
                self.phase_mod()
        for li, l in enumerate(self.layers):
            self.layer = l
            self.I = self.Il[l]
            last = (li == len(self.layers) - 1)
            if ph is None or "mod" in ph:
                self.phase_mod_load()
            if ph is None or "proj" in ph:
                self.phase_proj()
            if ph is None or "cmp" in ph:
                self.phase_compress()
            if ph is None or "attA" in ph:
                self.phase_att_a()
            if ph is None or "attB" in ph:
                self.phase_att_b()
            if ph is None or "attC" in ph:
                self.phase_att_c()
            if ph is None or "merge" in ph:
                self.phase_merge()
            if ph is None or "ffn" in ph:
                self.phase_ffn(xout if last else S["xown1"], exchange=not last)
            if not last:
                fw.fence(include_cc=True)
        fw.finish()
        fw.replay()
        st.close()
        return nc

    def phase_mod(self):
        fw, I, S, B = self.fw, self.I, self.S, self.fw.B
        with ExitStack() as p:
            ct = self.sb(p, "ct", [128, 16], F32)
            cs = self.sb(p, "cs", [128, 16], BF16)
            wa = [self.sb(p, f"wa{i}", [128, 16, 512], BF16) for i in range(2)]
            brow = self.sb(p, "brow", [1, 512], F32)
            mrow = self.sb(p, "mrow", [1, 512], F32)
            self.dma("sp", ct[:], I["ct"][:, :], [], [B("ct")])
            self.act(cs[:], ct[:], AF.Silu, [B("ct")], [B("cs")])
            wv = I["w_ada"].rearrange("(kc p) n -> p kc n", p=128)
            mq = self.S["modq"][self.layer]
            for cg in range(6):
                wt = wa[cg % 2]
                self.dma("pool", wt[:], wv[:, :, cg * 512:(cg + 1) * 512], [], [B("wa", cg % 2)])
                self.dma("sp", brow[:], I["b_ada"][cg * 512:(cg + 1) * 512].unsqueeze(0), [], [B("brow")])
                pt, pb = self.psum(cg % 2)
                for kc in range(16):
                    self.mm(pt[0:1, :], cs[:, kc:kc + 1], wt[:, kc, :], kc == 0, kc == 15,
                            [B("cs"), B("wa", cg % 2)], [pb])
                self.tt("dve", mrow[:], pt[0:1, :], brow[:], ALU.add, [pb, B("brow")], [B("mrow")])
                self.dma("sp", mq[:, cg * 512:(cg + 1) * 512], mrow[:], [B("mrow")], [B("s_modq")])
            mall = self.S["modall"][self.layer]
            fw.coll("pool", lambda e: e.collective_compute(
                "AllGather", ALU.bypass, replica_groups=[[0, 1, 2, 3], [4, 5, 6, 7]], ins=[mq[:, :]], outs=[mall[:, :]]),
                [B("s_modq")], [B("s_mod")])
            fw.fence(include_cc=True)

    def phase_mod_load(self):
        fw, I, S, B = self.fw, self.I, self.S, self.fw.B
        S["mod"] = S["modall"][self.layer].rearrange("a b -> (a b)")
        with ExitStack() as p:
            mv = S["mod"].rearrange("(a kc p) -> a p kc", a=6, p=128)
            for j, a in enumerate((0, 1, 3, 4)):
                self.dma("sp", self.modT[:, j, :], mv[a], [B("s_mod")], [B("modT")], allow_slow_non_contiguous=True)
            for j in (1, 3):
                self.ts("dve", self.modT[:, j, :], self.modT[:, j, :], 1.0, None, ALU.add, None, [B("modT")], [B("modT")])
            lm = self.sb(p, "lm", [1, 4, 128], F32)
            lt = self.sb(p, "lt", [1, 2, 128], F32)
            l2 = self.sb(p, "l2", [1, 8], F32)
            self.dma("sp", lm[:], I["lam"].unsqueeze(0), [], [B("lm")])
            self.dma("sp", l2[:, 4:5], I["laminit"].unsqueeze(0), [], [B("l2")])
            self.tt("dve", lt[:, 0, :], lm[:, 0, :], lm[:, 1, :], ALU.mult, [B("lm")], [B("lt")])
            self.tt("dve", lt[:, 1, :], lm[:, 2, :], lm[:, 3, :], ALU.mult, [B("lm")], [B("lt")])
            self.fw.op("dve", lambda e: e.reduce_sum(l2[:, 0:1], lt[:, 0, :], mybir.AxisListType.X), [B("lt"), B("l2")], [B("l2")])
            self.fw.op("dve", lambda e: e.reduce_sum(l2[:, 1:2], lt[:, 1, :], mybir.AxisListType.X), [B("lt"), B("l2")], [B("l2")])
            self.act(l2[:, 2:4], l2[:, 0:2], AF.Exp, [B("l2")], [B("l2")])
            self.tt("dve", l2[:, 5:6], l2[:, 3:4], l2[:, 2:3], ALU.subtract, [B("l2")], [B("l2")])
            self.tt("dve", l2[:, 5:6], l2[:, 5:6], l2[:, 4:5], ALU.subtract, [B("l2")], [B("l2")])
            self.ts("dve", l2[:, 6:7], l2[:, 4:5], -1.0, 1.0, ALU.mult, ALU.add, [B("l2")], [B("l2")])
            self.dma("sp", S["lamd"].unsqueeze(0), l2[:, 5:7], [B("l2")], [B("s_lamd")])
            self.dma("sp", self.lamv[:, 0:2], S["lamd"].partition_broadcast(128), [B("s_lamd")], [B("lamv")])
            fw.fence()

    def rope_tables(self, p, pos, npos, tag):
        I, B = self.I, self.fw.B
        res = []
        for (half, inv_name) in ((16, "inv16"), (8, "inv8")):
            inv = self.sb(p, f"inv{half}", [128, half], F32)
            self.dma("sp", inv[:], I[inv_name].partition_broadcast(128), [], [B(tag, "inv", half)])
            ang = self.sb(p, f"ang{half}", [128, npos, half], F32)
            CC = self.sb(p, f"CC{half}", [128, npos, 2 * half], F32)
            SS = self.sb(p, f"SS{half}", [128, npos, 2 * half], F32)
            tmp = self.sb(p, f"rtmp{half}", [128, npos, half], F32)
            ki = self.sb(p, f"rki{half}", [128, npos, half], I32)
            bA, bT, bK, bC, bS = B(tag, "ang", half), B(tag, "tmp", half), B(tag, "ki", half), B(tag, "CC", half), B(tag, "SS", half)
            shp = [128, npos, half]
            self.tt("dve", ang[:], pos.unsqueeze(2).to_broadcast(shp), inv[:].unsqueeze(1).to_broadcast(shp), ALU.mult,
                    [B(tag, "inv", half), B(tag, "pos")], [bA])
            for which, shift in (("sin", 0.0), ("cos", math.pi / 2)):
                self.ts("dve", tmp[:], ang[:], shift, 1.0 / (2 * math.pi), ALU.add, ALU.mult, [bA], [bT])
                self.cp("dve", ki[:], tmp[:], [bT], [bK])
                self.cp("dve", tmp[:], ki[:], [bK], [bT])
                self.stt(tmp[:], tmp[:], -2 * math.pi, ang[:], ALU.mult, ALU.add, [bT, bA], [bT])
                if shift != 0.0:
                    self.ts("dve", tmp[:], tmp[:], shift, None, ALU.add, None, [bT], [bT])
                dst = SS if which == "sin" else CC
                bD = bS if which == "sin" else bC
                w1 = dst[:, :, 0:half]
                self.ts("dve", w1, tmp[:], math.pi, -2 * math.pi, ALU.is_gt, ALU.mult, [bT], [bD])
                self.tt("dve", tmp[:], tmp[:], w1, ALU.add, [bT, bD], [bT])
                self.ts("dve", w1, tmp[:], -math.pi, 2 * math.pi, ALU.is_lt, ALU.mult, [bT], [bD])
                self.tt("dve", tmp[:], tmp[:], w1, ALU.add, [bT, bD], [bT])
                self.act(dst[:, :, half:2 * half], tmp[:], AF.Sin, [bT], [bD])
                if which == "sin":
                    self.ts("dve", dst[:, :, 0:half], dst[:, :, half:2 * half], -1.0, None, ALU.mult, None, [bD], [bD])
                else:
                    self.cp("dve", dst[:, :, 0:half], dst[:, :, half:2 * half], [bD], [bD])
            res += [CC, SS]
        return res

    def rope_apply(self, v, nh, hd, half, CCi, SSi, tmp, rb, tag):
        B = self.fw.B
        vv = v.rearrange("p (h d) -> p h d", d=hd)
        shp = [128, nh, half]
        x1, x2 = vv[:, :, 0:half], vv[:, :, half:2 * half]
        sneg = SSi[:, 0:half].unsqueeze(1).to_broadcast(shp)
        spos = SSi[:, half:2 * half].unsqueeze(1).to_broadcast(shp)
        cc = CCi.unsqueeze(1).to_broadcast([128, nh, 2 * half])
        tb = B("ropetmp", tag)
        self.tt("dve", tmp[:, 0:nh, 0:half], x2, sneg, ALU.mult, rb, [tb])
        self.tt("dve", tmp[:, 0:nh, half:2 * half], x1, spos, ALU.mult, rb, [tb])
        self.tt("dve", vv[:, :, 0:2 * half], vv[:, :, 0:2 * half], cc, ALU.mult, rb, rb)
        self.tt("dve", vv[:, :, 0:2 * half], vv[:, :, 0:2 * half], tmp[:, 0:nh, 0:2 * half], ALU.add, list(rb) + [tb], rb)

    def ln_to_uT(self, p, src_tile_fn, ntiles, uT, ubuf, modj, tag):
        B = self.fw.B
        xt = [self.sb(p, f"lnx{i}", [128, D], F32) for i in range(2)]
        xn = self.sb(p, "lnxn", [128, 4, D], BF16)
        stt_ = self.sb(p, "lnst", [128, 2, 4, 6], F32)
        mv = self.sb(p, "lnmv", [128, 2, 4], F32)
        for i in range(ntiles):
            k = i % 2
            src = src_tile_fn(i)
            self.dma("sp", xt[k][:], src, [B("G")], [B(tag, "x", k)])
            for c in range(4):
                self.fw.op("dve", lambda e, c=c, k=k: e.bn_stats(stt_[:, k, c, :], xt[k][:, c * 512:(c + 1) * 512]),
                           [B(tag, "x", k)], [B(tag, "st", k)])
            self.fw.op("dve", lambda e, k=k: e.bn_aggr(mv[:, k, 0:2], stt_[:, k, :, :].rearrange("p a b -> p (a b)")),
                       [B(tag, "st", k)], [B(tag, "mv", k)])
            self.act(mv[:, k, 2:3], mv[:, k, 1:2], AF.Sqrt, [B(tag, "mv", k)], [B(tag, "mv", k)], bias=self.epsc[:, 0:1])
            self.recip(mv[:, k, 3:4], mv[:, k, 2:3], [B(tag, "mv", k)], [B(tag, "mv", k)])
            self.ts("dve", xn[:, i % 4, :], xt[k][:], mv[:, k, 0:1], mv[:, k, 3:4], ALU.subtract, ALU.mult,
                    [B(tag, "x", k), B(tag, "mv", k)], [B(tag, "xn", i % 4)])
            if i % 4 == 3:
                g0 = (i // 4) * 512
                for kc in range(16):
                    pt, pb = self.psum(kc % 4)
                    ptb = pt[:].bitcast(BF16)
                    for j in range(4):
                        self.tr(ptb[:, j * 128:(j + 1) * 128], xn[:, j, kc * 128:(kc + 1) * 128], [B(tag, "xn", j)], [pb])
                    self.act(uT[:, kc, g0:g0 + 512], ptb[:, 0:512], AF.Identity, [pb, B("modT")], [ubuf(kc, i // 4)],
                             bias=self.modT[:, modj, kc:kc + 1], scale=self.modT[:, modj + 1, kc:kc + 1])

    def phase_proj(self):
        fw, I, S, B = self.fw, self.I, self.S, self.fw.B
        nc = self.nc
        with ExitStack() as p0:
            aup = self.sb(p0, "aup", [128, 4, 512], BF16)
            alg = self.sb(p0, "alg", [128, 4], F32)
            self.dma("pool", aup[:], I["a_up"].rearrange("(kc p) n -> p kc n", p=128), [], [B("aup")])
            self.dma("sp", alg[:], I["a_lat_g"][:, :], [], [B("alg")])
            for kc in range(4):
                self.ts("dve", aup[:, kc, :], aup[:, kc, :], alg[:, kc:kc + 1], None, ALU.mult, None,
                        [B("aup"), B("alg")], [B("aup")])
            for pas in range(3):
                with ExitStack() as p:
                    self.proj_pass(p, pas, aup)
                fw.fence()
                if pas == 1:
                    pairs = []
                    for i in range(2):
                        pairs += [(S["L_akT"][i], S["G_akT"][i]), (S["L_av"][i], S["G_av"][i]),
                                  (S["L_ckT"][i], S["G_ckT"][i]), (S["L_cv"][i], S["G_cv"][i])]
                    pairs += [(S["L_bK"], S["G_bK"]), (S["L_bV"], S["G_bV"])]
                    for (src_, dst_) in pairs:
                        fw.coll("pool", lambda e, src_=src_, dst_=dst_: e.collective_compute(
                            "AllGather", ALU.bypass, replica_groups=[[0, 1, 2, 3], [4, 5, 6, 7]],
                            ins=[src_[:, :]], outs=[dst_[:, :]]), [], [B("Gkv")])
        fw.fence(include_cc=True)

    def proj_pass(self, p, pas, aup):
        fw, I, S, B = self.fw, self.I, self.S, self.fw.B
        NT = 16 if pas < 2 else 8
        uT = self.sb(p, "uT", [128, 16, 2048], BF16)
        ubuf = lambda kc, grp: B("uT", grp)
        pos = self.sb(p, "pos", [128, NT], F32)
        if pas < 2:
            posi = self.sb(p, "posi", [128, NT], I32)
            fw.op("pool", lambda e: e.iota(posi[:], pattern=[[128, NT]], base=2048 * pas, channel_multiplier=1),
                  [], [B("posi")])
            self.cp("dve", pos[:], posi[:], [B("posi")], [B("rt", "pos")])
            if self.layer == self.layers[0]:
                src = lambda i: I["xf"][2048 * pas + 128 * i: 2048 * pas + 128 * (i + 1), :]
            else:
                def src(i):
                    t0_ = 2048 * pas + 128 * i
                    T = t0_ // 512
                    r_ = T if T <= 3 else 7 - T
                    o0 = (t0_ % 512) + (0 if T <= 3 else 512)
                    return S["G"][o0 // 128, r_ * 128:(r_ + 1) * 128, :]
        else:
            self.cp("dve", pos[:], self.POSq[:, 0:8], [B("POSq")], [B("rt", "pos")])
            if self.layer == self.layers[0]:
                src = lambda i: I["xo"][128 * i:128 * (i + 1), :]
            else:
                src = lambda i: S["xown1"][128 * i:128 * (i + 1), :]
        CC128, SS128, CC64, SS64 = self.rope_tables(p, pos[:], NT, "rt")
        with ExitStack() as pl:
            self.ln_to_uT(pl, src, NT, uT, ubuf, 0, "ln")
            fw.fence()
        if pas == 2:
            for kc in range(16):
                self.dma("sp", S["uT"][kc, :, :], uT[:, kc, 0:1024], [B("uT", 0), B("uT", 1)], [B("s_uT")])

        wts = [self.sb(p, f"wp{i}", [128, 16, 512], BF16) for i in range(2)]
        ev = [self.sb(p, f"ev{i}", [128, 512], BF16) for i in range(3)]
        rtmp = self.sb(p, "rtmp", [128, 8, 32], BF16)
        stg = [self.sb(p, f"stg{i}", [128, 4, 512], BF16) for i in range(2)]
        wv = I["w_in"].rearrange("(kc p) n -> p kc n", p=128)
        self.wslot = 0
        self.evslot = 0
        self.stgslot = 0

        def load_w(c0, n):
            k = self.wslot % 2
            self.wslot += 1
            self.dma("pool", wts[k][:, :, 0:n], wv[:, :, c0:c0 + n], [], [B("wp", k)])
            return wts[k], B("wp", k)

        def project(tt, wt, wb, n, lhs=None, nk=16):
            pt, pb = self.psum(4 + tt % 4)
            for kc in range(nk):
                if lhs is None:
                    l_ap, l_b = uT[:, kc, tt * 128:(tt + 1) * 128], B("uT", tt // 4)
                else:
                    l_ap, l_b = lhs(kc, tt)
                self.mm(pt[:, 0:n], l_ap, wt[:, kc, 0:n], kc == 0, kc == nk - 1, [l_b, wb], [pb])
            return pt, pb

        def evac(pt, pb, n):
            k = self.evslot % 3
            self.evslot += 1
            self.act(ev[k][:, 0:n], pt[:, 0:n], AF.Copy, [pb], [B("ev", k)])
            return ev[k], B("ev", k)

        def rope_T_store(tt, e, eb, n, rope, hd, dst_fn, t0_fn, tiles_per_store=4):
            nch = n // 128
            if rope:
                if hd == 128:
                    self.rope_apply(e[:, 0:n], nch, 128, 16, CC128[:, tt, :], SS128[:, tt, :], rtmp, [eb], "r")
                else:
                    self.rope_apply(e[:, 0:n], n // 64, 64, 8, CC64[:, tt, :], SS64[:, tt, :], rtmp, [eb], "r")
            pt, pb = self.psum(tt % 4)
            ptb = pt[:].bitcast(BF16)
            for c in range(nch):
                self.tr(ptb[:, c * 128:(c + 1) * 128], e[:, c * 128:(c + 1) * 128], [eb], [pb])
            sk = self.stgslot % 2
            sg = stg[sk]
            j = tt % 4
            self.cp("dve", sg[:, 0:nch, j * 128:(j + 1) * 128], ptb[:, 0:nch * 128].rearrange("p (c t) -> p c t", t=128),
                    [pb], [B("stg", sk)])
            if j == 3:
                self.stgslot += 1
                dst = dst_fn(tt // 4)
                if isinstance(dst, list):
                    for c_, d_ in enumerate(dst):
                        self.dma("sp", d_, sg[:, c_, :], [B("stg", sk)], [B("scr")])
                else:
                    self.dma("sp", dst, sg[:, 0:nch, :], [B("stg", sk)], [B("scr")])

        def job_T(c0, n, tiles, rope, hd, dst_fn):
            wt, wb = load_w(c0, n)
            prev = None
            for tt in tiles:
                pt, pb = project(tt, wt, wb, n)
                e, eb = evac(pt, pb, n)
                if prev is not None:
                    rope_T_store(*prev)
                prev = (tt, e, eb, n, rope, hd, dst_fn, None)
            rope_T_store(*prev)

        def job_V(c0, n, tiles, dst_fn):
            wt, wb = load_w(c0, n)
            for tt in tiles:
                pt, pb = project(tt, wt, wb, n)
                e, eb = evac(pt, pb, n)
                self.dma("sp", dst_fn(tt), e[:, 0:n], [eb], [B("scr")])

        if pas < 2:
            t0 = 2048 * pas
            tiles = range(16)
            latT = self.sb(p, "latT", [128, 4, 2048], BF16)
            ss = self.sb(p, "ss", [128, 4], F32)
            junk = self.sb(p, "junk", [128, 512], BF16)
            wt, wb = load_w(O_ALAT, 512)
            for tt in tiles:
                pt, pb = project(tt, wt, wb, 512)
                self.act(junk[:], pt[:, 0:512], AF.Square, [pb], [B("junk"), B("ss")], accum=ss[:, 0:1])
                self.act(ss[:, 1:2], ss[:, 0:1], AF.Sqrt, [B("ss")], [B("ss")], bias=self.epsc[:, 1:2], scale=1.0 / 512)
                self.recip(ss[:, 2:3], ss[:, 1:2], [B("ss")], [B("ss")])
                k = self.evslot % 3
                self.evslot += 1
                self.ts("dve", ev[k][:], pt[:, 0:512], ss[:, 2:3], None, ALU.mult, None, [pb, B("ss")], [B("ev", k)])
                pt2, pb2 = self.psum(tt % 4)
                ptb = pt2[:].bitcast(BF16)
                for c in range(4):
                    self.tr(ptb[:, c * 128:(c + 1) * 128], ev[k][:, c * 128:(c + 1) * 128], [B("ev", k)], [pb2])
                self.cp("dve", latT[:, :, tt * 128:(tt + 1) * 128], ptb[:, 0:512].rearrange("p (c t) -> p c t", t=128),
                        [pb2], [B("latT", tt)])
            lhs_lat = lambda kc, tt: (latT[:, kc, tt * 128:(tt + 1) * 128], B("latT", tt))
            akdst = lambda g4: [S["L_akT"][i][:, t0 + g4 * 512:t0 + (g4 + 1) * 512] for i in range(2)]
            prev = None
            for tt in tiles:
                pt, pb = self.psum(4 + tt % 4)
                for kc in range(4):
                    l_ap, l_b = lhs_lat(kc, tt)
                    self.mm(pt[:, 0:512], l_ap, aup[:, kc, :], kc == 0, kc == 3, [l_b, B("aup")], [pb])
                e, eb = evac(pt, pb, 512)
                self.dma("sp", S["L_av"][pas][tt * 128:(tt + 1) * 128, :], e[:, 256:512], [eb], [B("scr")])
                if prev is not None:
                    rope_T_store(*prev)
                prev = (tt, e, eb, 256, True, 128, akdst, None)
            rope_T_store(*prev)
            wt, wb = load_w(O_IK, 64)
            ikf = self.sb(p, "ikf", [128, 64], F32)
            ikst = self.sb(p, "ikst", [128, 12], F32)
            for tt in tiles:
                pt, pb = project(tt, wt, wb, 64)
                self.cp("act", ikf[:], pt[:, 0:64], [pb], [B("ikf")])
                fw.op("dve", lambda e: e.bn_stats(ikst[:, 0:6], ikf[:]), [B("ikf")], [B("ikst")])
                fw.op("dve", lambda e: e.bn_aggr(ikst[:, 6:8], ikst[:, 0:6]), [B("ikst")], [B("ikst")])
                self.act(ikst[:, 8:9], ikst[:, 7:8], AF.Sqrt, [B("ikst")], [B("ikst")], bias=self.epsc[:, 0:1])
                self.recip(ikst[:, 9:10], ikst[:, 8:9], [B("ikst")], [B("ikst")])
                k = self.evslot % 3
                self.evslot += 1
                self.ts("dve", ev[k][:, 0:64], ikf[:], ikst[:, 6:7], ikst[:, 9:10], ALU.subtract, ALU.mult,
                        [B("ikf"), B("ikst")], [B("ev", k)])
                self.rope_apply(ev[k][:, 0:64], 1, 64, 8, CC64[:, tt, :], SS64[:, tt, :], rtmp, [B("ev", k)], "r")
                self.cp("dve", ev[k][:, 64:128], ev[k][:, 0:64], [B("ev", k)], [B("ev", k)])
                rope_T_store(tt, ev[k], B("ev", k), 128, False, 128,
                             lambda g4: S["ikT"][:, t0 + g4 * 512:t0 + (g4 + 1) * 512].unsqueeze(1), None)
            wkv = I["w_kv"].rearrange("(kc p) n -> p kc n", p=128)

            def load_w2(c0, n):
                k = self.wslot % 2
                self.wslot += 1
                self.dma("pool", wts[k][:, :, 0:n], wkv[:, :, c0:c0 + n], [], [B("wp", k)])
                return wts[k], B("wp", k)

            def job_T2(c0, n, rope, dst_fn):
                wt, wb = load_w2(c0, n)
                prev = None
                for tt in tiles:
                    pt, pb = project(tt, wt, wb, n)
                    e, eb = evac(pt, pb, n)
                    if prev is not None:
                        rope_T_store(*prev)
                    prev = (tt, e, eb, n, rope, 128, dst_fn, None)
                rope_T_store(*prev)

            def job_V2(c0, n, dst_fn):
                wt, wb = load_w2(c0, n)
                for tt in tiles:
                    pt, pb = project(tt, wt, wb, n)
                    e, eb = evac(pt, pb, n)
                    self.dma("sp", dst_fn(tt), e[:, 0:n], [eb], [B("scr")])

            job_T2(0, 128, True, lambda g4: [S["L_bK"][:, t0 + g4 * 512:t0 + (g4 + 1) * 512]])
            job_V2(128, 128, lambda tt: S["L_bV"][t0 + tt * 128:t0 + (tt + 1) * 128, :])
            job_T2(256, 256, True,
                   lambda g4: S["kwT"][0:2, :, t0 + g4 * 512:t0 + (g4 + 1) * 512].rearrange("h d t -> d h t"))
            job_V2(512, 256, lambda tt: S["vw"][t0 + tt * 128:t0 + (tt + 1) * 128, :])
            job_T2(768, 256, True, lambda g4: [S["L_ckT"][i][:, t0 + g4 * 512:t0 + (g4 + 1) * 512] for i in range(2)])
            job_V2(1024, 256, lambda tt: S["L_cv"][pas][tt * 128:(tt + 1) * 128, :])
        else:
            own = range(8)
            for hg in range(2):
                job_T(O_AQ + hg * 512, 512, own, True, 128,
                      lambda g4, hg=hg: S["aqT"][hg * 4:(hg + 1) * 4, :, g4 * 512:(g4 + 1) * 512].rearrange("h d t -> d h t"))
            for hg in range(2):
                job_T(O_IQ + hg * 512, 512, own, True, 64,
                      lambda g4, hg=hg: S["iqT"][hg * 4:(hg + 1) * 4, :, g4 * 512:(g4 + 1) * 512].rearrange("h d t -> d h t"))
            for hg in range(2):
                job_T(O_BQ + hg * 512, 512, own, True, 128,
                      lambda g4, hg=hg: S["bqT"][hg * 4:(hg + 1) * 4, :, g4 * 512:(g4 + 1) * 512].rearrange("h d t -> d h t"))
            for hg in range(2):
                job_T(O_CQ + hg * 512, 512, own, True, 128,
                      lambda g4, hg=hg: S["cqT"][hg * 4:(hg + 1) * 4, :, g4 * 512:(g4 + 1) * 512].rearrange("h d t -> d h t"))
            sm = self.sb(p, "sm", [128, 24], F32)
            wt, wb = load_w(O_IW, 16)
            for tt in own:
                pt, pb = project(tt, wt, wb, 16)
                self.act(sm[:, 0:16], pt[:, 0:16], AF.Copy, [pb], [B("sm")], scale=(16 ** -0.5) * (64 ** -0.5))
                self.dma("sp", S["iw"][tt * 128:(tt + 1) * 128, :], sm[:, 0:16], [B("sm")], [B("scr")])
            wt, wb = load_w(O_BG, 24)
            for tt in own:
                pt, pb = project(tt, wt, wb, 24)
                self.act(sm[:, 0:24], pt[:, 0:24], AF.Sigmoid, [pb], [B("sm")])
                self.dma("sp", S["bg"][tt * 128:(tt + 1) * 128, :], sm[:, 0:24], [B("sm")], [B("scr")])

    def phase_compress(self):
        fw, I, S, B = self.fw, self.I, self.S, self.fw.B
        with ExitStack() as p:
            w1 = self.sb(p, "cw1", [128, 32, 128], BF16)
            w2 = self.sb(p, "cw2", [128, 128], BF16)
            peT = self.sb(p, "cpeT", [128, 32], F32)
            peTb = self.sb(p, "cpeTb", [128, 32], BF16)
            cb = self.sb(p, "ccb", [128, 1], F32)
            xT = self.sb(p, "cxT", [128, L], BF16)
            vtm = self.sb(p, "cvtm", [128, 32, 128], BF16)
            hT = self.sb(p, "chT", [128, 256], BF16)
            cov = self.sb(p, "cov", [128, 2, 64], BF16)
            self.ms("dve", self.cmpR[:].rearrange("p a b c -> p (a b c)"), 0.0, [B("cmpR")])
            self.ms("dve", self.kcT[:].rearrange("p a b -> p (a b)"), 0.0, [B("kcT")])
            self.ms("dve", hT[:], 0.0, [B("chT")])
            for c in range(2):
                self.ms("pool", cov[:, c, :], 1.0, [B("cov")])
                fw.op("pool", lambda e, c=c: e.affine_select(cov[:, c, :], cov[:, c, :], pattern=[[64, 64]], compare_op=ALU.is_gt,
                                                             fill=0.0, base=64 - 2048 * c, channel_multiplier=-16),
                      [B("cov")], [B("cov")])
                fw.op("pool", lambda e, c=c: e.affine_select(cov[:, c, :], cov[:, c, :], pattern=[[-64, 64]], compare_op=ALU.is_gt,
                                                             fill=0.0, base=32 + 2048 * c, channel_multiplier=16),
                      [B("cov")], [B("cov")])
            fw.op("pool", lambda e: e.affine_select(cov[:, 1, :], cov[:, 1, :], pattern=[[0, 64]], compare_op=ALU.is_ge,
                                                    fill=0.0, base=126, channel_multiplier=-1),
                  [B("cov")], [B("cov")])
            for g in range(2):
                for c in range(2):
                    self.cp("dve", self.cmpR[:, g, c, 129:193], cov[:, c, :], [B("cov"), B("cmpR")], [B("cmpR")])
                    self.ms("dve", self.cmpR[:, g, c, 128:129], 1.0, [B("cmpR")])
            for kv in range(2):
                self.dma("pool", w1[:], I["cmp_w1"][kv].rearrange("(j d) f -> d j f", d=128), [], [B("cw1")])
                self.dma("pool", w2[:], I["cmp_w2"][kv], [], [B("cw2")])
                self.dma("sp", peT[:], I["cmp_peT"][kv], [], [B("cpeT")])
                self.cp("dve", peTb[:], peT[:], [B("cpeT")], [B("cpeTb")])
                pt, pb = self.psum(0)
                for j in range(32):
                    self.mm(pt[:, 0:1], w1[:, j, :], peTb[:, j:j + 1], j == 0, j == 31, [B("cw1"), B("cpeTb")], [pb])
                self.cp("dve", cb[:], pt[:, 0:1], [pb], [B("ccb")])
                for g in range(2):
                    if kv == 0:
                        self.dma("sp", xT[:], S["G_bK"][g * 128:(g + 1) * 128, :], [B("scr")], [B("cxT")])
                    else:
                        self.dma("sp", vtm[:], S["G_bV"][g * L:(g + 1) * L, :].rearrange("(c p) d -> p c d", p=128), [B("scr")], [B("cvtm")])
                        for c4 in range(8):
                            ptt, ptb_ = self.psum(4 + c4 % 4)
                            ptv = ptt[:].bitcast(BF16)
                            for j_ in range(4):
                                self.tr(ptv[:, j_ * 128:(j_ + 1) * 128], vtm[:, c4 * 4 + j_, :], [B("cvtm")], [ptb_])
                            self.cp("dve", xT[:, c4 * 512:(c4 + 1) * 512], ptv[:, 0:512], [ptb_], [B("cxT")])
                    pt, pb = self.psum(1)
                    xv = xT[:].rearrange("p (i s) -> p s i", s=16)
                    for j in range(32):
                        s_, i0 = j % 16, j // 16
                        self.mm(pt[:, 0:255], w1[:, j, :], xv[:, s_, i0:i0 + 255], j == 0, j == 31, [B("cw1"), B("cxT")], [pb])
                    self.act(hT[:, 0:255], pt[:, 0:255], AF.Silu, [pb, B("ccb")], [B("chT")], bias=cb[:, 0:1])
                    if kv == 0:
                        pt2, pb2 = self.psum(2)
                        self.mm(pt2[:, 0:256], w2[:], hT[:, 0:256], True, True, [B("cw2"), B("chT")], [pb2])
                        self.cp("dve", self.kcT[:, g, 0:255], pt2[:, 0:255], [pb2], [B("kcT")])
                    else:
                        for c in range(2):
                            pt2, pb2 = self.psum(2 + c)
                            self.mm(pt2[:, 0:128], hT[:, c * 128:(c + 1) * 128], w2[:], True, True, [B("cw2"), B("chT")], [pb2])
                            npart = 128 if c == 0 else 127
                            self.cp("dve", self.cmpR[0:npart, g, c, 0:128], pt2[0:npart, 0:128], [pb2], [B("cmpR")])
            fw.fence()

    def att_head(self, p, kT, kTb, nchunks, qT_ap, qb, V_fn, dv, mask_fn, scale, po_banks, tagbase, exps):
        B = self.fw.B
        LOOK = 4
        pend = []
        for it in range(nchunks + LOOK):
            if it < nchunks:
                kc = it
                pt, pb = self.psum(kc % 4)
                self.mm(pt[:, :], kT[:, kc * 128:(kc + 1) * 128], qT_ap, True, True, [kTb, qb], [pb])
                k = self.expslot % len(exps)
                self.expslot += 1
                e, eb = exps[k], B("exp", k)
                self.act(e[:], pt[:, :], AF.Exp, [pb], [eb], scale=scale)
                m_ = mask_fn(kc)
                if m_ is not None:
                    m_ap, mb = m_
                    self.tt("dve", e[:], e[:], m_ap, ALU.mult, [eb, mb], [eb])
                pend.append((e, eb))
            if it >= LOOK:
                kc = it - LOOK
                e, eb = pend[kc]
                v_ap, vb = V_fn(kc)
                for qt in range(4):
                    po, pob = self.psum(po_banks[qt])
                    self.mm(po[:, 0:dv + 1], e[:, qt * 128:(qt + 1) * 128], v_ap, kc == 0, kc == nchunks - 1, [eb, vb], [pob])

    def phase_att_a(self):
        fw, I, S, B = self.fw, self.I, self.S, self.fw.B
        NIT = 18
        R0 = 64.0
        with ExitStack() as p:
            aqT = self.sb(p, "aqT", [128, 8, NOWN], BF16)
            iqT = self.sb(p, "iqT", [128, 8, NOWN], BF16)
            ikT = self.sb(p, "ikT", [128, L], BF16)
            iw = self.sb(p, "iw", [128, 8, 16], F32)
            kpos = self.sb(p, "kpos", [128, 512], F32)
            kposi = self.sb(p, "kposi", [128, 512], I32)
            qsh = self.sb(p, "qsh", [128, 8, 8], F32)
            acc = self.sb(p, "acc", [128, L], F32)
            acc2 = self.sb(p, "acc2", [128, L], F32)
            rl = [self.sb(p, f"rl{i}", [128, 512], BF16) for i in range(2)]
            bs = self.sb(p, "bs", [128, 8], F32)
            bs2 = self.sb(p, "bs2", [128, 8], F32)
            mk = self.sb(p, "mk", [128, L], BF16)
            mkT = self.sb(p, "mkT", [128, 32, 512], BF16)
            kTs = [self.sb(p, f"akT{i}", [128, L], BF16) for i in range(2)]
            Vs = [self.sb(p, f"aV{i}", [128, 32, 129], BF16) for i in range(2)]
            exps = [self.sb(p, f"aexp{i}", [128, 512], BF16) for i in range(6)]
            fin = self.sb(p, "afin", [128, 4], F32)
            yo = [self.sb(p, f"ayo{i}", [128, 1024], BF16) for i in range(4)]
            self.expslot = 0
            for h in range(8):
                self.dma("sp", aqT[:, h, :], S["aqT"][h], [B("scr")], [B("aqT")])
                self.dma("sp", iqT[:, h, :], S["iqT"][h], [B("scr")], [B("iqT")])
            self.dma("sp", ikT[:], S["ikT"][:, :], [B("scr")], [B("ikT")])
            self.dma("sp", iw[:], S["iw"].rearrange("(t p) h -> p t h", p=128), [B("scr")], [B("iw")])
            fw.op("pool", lambda e: e.iota(kposi[:], pattern=[[1, 512]], base=0, channel_multiplier=0), [], [B("kposi")])
            self.cp("dve", kpos[:], kposi[:], [B("kposi")], [B("kpos")])
            for c in range(8):
                self.ts("dve", qsh[:, :, c], self.POSq[:, 0:8], -512.0 * c, None, ALU.add, None, [B("POSq")], [B("qsh")])
            for i in range(2):
                self.ms("dve", Vs[i][:, :, 128:129], 1.0, [B("aV", i)])
            accs = [acc, acc2]
            bss = [bs, bs2]

            def gen_index(qt):
                Sk_ = 2048 if qt < 4 else 4096
                ac = accs[qt % 2]
                ab = lambda c: B("acc", qt % 2, c)
                for c in range(Sk_ // 512):
                    self.ts("dve", ac[:, c * 512:(c + 1) * 512], kpos[:], qsh[:, qt, c:c + 1], NEG, ALU.is_gt, ALU.mult,
                            [B("kpos"), B("qsh")], [ab(c)])
                for h in range(16):
                    hp = (h % 2) * 64
                    for c in range(Sk_ // 512):
                        pt, pb = self.psum((h * 8 + c) % 4)
                        self.mm(pt[:, :], iqT[hp:hp + 64, h // 2, qt * 128:(qt + 1) * 128], ikT[hp:hp + 64, c * 512:(c + 1) * 512],
                                True, True, [B("iqT"), B("ikT")], [pb])
                        k = (h * 8 + c) % 2
                        if (h * 8 + c) % 5 == 4:
                            self.ts("dve", rl[k][:], pt[:, :], 0.0, None, ALU.max, None, [pb], [B("rl", k)])
                        else:
                            self.act(rl[k][:], pt[:, :], AF.Relu, [pb], [B("rl", k)])
                        self.stt(ac[:, c * 512:(c + 1) * 512], rl[k][:], iw[:, qt, h:h + 1], ac[:, c * 512:(c + 1) * 512],
                                 ALU.mult, ALU.add, [B("rl", k), B("iw"), ab(c)], [ab(c)])
                    yield

            def gen_bisect(qt):
                Sk_ = 2048 if qt < 4 else 4096
                nch_ = Sk_ // 128
                qt4 = qt % 4
                ac = accs[qt % 2]
                bsq = bss[qt % 2]
                bb = B("bs", qt % 2)
                accb = [B("acc", qt % 2, c) for c in range(Sk_ // 512)]
                self.ms("dve", bsq[:, 0:1], 0.0, [bb])
                self.ms("dve", bsq[:, 3:4], float(Sk_ - 511), [bb])
                for it in range(NIT):
                    self.act(mk[:, 0:Sk_], ac[:, 0:Sk_], AF.Sign, accb + [bb], [B("mk"), bb],
                             bias=bsq[:, 0:1], scale=1.0, accum=bsq[:, 1:2])
                    self.act(bsq[:, 2:3], bsq[:, 1:2], AF.Sign, [bb], [bb], bias=bsq[:, 3:4], scale=1.0)
                    if it < NIT - 1:
                        w_next = R0 * (0.5 ** (it + 1))
                        self.act(bsq[:, 0:1], bsq[:, 2:3], AF.Identity, [bb], [bb], bias=bsq[:, 0:1], scale=-w_next)
                    else:
                        w_it = R0 * (0.5 ** it)
                        self.ts("dve", bsq[:, 4:5], bsq[:, 2:3], 0.5 * w_it, -0.5 * w_it, ALU.mult, ALU.add, [bb], [bb])
                        self.tt("dve", bsq[:, 0:1], bsq[:, 4:5], bsq[:, 0:1], ALU.subtract, [bb], [bb])
                    yield
                self.ts("dve", mk[:, 0:Sk_], ac[:, 0:Sk_], bsq[:, 0:1], None, ALU.is_ge, None, accb + [bb], [B("mk")])
                for c4 in range(nch_ // 4):
                    pt, pb = self.psum(4 + c4 % 4)
                    ptb = pt[:].bitcast(BF16)
                    for j in range(4):
                        kc = c4 * 4 + j
                        self.tr(ptb[:, j * 128:(j + 1) * 128], mk[:, kc * 128:(kc + 1) * 128], [B("mk")], [pb])
                    self.cp("act", mkT[:, c4 * 4:(c4 + 1) * 4, qt4 * 128:(qt4 + 1) * 128],
                            ptb[:, 0:512].rearrange("p (c t) -> p c t", t=128), [pb], [B("mkT")])
                yield

            for _ in gen_index(0):
                pass
            for slot in range(2):
                nch = 16 if slot == 0 else 32
                Sk = nch * 128
                for qt4 in range(4):
                    qt = slot * 4 + qt4
                    gb = gen_bisect(qt)
                    gn = gen_index(qt + 1) if qt < 7 else None
                    done_b, done_n = False, gn is None
                    while not (done_b and done_n):
                        if not done_b:
                            try:
                                next(gb)
                            except StopIteration:
                                done_b = True
                        if not done_n:
                            try:
                                next(gn)
                            except StopIteration:
                                done_n = True
                for h in range(8):
                    k = h % 2
                    self.dma("sp", kTs[k][:, 0:Sk], S["G_akT"][h % 2][(h // 2) * 128:(h // 2 + 1) * 128, 0:Sk], [B("scr")], [B("akT", k)])
                    for hf in range(Sk // 2048):
                        self.dma("sp", Vs[k][:, hf * 16:(hf + 1) * 16, 0:128],
                                 S["G_av"][hf][(h // 2) * 2048:(h // 2 + 1) * 2048, (h % 2) * 128:(h % 2 + 1) * 128].rearrange("(c p) d -> p c d", p=128),
                                 [B("scr")], [B("aV", k)])
                    self.att_head(p, kTs[k], B("akT", k), nch, aqT[:, h, slot * 512:(slot + 1) * 512], B("aqT"),
                                  lambda kc, k=k: (Vs[k][:, kc, :], B("aV", k)), 128,
                                  lambda kc: (mkT[:, kc, :], B("mkT")), 128 ** -0.5, [4, 5, 6, 7], "a", exps)
                    for qt4 in range(4):
                        po, pob = self.psum(4 + qt4)
                        self.ts("dve", fin[:, 0:1], po[:, 128:129], 1e-30, None, ALU.max, None, [pob], [B("afin")])
                        self.recip(fin[:, 1:2], fin[:, 0:1], [B("afin")], [B("afin")])
                        self.ts("dve", yo[qt4][:, h * 128:(h + 1) * 128], po[:, 0:128], fin[:, 1:2], None, ALU.mult, None,
                                [pob, B("afin")], [B("ayo", qt4)])
                for qt4 in range(4):
                    qt = slot * 4 + qt4
                    self.dma("sp", S["y"][qt * 128:(qt + 1) * 128, 0:1024], yo[qt4][:], [B("ayo", qt4)], [B("s_y")])
            fw.fence()

    def phase_att_b(self):
        fw, I, S, B = self.fw, self.I, self.S, self.fw.B
        sc = 128 ** -0.5
        with ExitStack() as p:
            bqT = self.sb(p, "bqT", [128, 8, NOWN], BF16)
            bg = self.sb(p, "bg", [128, 8, 24], F32)
            E = self.sb(p, "E", [64, L], BF16)
            cend = self.sb(p, "cend", [128, 2], F32)
            cendi = self.sb(p, "cendi", [128, 2], I32)
            m64 = self.sb(p, "m64", [128, 64], F32)
            m64i = self.sb(p, "m64i", [128, 64], I32)
            kposc = self.sb(p, "kposc", [128, 64], F32)
            kposci = self.sb(p, "kposci", [128, 32], I32)
            mc = self.sb(p, "mc", [128, 2, 128], BF16)
            ec = self.sb(p, "ec", [128, 2, 128], BF16)
            imp = self.sb(p, "imp", [128, 2, 64], F32)
            t64 = self.sb(p, "t64", [128, 4, 64], F32)
            m8 = self.sb(p, "m8", [128, 16], F32)
            sel = self.sb(p, "sel", [128, 64], BF16)
            selT = self.sb(p, "selT", [64, 2, 512], BF16)
            fin = self.sb(p, "bfin", [128, 8], F32)
            yo = [self.sb(p, f"byo{i}", [128, 1024], F32) for i in range(4)]
            yob = self.sb(p, "byob", [128, 1024], BF16)
            msk = self.sb(p, "bmsk", [128, 32, 512], BF16)
            mtmp = self.sb(p, "bmtmp", [128, 512], BF16)
            kTs = [self.sb(p, f"bkT{i}", [128, L], BF16) for i in range(2)]
            Vs = [self.sb(p, f"bV{i}", [128, 32, 129], BF16) for i in range(2)]
            exps = [self.sb(p, f"bexp{i}", [128, 512], BF16) for i in range(6)]
            self.expslot = 0
            for h in range(8):
                self.dma("sp", bqT[:, h, :], S["bqT"][h], [B("scr")], [B("bqT")])
            self.dma("sp", bg[:], S["bg"].rearrange("(t p) h -> p t h", p=128), [B("scr")], [B("bg")])
            self.ms("pool", E[:], 1.0, [B("E")])
            fw.op("pool", lambda e: e.affine_select(E[:], E[:], pattern=[[1, L]], compare_op=ALU.is_ge, fill=0.0, base=0,
                                                    channel_multiplier=-64), [B("E")], [B("E")])
            fw.op("pool", lambda e: e.affine_select(E[:], E[:], pattern=[[-1, L]], compare_op=ALU.is_ge, fill=0.0, base=63,
                                                    channel_multiplier=64), [B("E")], [B("E")])
            fw.op("pool", lambda e: e.iota(cendi[:], pattern=[[2048, 2]], base=31, channel_multiplier=16), [], [B("cendi")])
            self.cp("dve", cend[:], cendi[:], [B("cendi")], [B("cend")])
            fw.op("pool", lambda e: e.iota(m64i[:], pattern=[[64, 64]], base=0, channel_multiplier=0), [], [B("m64i")])
            self.cp("dve", m64[:], m64i[:], [B("m64i")], [B("m64")])
            fw.op("pool", lambda e: e.iota(kposci[:], pattern=[[128, 32]], base=0, channel_multiplier=1), [], [B("kposci")])
            self.cp("dve", kposc[:, 0:32], kposci[:], [B("kposci")], [B("kposc")])
            self.ts("dve", kposc[:, 32:64], kposc[:, 0:32], 512.0, None, ALU.add, None, [B("kposc")], [B("kposc")])
            for i in range(2):
                self.ms("dve", Vs[i][:, :, 128:129], 1.0, [B("bV", i)])

            for slot in range(2):
                nch = 16 if slot == 0 else 32
                Sk = nch * 128
                qB = self.qposB[:, slot * 512:(slot + 1) * 512]
                for qt4 in range(4):
                    self.ms("dve", yo[qt4][:], 0.0, [B("byo", qt4)])
                for qt4 in range(4):
                    qt = slot * 4 + qt4
                    qcol = self.POSq[:, qt:qt + 1]
                    qBt = self.qposB[:, qt * 128:(qt + 1) * 128]
                    for c in range(2):
                        self.ts("dve", mc[:, c, :], qBt, cend[:, c:c + 1], None, ALU.is_ge, None, [B("qposB"), B("cend")], [B("mc")])
                    for g in range(2):
                        self.ms("dve", imp[:, g, :], 0.0, [B("imp", g)])
                    for h in range(8):
                        g = h // 4
                        pt, pb = self.psum(h % 3)
                        for c in range(2):
                            self.mm(pt[:, c * 128:(c + 1) * 128], self.kcT[:, g, c * 128:(c + 1) * 128], bqT[:, h, qt * 128:(qt + 1) * 128],
                                    True, True, [B("kcT"), B("bqT")], [pb])
                        self.act(ec[:].rearrange("p a b -> p (a b)"), pt[:, 0:256], AF.Exp, [pb], [B("ec")], scale=sc)
                        self.tt("dve", ec[:], ec[:], mc[:], ALU.mult, [B("ec"), B("mc")], [B("ec")])
                        po, pob = self.psum(3)
                        for c in range(2):
                            self.mm(po[:, 0:193], ec[:, c, :], self.cmpR[:, g, c, :], c == 0, c == 1, [B("ec"), B("cmpR")], [pob])
                        self.ts("dve", fin[:, 0:1], po[:, 128:129], 1e-30, None, ALU.max, None, [pob], [B("bfin")])
                        self.recip(fin[:, 1:2], fin[:, 0:1], [B("bfin")], [B("bfin")])
                        self.tt("dve", fin[:, 2:3], fin[:, 1:2], bg[:, qt, h * 3:h * 3 + 1], ALU.mult, [B("bfin"), B("bg")], [B("bfin")])
                        self.ts("dve", yo[qt4][:, h * 128:(h + 1) * 128], po[:, 0:128], fin[:, 2:3], None, ALU.mult, None,
                                [pob, B("bfin")], [B("byo", qt4)])
                        self.stt(imp[:, g, :], po[:, 129:193], fin[:, 1:2], imp[:, g, :], ALU.mult, ALU.add,
                                 [pob, B("bfin"), B("imp", g)], [B("imp", g)])
                    for g in range(2):
                        bt = B("t64")
                        self.ts("dve", t64[:, 0, :], m64[:], qcol, None, ALU.is_le, None, [B("m64"), B("POSq")], [bt])
                        self.ts("dve", t64[:, 1, :], m64[:], 128.0, qcol, ALU.add, ALU.is_gt, [B("m64"), B("POSq")], [bt])
                        self.tt("dve", t64[:, 1, :], t64[:, 1, :], t64[:, 0, :], ALU.mult, [bt], [bt])
                        self.ts("dve", t64[:, 2, :], m64[:], 0.0, None, ALU.is_equal, None, [B("m64")], [bt])
                        self.tt("dve", t64[:, 1, :], t64[:, 1, :], t64[:, 2, :], ALU.max, [bt], [bt])
                        self.stt(t64[:, 3, :], t64[:, 1, :], 1e6, imp[:, g, :], ALU.mult, ALU.add, [bt, B("imp", g)], [bt])
                        self.ts("dve", t64[:, 2, :], t64[:, 0, :], -1.0, -NEG, ALU.add, ALU.mult, [bt], [bt])
                        self.tt("dve", t64[:, 3, :], t64[:, 3, :], t64[:, 2, :], ALU.add, [bt], [bt])
                        fw.op("dve", lambda e: e.max(m8[:, 0:8], t64[:, 3, :]), [bt], [B("m8")])
                        fw.op("dve", lambda e: e.match_replace(t64[:, 2, :], m8[:, 0:8], t64[:, 3, :], -3e38), [bt, B("m8")], [bt])
                        fw.op("dve", lambda e: e.max(m8[:, 8:16], t64[:, 2, :]), [bt], [B("m8")])
                        self.ts("dve", sel[:], t64[:, 3, :], m8[:, 15:16], None, ALU.is_ge, None, [bt, B("m8")], [B("sel")])
                        pt, pb = self.psum(g)
                        ptb = pt[:].bitcast(BF16)
                        self.tr(ptb[0:64, 0:128], sel[:], [B("sel")], [pb])
                        self.cp("dve", selT[:, g, qt4 * 128:(qt4 + 1) * 128], ptb[0:64, 0:128], [pb], [B("selT", g)])
                for g in range(2):
                    k = g % 2
                    self.dma("sp", kTs[k][:, 0:Sk], S["G_bK"][(2 + g) * 128:(3 + g) * 128, 0:Sk], [B("scr")], [B("bkT", k)])
                    self.dma("sp", Vs[k][:, 0:nch, 0:128], S["G_bV"][(2 + g) * L:(2 + g) * L + Sk, :].rearrange("(c p) d -> p c d", p=128),
                             [B("scr")], [B("bV", k)])
                    for kc in range(nch):
                        pt, pb = self.psum(3)
                        self.mm(pt[:, :], E[:, kc * 128:(kc + 1) * 128], selT[:, g, :], True, True, [B("E"), B("selT", g)], [pb])
                        if slot == 1 and kc < 16:
                            self.cp("act", msk[:, kc, :], pt[:, :], [pb], [B("bmsk", kc)])
                        else:
                            self.ts("dve", mtmp[:], qB, kposc[:, kc:kc + 1], None, ALU.is_ge, None, [B("qposB"), B("kposc")], [B("bmtmp")])
                            self.tt("dve", msk[:, kc, :], pt[:, :], mtmp[:], ALU.mult, [pb, B("bmtmp")], [B("bmsk", kc)])
                    for h4 in range(4):
                        h = g * 4 + h4
                        self.att_head(p, kTs[k], B("bkT", k), nch, bqT[:, h, slot * 512:(slot + 1) * 512], B("bqT"),
                                      lambda kc, k=k: (Vs[k][:, kc, :], B("bV", k)), 128,
                                      lambda kc: (msk[:, kc, :], B("bmsk", kc)), sc, [4, 5, 6, 7], "b", exps)
                        self.b_finish(slot, h, 1, fin, bg, yo)
                c0, c1 = (0, 16) if slot == 0 else (12, 32)
                nw = c1 - c0
                for g in range(2):
                    k = g % 2
                    self.dma("sp", kTs[k][:, 0:nw * 128], S["kwT"][g, :, c0 * 128:c1 * 128], [B("scr")], [B("bkT", k)])
                    self.dma("sp", Vs[k][:, 0:nw, 0:128],
                             S["vw"][c0 * 128:c1 * 128, g * 128:(g + 1) * 128].rearrange("(c p) d -> p c d", p=128),
                             [B("scr")], [B("bV", k)])
                    if g == 0:
                        for kc in range(nw):
                            self.ts("dve", mtmp[:], qB, kposc[:, c0 + kc:c0 + kc + 1], None, ALU.is_ge, None,
                                    [B("qposB"), B("kposc")], [B("bmtmp")])
                            self.stt(msk[:, kc, :], qB, kposc[:, 32 + c0 + kc:33 + c0 + kc], mtmp[:], ALU.is_lt, ALU.mult,
                                     [B("qposB"), B("kposc"), B("bmtmp")], [B("bmsk", kc)])
                    for h4 in range(4):
                        h = g * 4 + h4
                        self.att_head(p, kTs[k], B("bkT", k), nw, bqT[:, h, slot * 512:(slot + 1) * 512], B("bqT"),
                                      lambda kc, k=k: (Vs[k][:, kc, :], B("bV", k)), 128,
                                      lambda kc: (msk[:, kc, :], B("bmsk", kc)), sc, [4, 5, 6, 7], "b", exps)
                        self.b_finish(slot, h, 2, fin, bg, yo)
                for qt4 in range(4):
                    qt = slot * 4 + qt4
                    self.cp("dve", yob[:], yo[qt4][:], [B("byo", qt4)], [B("byob")])
                    self.dma("sp", S["y"][qt * 128:(qt + 1) * 128, 1024:2048], yob[:], [B("byob")], [B("s_y")])
            fw.fence()

    def b_finish(self, slot, h, br, fin, bg, yo):
        B = self.fw.B
        for qt4 in range(4):
            qt = slot * 4 + qt4
            po, pob = self.psum(4 + qt4)
            self.ts("dve", fin[:, 0:1], po[:, 128:129], 1e-30, None, ALU.max, None, [pob], [B("bfin")])
            self.recip(fin[:, 1:2], fin[:, 0:1], [B("bfin")], [B("bfin")])
            self.tt("dve", fin[:, 2:3], fin[:, 1:2], bg[:, qt, h * 3 + br:h * 3 + br + 1], ALU.mult, [B("bfin"), B("bg")], [B("bfin")])
            self.stt(yo[qt4][:, h * 128:(h + 1) * 128], po[:, 0:128], fin[:, 2:3], yo[qt4][:, h * 128:(h + 1) * 128],
                     ALU.mult, ALU.add, [pob, B("bfin"), B("byo", qt4)], [B("byo", qt4)])

    def phase_att_c(self):
        fw, I, S, B = self.fw, self.I, self.S, self.fw.B
        sc = 128 ** -0.5
        with ExitStack() as p:
            cqT = self.sb(p, "cqT", [128, 8, NOWN], BF16)
            kposc = self.sb(p, "ckposc", [128, 32], F32)
            kposci = self.sb(p, "ckposci", [128, 32], I32)
            msk = self.sb(p, "cmsk", [128, 32, 512], BF16)
            kTs = [self.sb(p, f"ckT{i}", [128, L], BF16) for i in range(2)]
            Vs = [self.sb(p, f"cV{i}", [128, 32, 257], BF16) for i in range(2)]
            exps = [self.sb(p, f"cexp{i}", [128, 512], BF16) for i in range(6)]
            o0 = [self.sb(p, f"co0{i}", [128, 256], F32) for i in range(4)]
            dd = self.sb(p, "cdd", [128, 256], F32)
            junk = self.sb(p, "cjunk", [128, 256], F32)
            gB = self.sb(p, "cgB", [128, 256], F32)
            fin = self.sb(p, "cfin", [128, 8], F32)
            yo = [self.sb(p, f"cyo{i}", [128, 1024], BF16) for i in range(4)]
            self.expslot = 0
            for h in range(8):
                self.dma("sp", cqT[:, h, :], S["cqT"][h], [B("scr")], [B("cqT")])
            self.dma("sp", gB[:], I["c_subln_g"].partition_broadcast(128), [], [B("cgB")])
            fw.op("pool", lambda e: e.iota(kposci[:], pattern=[[128, 32]], base=0, channel_multiplier=1), [], [B("ckposci")])
            self.cp("dve", kposc[:], kposci[:], [B("ckposci")], [B("ckposc")])
            for i in range(2):
                self.ms("dve", Vs[i][:, :, 256:257], 1.0, [B("cV", i)])
            for slot in range(2):
                nch = 16 if slot == 0 else 32
                Sk = nch * 128
                qB = self.qposB[:, slot * 512:(slot + 1) * 512]
                vis = 0 if slot == 0 else 16
                for kc in range(vis, nch):
                    self.ts("dve", msk[:, kc, :], qB, kposc[:, kc:kc + 1], None, ALU.is_ge, None, [B("qposB"), B("ckposc")], [B("cmsk", kc)])
                for h in range(4):
                    vk = h % 2
                    for hf in range(Sk // 2048):
                        self.dma("sp", Vs[vk][:, hf * 16:(hf + 1) * 16, 0:256],
                                 S["G_cv"][hf][h * 2048:(h + 1) * 2048, :].rearrange("(c p) d -> p c d", p=128),
                                 [B("scr")], [B("cV", vk)])
                    for m in range(2):
                        hm = h * 2 + m
                        k = hm % 2
                        self.dma("sp", kTs[k][:, 0:Sk], S["G_ckT"][m][h * 128:(h + 1) * 128, 0:Sk], [B("scr")], [B("ckT", k)])
                        self.att_head(p, kTs[k], B("ckT", k), nch, cqT[:, hm, slot * 512:(slot + 1) * 512], B("cqT"),
                                      lambda kc, vk=vk: (Vs[vk][:, kc, :], B("cV", vk)), 256,
                                      lambda kc, vis=vis: ((msk[:, kc, :], B("cmsk", kc)) if kc >= vis else None), sc, [4, 5, 6, 7], "c", exps)
                        for qt4 in range(4):
                            po, pob = self.psum(4 + qt4)
                            self.ts("dve", fin[:, 0:1], po[:, 256:257], 1e-30, None, ALU.max, None, [pob], [B("cfin")])
                            self.recip(fin[:, 1:2], fin[:, 0:1], [B("cfin")], [B("cfin")])
                            if m == 0:
                                self.ts("dve", o0[qt4][:], po[:, 0:256], fin[:, 1:2], None, ALU.mult, None,
                                        [pob, B("cfin")], [B("co0", qt4)])
                            else:
                                self.tt("dve", fin[:, 2:3], fin[:, 1:2], self.lamv[:, 0:1], ALU.mult, [B("cfin"), B("lamv")], [B("cfin")])
                                self.stt(dd[:], po[:, 0:256], fin[:, 2:3], o0[qt4][:], ALU.mult, ALU.add,
                                         [pob, B("cfin"), B("co0", qt4)], [B("cdd")])
                                self.act(junk[:], dd[:], AF.Square, [B("cdd")], [B("cjunk"), B("cfin")], accum=fin[:, 3:4])
                                self.act(fin[:, 4:5], fin[:, 3:4], AF.Sqrt, [B("cfin")], [B("cfin")], bias=self.epsc[:, 1:2], scale=1.0 / 256)
                                self.recip(fin[:, 5:6], fin[:, 4:5], [B("cfin")], [B("cfin")])
                                self.tt("dve", fin[:, 6:7], fin[:, 5:6], self.lamv[:, 1:2], ALU.mult, [B("cfin"), B("lamv")], [B("cfin")])
                                self.stt(yo[qt4][:, h * 256:(h + 1) * 256], dd[:], fin[:, 6:7], gB[:], ALU.mult, ALU.mult,
                                         [B("cdd"), B("cfin"), B("cgB")], [B("cyo", qt4)])
                for qt4 in range(4):
                    qt = slot * 4 + qt4
                    self.dma("sp", S["y"][qt * 128:(qt + 1) * 128, 2048:3072], yo[qt4][:], [B("cyo", qt4)], [B("s_y")])
            fw.fence()

    def phase_merge(self):
        fw, I, S, B = self.fw, self.I, self.S, self.fw.B
        with ExitStack() as p:
          mg = self.sb(p, "mg", [128, 16, NOWN], BF16)
          with ExitStack() as p:
            yT = self.sb(p, "yT", [128, 24, NOWN], BF16)
            uT = self.sb(p, "muT", [128, 16, NOWN], BF16)
            for kc in range(16):
                self.dma("sp", uT[:, kc, :], S["uT"][kc], [B("s_uT")], [B("muT")])
            with ExitStack() as p1:
                yt = [self.sb(p1, f"yt{i}", [128, 3072], BF16) for i in range(2)]
                for tt in range(8):
                    k = tt % 2
                    self.dma("sp", yt[k][:], S["y"][tt * 128:(tt + 1) * 128, :], [B("s_y")], [B("yt", k)])
                    for c4 in range(6):
                        pt, pb = self.psum(c4 % 4)
                        ptb = pt[:].bitcast(BF16)
                        for j in range(4):
                            c = c4 * 4 + j
                            self.tr(ptb[:, j * 128:(j + 1) * 128], yt[k][:, c * 128:(c + 1) * 128], [B("yt", k)], [pb])
                        self.cp("act" if c4 % 2 else "dve", yT[:, c4 * 4:(c4 + 1) * 4, tt * 128:(tt + 1) * 128],
                                ptb[:, 0:512].rearrange("p (c t) -> p c t", t=128), [pb], [B("yT", tt)])
                fw.fence()
            yTb = [B("yT", tt) for tt in range(8)]
            with ExitStack() as p2:
                wbr = [self.sb(p2, f"wbr{i}", [128, 8, 512], BF16) for i in range(2)]
                wg = [self.sb(p2, f"wg{i}", [128, 16, 512], BF16) for i in range(2)]
                gs = [self.sb(p2, f"gs{i}", [128, 512], F32) for i in range(2)]
                ma = self.sb(p2, "ma", [128, 512], F32)
                mt = self.sb(p2, "mt", [128, 512], F32)
                wiv = I["w_in"].rearrange("(kc p) n -> p kc n", p=128)
                n = 0
                for dg in range(4):
                    for r in range(3):
                        k = n % 2
                        n += 1
                        self.dma("pool", wbr[k][:], I["w_br"][r].rearrange("(kc p) n -> p kc n", p=128)[:, :, dg * 512:(dg + 1) * 512],
                                 [], [B("wbr", k)])
                        self.dma("pool", wg[k][:], wiv[:, :, O_GL + r * 2048 + dg * 512:O_GL + r * 2048 + (dg + 1) * 512],
                                 [], [B("wg", k)])
                        for dc in range(4):
                            for half in range(2):
                                tsl = slice(half * 512, (half + 1) * 512)
                                pg, pgb = self.psum(0 + (dc * 2 + half) % 2)
                                for kc in range(16):
                                    self.mm(pg[:, :], wg[k][:, kc, dc * 128:(dc + 1) * 128], uT[:, kc, tsl], kc == 0, kc == 15,
                                            [B("wg", k), B("muT")], [pgb])
                                gk = (dc * 2 + half) % 2
                                self.act(gs[gk][:], pg[:, :], AF.Sigmoid, [pgb], [B("gs", gk)])
                                pbr, pbrb = self.psum(2 + (dc * 2 + half) % 2)
                                for kc in range(8):
                                    self.mm(pbr[:, :], wbr[k][:, kc, dc * 128:(dc + 1) * 128], yT[:, r * 8 + kc, tsl], kc == 0, kc == 7,
                                            [B("wbr", k)] + yTb[half * 4:(half + 1) * 4], [pbrb])
                                dst = mg[:, dg * 4 + dc, tsl]
                                db = B("mg", dg * 4 + dc, half)
                                if r == 0:
                                    self.tt("dve", dst, pbr[:, :], gs[gk][:], ALU.mult, [pbrb, B("gs", gk)], [db])
                                else:
                                    self.tt("dve", mt[:], pbr[:, :], gs[gk][:], ALU.mult, [pbrb, B("gs", gk)], [B("mt")])
                                    self.tt("dve", dst, dst, mt[:], ALU.add, [db, B("mt")], [db])
                fw.fence()
          fw.fence()
          mgb = lambda tt: [B("mg", c, tt // 4) for c in range(16)]
          with ExitStack() as p3:
              self.dense_out_ln(p3, mg, mgb, 16, I["w_o"], 2,
                                (lambda tt: I["xo"][tt * 128:(tt + 1) * 128, :]) if self.layer == self.layers[0] else
                                (lambda tt: S["xown1"][tt * 128:(tt + 1) * 128, :]), 0,
                                lambda tt: S["x1"][tt * 128:(tt + 1) * 128, :], B("s_x1"), "mo")
          fw.fence()

    def dense_out_ln(self, p, actT, actb, nk, w_dram, g_part, x_src, ln_idx, dst_fn, dstb, tag, halves=1, exchange=False):
        fw, I, S, B = self.fw, self.I, self.S, self.fw.B
        gB = self.sb(p, "gB", [128, D], F32)
        lg = self.sb(p, "lg", [128, D], F32)
        lb = self.sb(p, "lb", [128, D], F32)
        self.dma("sp", gB[:], S["mod"][g_part * D:(g_part + 1) * D].partition_broadcast(128), [B("s_mod")], [B(tag, "gB")])
        self.dma("sp", lg[:], I["ln_g"][ln_idx].partition_broadcast(128), [], [B(tag, "lg")])
        self.dma("sp", lb[:], I["ln_b"][ln_idx].partition_broadcast(128), [], [B(tag, "lb")])
        ntt = 8 // halves
        vb = self.sb(p, "vb", [128, ntt, D], F32)
        NW = 128 if nk > 16 else 512
        ws = [self.sb(p, f"wo{i}", [128, nk, NW], BF16) for i in range(2)]
        zt = self.sb(p, "zt", [128, 512], F32)
        stt_ = self.sb(p, "ost", [128, 4, 6], F32)
        mv = self.sb(p, "omv", [128, 4], F32)
        wv = w_dram.rearrange("(kc p) n -> p kc n", p=128)
        n = 0
        for hf in range(halves):
            for j in range(ntt):
                tt = hf * ntt + j
                self.dma("sp", vb[:, j, :], x_src(tt), [], [B(tag, "vb", j)])
            for ng in range(D // NW):
                k = n % 2
                n += 1
                self.dma("pool", ws[k][:], wv[:, :, ng * NW:(ng + 1) * NW], [], [B(tag, "w", k)])
                for j in range(ntt):
                    tt = hf * ntt + j
                    pt, pb = self.psum(4 + j % 4)
                    for kc in range(nk):
                        self.mm(pt[:, 0:NW], actT[:, kc, tt * 128:(tt + 1) * 128], ws[k][:, kc, :], kc == 0, kc == nk - 1,
                                [B(tag, "w", k)] + actb(tt), [pb])
                    csl = slice(ng * NW, (ng + 1) * NW)
                    self.tt("dve", zt[:, 0:NW], pt[:, 0:NW], gB[:, csl], ALU.mult, [pb, B(tag, "gB")], [B(tag, "zt")])
                    self.stt(vb[:, j, csl], vb[:, j, csl], ALPHA, zt[:, 0:NW], ALU.mult, ALU.add,
                             [B(tag, "vb", j), B(tag, "zt")], [B(tag, "vb", j)])
            for j in range(ntt):
                tt = hf * ntt + j
                vbj = B(tag, "vb", j)
                for c in range(4):
                    fw.op("dve", lambda e, c=c, j=j: e.bn_stats(stt_[:, c, :], vb[:, j, c * 512:(c + 1) * 512]), [vbj], [B(tag, "st")])
                fw.op("dve", lambda e: e.bn_aggr(mv[:, 0:2], stt_[:].rearrange("p a b -> p (a b)")), [B(tag, "st")], [B(tag, "mv")])
                self.act(mv[:, 2:3], mv[:, 1:2], AF.Sqrt, [B(tag, "mv")], [B(tag, "mv")], bias=self.epsc[:, 0:1])
                self.recip(mv[:, 3:4], mv[:, 2:3], [B(tag, "mv")], [B(tag, "mv")])
                self.ts("dve", vb[:, j, :], vb[:, j, :], mv[:, 0:1], mv[:, 3:4], ALU.subtract, ALU.mult, [vbj, B(tag, "mv")], [vbj])
                self.tt("dve", vb[:, j, :], vb[:, j, :], lg[:], ALU.mult, [vbj, B(tag, "lg")], [vbj])
                self.tt("dve", vb[:, j, :], vb[:, j, :], lb[:], ALU.add, [vbj, B(tag, "lb")], [vbj])
                if exchange:
                    xb_ = B("xown1", tt)
                    self.fw.dma("sp", dst_fn(tt), vb[:, j, :], [vbj], [xb_])
                    xo1, G = self.S["xown1"], self.S["G"]
                    self.fw.coll("pool", lambda e, tt=tt: e.collective_compute(
                        "AllGather", ALU.bypass, replica_groups=[[0, 1, 2, 3], [4, 5, 6, 7]],
                        ins=[xo1[tt * 128:(tt + 1) * 128, :]], outs=[G[tt]]), [xb_], [B("G")])
                else:
                    self.dma("sp", dst_fn(tt), vb[:, j, :], [vbj], [dstb])

    def phase_ffn(self, xout, exchange=False):
        fw, I, S, B = self.fw, self.I, self.S, self.fw.B
        with ExitStack() as p:
            hT = self.sb(p, "hT", [128, 44, NOWN], BF16)
            with ExitStack() as p1:
                u2 = self.sb(p1, "u2T", [128, 16, NOWN], BF16)
                with ExitStack() as pl:
                    self.ln_to_uT(pl, lambda i: S["x1"][i * 128:(i + 1) * 128, :], 8, u2, lambda kc, grp: B("u2T", grp), 2, "ln2")
                    fw.fence()
                wgt = [self.sb(p1, f"fwg{i}", [128, 16, 256], BF16) for i in range(2)]
                wup = [self.sb(p1, f"fwu{i}", [128, 16, 256], BF16) for i in range(2)]
                sg = [self.sb(p1, f"fsg{i}", [128, 512], F32) for i in range(2)]
                wv = I["w_ffn_in"].rearrange("(kc p) n -> p kc n", p=128)
                for fg in range(22):
                    k = fg % 2
                    self.dma("pool", wgt[k][:], wv[:, :, fg * 256:(fg + 1) * 256], [], [B("fwg", k)])
                    self.dma("pool", wup[k][:], wv[:, :, DFF + fg * 256:DFF + (fg + 1) * 256], [], [B("fwu", k)])
                    for dc in range(2):
                        for half in range(2):
                            tsl = slice(half * 512, (half + 1) * 512)
                            ub = [B("u2T", half)]
                            pg, pgb = self.psum((dc * 2 + half) % 2)
                            for kc in range(16):
                                self.mm(pg[:, :], wgt[k][:, kc, dc * 128:(dc + 1) * 128], u2[:, kc, tsl], kc == 0, kc == 15,
                                        [B("fwg", k)] + ub, [pgb])
                            pu, pub = self.psum(2 + (dc * 2 + half) % 2)
                            for kc in range(16):
                                self.mm(pu[:, :], wup[k][:, kc, dc * 128:(dc + 1) * 128], u2[:, kc, tsl], kc == 0, kc == 15,
                                        [B("fwu", k)] + ub, [pub])
                            sk = (dc * 2 + half) % 2
                            self.act(sg[sk][:], pg[:, :], AF.Silu, [pgb], [B("fsg", sk)])
                            self.tt("dve", hT[:, fg * 2 + dc, tsl], pu[:, :], sg[sk][:], ALU.mult, [pub, B("fsg", sk)],
                                    [B("hT", fg * 2 + dc, half)])
                fw.fence()
            hb = lambda tt: [B("hT", c, tt // 4) for c in range(44)]
            with ExitStack() as p3:
                self.dense_out_ln(p3, hT, hb, 44, I["w_ffn_out"], 5, lambda tt: S["x1"][tt * 128:(tt + 1) * 128, :], 1,
                                  lambda tt: xout[tt * 128:(tt + 1) * 128, :], B("xout"), "fo", halves=2, exchange=exchange)
            fw.fence()


_PROG_CACHE = {}


def get_prog(debug=False, phases=None, layers=(0, 1)):
    key = (debug, tuple(phases) if phases else None, tuple(layers))
    if key not in _PROG_CACHE:
        pr = Prog(debug=debug, phases=phases, layers=layers)
        _PROG_CACHE[key] = (pr, pr.build())
    return _PROG_CACHE[key]


def core_tokens(j):
    a = np.arange(512 * j, 512 * j + 512)
    b = np.arange(512 * (7 - j), 512 * (7 - j) + 512)
    return np.concatenate([a, b])


def _grow(T):
    return T * 1024 if T <= 3 else (7 - T) * 1024 + 512


def make_maps(inputs, layers=(0, 1)):
    f = np.float32
    x = np.asarray(inputs["x"], dtype=f)
    inv16 = (ROPE_THETA ** (-(np.arange(16, dtype=np.float32) * 2.0) / 32)).astype(f)
    inv8 = (ROPE_THETA ** (-(np.arange(8, dtype=np.float32) * 2.0) / 16)).astype(f)
    shared = {"inv16": inv16, "inv8": inv8}
    for l in layers:
        sfx = str(l)
        lam_init = 0.8 - 0.6 * math.exp(-0.3 * l)
        shared.update({
            "laminit" + sfx: np.array([lam_init], f),
            "w_in" + sfx: inputs["w_in"][l],
            "a_lat_g" + sfx: np.ascontiguousarray(inputs["a_lat_g"][l].reshape(4, 128).T),
            "cmp_w1" + sfx: inputs["cmp_w1"][l], "cmp_w2" + sfx: inputs["cmp_w2"][l],
            "cmp_peT" + sfx: np.ascontiguousarray(inputs["cmp_pe"][l].transpose(0, 2, 1)),
            "lam" + sfx: inputs["lam"][l], "c_subln_g" + sfx: inputs["c_subln_g"][l], "w_br" + sfx: inputs["w_br"][l],
            "w_o" + sfx: inputs["w_o"][l], "w_ffn_in" + sfx: inputs["w_ffn_in"][l], "w_ffn_out" + sfx: inputs["w_ffn_out"][l],
            "ln_g" + sfx: inputs["ln_g"][l], "ln_b" + sfx: inputs["ln_b"][l],
        })
    maps = []
    for i in range(8):
        b, j = i // 4, i % 4
        tok = core_tokens(j)
        qpos = tok.astype(f)
        m = dict(shared)
        for l in layers:
            m["w_ada" + str(l)] = np.ascontiguousarray(inputs["w_ada"][l][:, j * 3072:(j + 1) * 3072])
            m["b_ada" + str(l)] = np.ascontiguousarray(inputs["b_ada"][l][j * 3072:(j + 1) * 3072])
            wi, au = inputs["w_in"][l], inputs["a_up"][l]
            br_, g_ = j // 2, j % 2
            ck_ = O_BKV + ((br_ * 2 + 0) * 2 + g_) * 128
            cv_ = O_BKV + ((br_ * 2 + 1) * 2 + g_) * 128
            m["w_kv" + str(l)] = np.ascontiguousarray(np.concatenate([
                wi[:, ck_:ck_ + 128], wi[:, cv_:cv_ + 128], wi[:, O_BKV + 1024:O_BKV + 1536],
                wi[:, O_CK + j * 256:O_CK + (j + 1) * 256], wi[:, O_CV + j * 256:O_CV + (j + 1) * 256]], axis=1))
            m["a_up" + str(l)] = np.ascontiguousarray(np.concatenate([au[:, j * 256:(j + 1) * 256], au[:, 1024 + j * 256:1024 + (j + 1) * 256]], axis=1))
        m.update({
            "xf": x[b], "xo": np.ascontiguousarray(x[b][tok]),
            "qpos": qpos, "qposc": np.ascontiguousarray(qpos.reshape(8, 128).T),
            "ct": np.ascontiguousarray(np.asarray(inputs["c"], dtype=f)[b].reshape(16, 128).T),
        })
        maps.append(m)
    return maps


def kernel(**inputs):
    inputs = {k: np.asarray(v) for k, v in inputs.items()}
    pr, nc = get_prog()
    maps = make_maps(inputs)
    res = run_bass_kernel_spmd(nc, maps, core_ids=list(range(8)))
    out = np.empty((2, L, D), np.float32)
    for i in range(8):
        b, j = i // 4, i % 4
        out[b][core_tokens(j)] = res.results[i]["xout"]
    return out
```

```python
import math
from collections import defaultdict
from contextlib import ExitStack

import numpy as np
import concourse.bass as bass
import concourse.mybir as mybir
from concourse.bass_utils import run_bass_kernel_spmd

F32 = mybir.dt.float32
BF16 = mybir.dt.bfloat16
I32 = mybir.dt.int32
ALU = mybir.AluOpType
AF = mybir.ActivationFunctionType

D = 2048
L = 4096
NOWN = 1024
DFF = 5632
NIN = 14440
ROPE_THETA = 500000.0
ALPHA = 4 ** 0.25
NEG = -1e30

O_AQ, O_ALAT, O_IQ, O_IK, O_IW, O_BQ, O_BKV, O_BG, O_CQ, O_CK, O_CV, O_GL = (
    0, 1024, 1536, 2560, 2624, 2640, 3664, 5200, 5224, 6248, 7272, 8296)

EPOCH = 30000
DEBUG_TB = False
NDMASEM = 24


class Buf:
    __slots__ = ("w", "r")

    def __init__(self):
        self.w = None
        self.r = {}


class FW:
    def __init__(self, nc, stack):
        self.nc = nc
        self.stack = stack
        self.engs = {"pe": nc.tensor, "act": nc.scalar, "dve": nc.vector, "pool": nc.gpsimd, "sp": nc.sync}
        self.ops = {k: [] for k in self.engs}
        self.cnt = {k: 0 for k in self.engs}
        self.sems = {}
        self.waited = {k: {} for k in self.engs}
        self.dma_i = 0
        self.dma_last = [None] * NDMASEM
        self.dma_val = [0] * NDMASEM
        for i in range(NDMASEM):
            self.sems[("dma", i)] = stack.enter_context(nc.semaphore(f"dsem{i}"))
        self.n_inst = 0
        self.bufs = defaultdict(Buf)

    def B(self, *key):
        return self.bufs[key]

    def _engsem(self, eng, epoch):
        key = (eng, epoch)
        if key not in self.sems:
            self.sems[key] = self.stack.enter_context(self.nc.semaphore(f"s_{eng}_{epoch}"))
        return key

    def _emit_waits(self, eng, toks):
        need = {}
        for (k, v) in toks:
            if need.get(k, 0) < v:
                need[k] = v
        for k, v in need.items():
            if self.waited[eng].get(k, 0) >= v:
                continue
            self.waited[eng][k] = v
            h = self.sems[k]
            self.ops[eng].append(lambda e, h=h, v=v: e.wait_ge(h, v))
            self.n_inst += 1

    def _deps(self, eng, reads, writes):
        toks = []
        for b in reads:
            if b.w is not None:
                toks.append(b.w)
        for b in writes:
            if b.w is not None:
                toks.append(b.w)
            for k, v in b.r.items():
                toks.append((k, v))
        if eng == "pe":
            toks = [t for t in toks if t[0][0] != "pe"]
        return toks

    def _mark(self, tok, reads, writes):
        key, val = tok
        for b in reads:
            if b.r.get(key, 0) < val:
                b.r[key] = val
        for b in writes:
            b.w = tok
            b.r = {}

    def _last_tok(self, eng):
        c = self.cnt[eng]
        if not c:
            return None
        return ((eng, (c - 1) // EPOCH), (c - 1) % EPOCH + 1)

    def op(self, eng, fn, reads=(), writes=()):
        self._emit_waits(eng, self._deps(eng, reads, writes))
        self.cnt[eng] += 1
        c = self.cnt[eng]
        epoch, val = (c - 1) // EPOCH, (c - 1) % EPOCH + 1
        key = self._engsem(eng, epoch)
        if val == 1 and epoch > 0:
            pass
        h = self.sems[key]
        if DEBUG_TB:
            import traceback
            tb = traceback.extract_stack(limit=6)

            def run(e, fn=fn, h=h, tb=tb):
                try:
                    return fn(e).then_inc(h, 1)
                except Exception:
                    print("".join(traceback.format_list(tb)))
                    raise
            self.ops[eng].append(run)
        else:
            self.ops[eng].append(lambda e, fn=fn, h=h: fn(e).then_inc(h, 1))
        self.n_inst += 1
        tok = (key, val)
        self._mark(tok, reads, writes)
        return tok

    def dma(self, eng, out, in_, reads=(), writes=(), **kw):
        i = self.dma_i % NDMASEM
        self.dma_i += 1
        toks = self._deps(eng, reads, writes)
        if self.dma_last[i] is not None:
            toks.append(self.dma_last[i])
        self._emit_waits(eng, toks)
        self.dma_val[i] += 16
        key = ("dma", i)
        val = self.dma_val[i]
        h = self.sems[key]
        self.ops[eng].append(lambda e, h=h: e.dma_start(out=out, in_=in_, **kw).then_inc(h, 16))
        self.n_inst += 1
        tok = (key, val)
        self.dma_last[i] = tok
        self._mark(tok, reads, writes)
        return tok

    def coll(self, eng, fn, reads=(), writes=()):
        key = ("cc", 0)
        if key not in self.sems:
            self.sems[key] = self.stack.enter_context(self.nc.semaphore("ccsem"))
            self.cc_val = 0
        self._emit_waits(eng, self._deps(eng, reads, writes))
        self.cc_val += 1
        val = self.cc_val
        h = self.sems[key]
        self.ops[eng].append(lambda e, h=h: fn(e).then_inc(h, 1))
        self.n_inst += 1
        tok = (key, val)
        self._mark(tok, reads, writes)
        self.cc_last = tok
        return tok

    def dma_like(self, eng, fn, reads=(), writes=()):
        i = self.dma_i % NDMASEM
        self.dma_i += 1
        toks = self._deps(eng, reads, writes)
        if self.dma_last[i] is not None:
            toks.append(self.dma_last[i])
        self._emit_waits(eng, toks)
        self.dma_val[i] += 16
        key = ("dma", i)
        val = self.dma_val[i]
        h = self.sems[key]
        self.ops[eng].append(lambda e, h=h: fn(e).then_inc(h, 16))
        self.n_inst += 1
        tok = (key, val)
        self.dma_last[i] = tok
        self._mark(tok, reads, writes)
        return tok

    def fence(self, include_cc=False):
        toks = []
        for eng in self.engs:
            t = self._last_tok(eng)
            if t:
                toks.append(t)
        for i in range(NDMASEM):
            if self.dma_last[i] is not None:
                toks.append(self.dma_last[i])
        if include_cc and getattr(self, "cc_last", None) is not None:
            toks.append(self.cc_last)
        for eng in self.engs:
            self._emit_waits(eng, [t for t in toks if not (t[0][0] == eng)])

    def finish(self):
        toks = []
        for eng in self.engs:
            t = self._last_tok(eng)
            if t:
                toks.append(t)
        for i in range(NDMASEM):
            if self.dma_last[i] is not None:
                toks.append(self.dma_last[i])
        if getattr(self, "cc_last", None) is not None:
            toks.append(self.cc_last)
        self._emit_waits("sp", toks)

    def replay(self):
        with self.nc.Block() as block:
            @block.sync
            def _(e):
                for f in self.ops["sp"]:
                    f(e)

            @block.tensor
            def _(e):
                for f in self.ops["pe"]:
                    f(e)

            @block.scalar
            def _(e):
                for f in self.ops["act"]:
                    f(e)

            @block.vector
            def _(e):
                for f in self.ops["dve"]:
                    f(e)

            @block.gpsimd
            def _(e):
                for f in self.ops["pool"]:
                    f(e)


class Prog:
    def __init__(self, debug=False, phases=None, layers=(0, 1)):
        self.debug = debug
        self.layers = tuple(layers)
        self.phases = phases
        self.nc = bass.Bass("TRN2", target_bir_lowering=False)
        self.st = ExitStack()
        self.fw = FW(self.nc, self.st)
        self.uid = 0

    def din(self, name, shape, dt=F32):
        return self.nc.dram_tensor(name, list(shape), dt, kind="ExternalInput").ap()

    def dout(self, name, shape, dt=F32):
        return self.nc.dram_tensor(name, list(shape), dt, kind="ExternalOutput").ap()

    def dscr(self, name, shape, dt=BF16):
        kind = "ExternalOutput" if self.debug else "Internal"
        return self.nc.dram_tensor(name, list(shape), dt, kind=kind).ap()

    def sb(self, stack, name, shape, dt):
        self.uid += 1
        return stack.enter_context(self.nc.sbuf_tensor(f"{name}_{self.uid}", list(shape), dt))

    def mm(self, out, lhsT, rhs, start, stop, r, w):
        self.fw.op("pe", lambda e: e.matmul(out, lhsT, rhs, start=start, stop=stop), r, w)

    def tr(self, out, in_, r, w):
        ident = self.ident[0:in_.shape[0], 0:in_.shape[0]]
        self.fw.op("pe", lambda e: e.transpose(out, in_, ident), list(r) + [self.fw.B("ident")], w)

    def act(self, out, in_, func, r, w, bias=0.0, scale=1.0, accum=None):
        if accum is None:
            self.fw.op("act", lambda e: e.activation(out, in_, func, bias=bias, scale=scale), r, w)
        else:
            self.fw.op("act", lambda e: e.activation(out, in_, func, bias=bias, scale=scale, accum_out=accum), r, w)

    def ts(self, eng, out, in0, s1, s2, op0, op1, r, w, accum=None):
        if accum is None:
            if op1 is None:
                self.fw.op(eng, lambda e: e.tensor_scalar(out, in0, s1, None, op0), r, w)
            else:
                self.fw.op(eng, lambda e: e.tensor_scalar(out, in0, s1, s2, op0, op1), r, w)
        else:
            self.fw.op(eng, lambda e: e.tensor_scalar(out, in0, s1, s2, op0, op1, accum_out=accum), r, w)

    def tt(self, eng, out, in0, in1, op, r, w):
        self.fw.op(eng, lambda e: e.tensor_tensor(out, in0, in1, op), r, w)

    def stt(self, out, in0, scalar, in1, op0, op1, r, w, accum=None):
        if accum is None:
            self.fw.op("dve", lambda e: e.scalar_tensor_tensor(out, in0, scalar, in1, op0, op1), r, w)
        else:
            self.fw.op("dve", lambda e: e.scalar_tensor_tensor(out, in0, scalar, in1, op0, op1, accum_out=accum), r, w)

    def cp(self, eng, out, in_, r, w):
        if eng == "act":
            self.fw.op("act", lambda e: e.copy(out, in_), r, w)
        else:
            self.fw.op(eng, lambda e: e.tensor_copy(out, in_), r, w)

    def ms(self, eng, ap, val, w):
        self.fw.op(eng, lambda e: e.memset(ap, val), (), w)

    def recip(self, out, in_, r, w):
        self.fw.op("dve", lambda e: e.reciprocal(out, in_), r, w)

    UNTRACKED = (("scr",), ("s_y",), ("s_uT",), ("s_x1",), ("xout",))

    def dma(self, q, out, in_, r, w, **kw):
        un = [self.fw.bufs[k] for k in self.UNTRACKED]
        r = [b for b in r if not any(b is u for u in un)]
        w = [b for b in w if not any(b is u for u in un)]
        self.fw.dma(q, out, in_, r, w, **kw)

    def psum(self, i):
        return self.ps[i], self.fw.B("ps", i)

    def build(self):
        nc, fw, st = self.nc, self.fw, self.st
        B = fw.B
        C = {}
        C["xf"] = self.din("xf", [L, D])
        C["xo"] = self.din("xo", [NOWN, D])
        C["qpos"] = self.din("qpos", [NOWN])
        C["qposc"] = self.din("qposc", [128, 8])
        C["ct"] = self.din("ct", [128, 16])
        C["inv16"] = self.din("inv16", [16])
        C["inv8"] = self.din("inv8", [8])
        self.Il = {}
        for l in self.layers:
            W = dict(C)
            sfx = str(l)
            W["laminit"] = self.din("laminit" + sfx, [1])
            W["w_ada"] = self.din("w_ada" + sfx, [D, 3072])
            W["b_ada"] = self.din("b_ada" + sfx, [3072])
            W["w_in"] = self.din("w_in" + sfx, [D, NIN])
            W["a_lat_g"] = self.din("a_lat_g" + sfx, [128, 4])
            W["a_up"] = self.din("a_up" + sfx, [512, 512])
            W["w_kv"] = self.din("w_kv" + sfx, [D, 1280])
            W["cmp_w1"] = self.din("cmp_w1" + sfx, [2, 4096, 128])
            W["cmp_w2"] = self.din("cmp_w2" + sfx, [2, 128, 128])
            W["cmp_peT"] = self.din("cmp_peT" + sfx, [2, 128, 32])
            W["lam"] = self.din("lam" + sfx, [4, 128])
            W["c_subln_g"] = self.din("c_subln_g" + sfx, [256])
            W["w_br"] = self.din("w_br" + sfx, [3, 1024, D])
            W["w_o"] = self.din("w_o" + sfx, [D, D])
            W["w_ffn_in"] = self.din("w_ffn_in" + sfx, [D, 2 * DFF])
            W["w_ffn_out"] = self.din("w_ffn_out" + sfx, [DFF, D])
            W["ln_g"] = self.din("ln_g" + sfx, [2, D])
            W["ln_b"] = self.din("ln_b" + sfx, [2, D])
            self.Il[l] = W
        I = self.Il[self.layers[0]]
        self.I = I
        xout = self.dout("xout", [NOWN, D])

        S = {}
        S["modq"] = [self.nc.dram_tensor(f"s_modq{l}", [1, 3072], F32, kind="Internal").ap() for l in range(2)]
        S["modall"] = [self.nc.dram_tensor(f"s_modall{l}", [4, 3072], F32, kind="Internal").ap() for l in range(2)]
        S["lamd"] = self.dscr("s_lamd", [2], F32)
        idr = lambda n, shp: self.nc.dram_tensor(n, list(shp), BF16, kind="Internal").ap()
        S["L_akT"] = [idr(f"l_akT{i}", [128, L]) for i in range(2)]
        S["G_akT"] = [idr(f"g_akT{i}", [512, L]) for i in range(2)]
        S["L_av"] = [idr(f"l_av{i}", [2048, 256]) for i in range(2)]
        S["G_av"] = [idr(f"g_av{i}", [4 * 2048, 256]) for i in range(2)]
        S["L_bK"] = idr("l_bK", [128, L])
        S["G_bK"] = idr("g_bK", [512, L])
        S["L_bV"] = idr("l_bV", [L, 128])
        S["G_bV"] = idr("g_bV", [4 * L, 128])
        S["L_ckT"] = [idr(f"l_ckT{i}", [128, L]) for i in range(2)]
        S["G_ckT"] = [idr(f"g_ckT{i}", [512, L]) for i in range(2)]
        S["L_cv"] = [idr(f"l_cv{i}", [2048, 256]) for i in range(2)]
        S["G_cv"] = [idr(f"g_cv{i}", [4 * 2048, 256]) for i in range(2)]
        S["ikT"] = self.dscr("s_ikT", [128, L])
        S["aqT"] = self.dscr("s_aqT", [8, 128, NOWN])
        S["iqT"] = self.dscr("s_iqT", [8, 128, NOWN])
        S["iw"] = self.dscr("s_iw", [NOWN, 16], F32)
        S["bqT"] = self.dscr("s_bqT", [8, 128, NOWN])
        S["bg"] = self.dscr("s_bg", [NOWN, 24], F32)
        S["cqT"] = self.dscr("s_cqT", [8, 128, NOWN])
        S["kwT"] = self.dscr("s_kwT", [2, 128, L])
        S["vw"] = self.dscr("s_vw", [L, 256])
        S["uT"] = self.dscr("s_uT", [16, 128, NOWN])
        S["y"] = self.dscr("s_y", [NOWN, 3072])
        S["x1"] = self.dscr("s_x1", [NOWN, D], F32)
        S["xown1"] = self.nc.dram_tensor("s_xown1", [NOWN, D], F32, kind="Internal").ap()
        S["G"] = self.nc.dram_tensor("s_G", [8, 512, D], F32, kind="Internal").ap()
        self.S = S

        self.ps = [st.enter_context(nc.psum_tensor(f"ps{i}", [128, 512], F32)) for i in range(8)]
        g = st
        self.ident = self.sb(g, "ident", [128, 128], BF16)
        self.ones_bf = self.sb(g, "ones_bf", [128, 128], BF16)
        self.modT = self.sb(g, "modT", [128, 4, 16], F32)
        self.POSq = self.sb(g, "POSq", [128, 16], F32)
        self.qposB = self.sb(g, "qposB", [128, NOWN], F32)
        self.cmpR = self.sb(g, "cmpR", [128, 2, 2, 193], BF16)
        self.kcT = self.sb(g, "kcT", [128, 2, 256], BF16)
        self.lamv = self.sb(g, "lamv", [128, 4], F32)
        self.epsc = self.sb(g, "epsc", [128, 2], F32)
        self.kpos512 = self.sb(g, "kpos512", [128, 512], F32)
        kpi = self.sb(g, "kpos512i", [128, 512], I32)
        fw.op("pool", lambda e: e.iota(kpi[:], pattern=[[1, 512]], base=0, channel_multiplier=0), [], [B("kposi")])
        self.cp("dve", self.kpos512[:], kpi[:], [B("kposi")], [B("kpos")])
        self.ms("dve", self.epsc[:, 0:1], 1e-5, [B("epsc")])
        self.ms("dve", self.epsc[:, 1:2], 1e-6, [B("epsc")])

        ident, ones_bf = self.ident, self.ones_bf
        self.ms("pool", ident[:], 1.0, [B("ident")])
        fw.op("pool", lambda e: e.affine_select(ident[:], ident[:], pattern=[[-1, 128]], compare_op=ALU.is_equal,
                                                fill=0.0, base=0, channel_multiplier=1),
              [B("ident")], [B("ident")])
        self.ms("pool", ones_bf[:], 1.0, [B("ones")])
        self.dma("sp", self.POSq[:, 0:8], I["qposc"][:, :], [], [B("POSq")])
        self.dma("sp", self.qposB[:], I["qpos"].partition_broadcast(128), [], [B("qposB")])

        ph = self.phases
        if ph is None or "mod" in ph:
            for l in self.layers:
                self.layer = l
                self.I = self.Il[l]
                self.phase_mod()
        for li, l in enumerate(self.layers):
            self.layer = l
            self.I = self.Il[l]
            last = (li == len(self.layers) - 1)
            if ph is None or "mod" in ph:
                self.phase_mod_load()
            if ph is None or "proj" in ph:
                self.phase_proj()
            if ph is None or "attA" in ph:
                self.phase_att_a()
            if ph is None or "cmp" in ph:
                self.phase_compress()
            if ph is None or "attB" in ph:
                self.phase_att_b()
            if ph is None or "attC" in ph:
                self.phase_att_c()
            if ph is None or "merge" in ph:
                self.phase_merge()
            if ph is None or "ffn" in ph:
                self.phase_ffn(xout if last else S["xown1"], exchange=not last)
            if not last:
                fw.fence(include_cc=True)
        fw.finish()
        fw.replay()
        st.close()
        return nc

    def phase_mod(self):
        fw, I, S, B = self.fw, self.I, self.S, self.fw.B
        with ExitStack() as p:
            ct = self.sb(p, "ct", [128, 16], F32)
            cs = self.sb(p, "cs", [128, 16], BF16)
            wa = [self.sb(p, f"wa{i}", [128, 16, 512], BF16) for i in range(2)]
            brow = self.sb(p, "brow", [1, 512], F32)
            mrow = self.sb(p, "mrow", [1, 512], F32)
            self.dma("sp", ct[:], I["ct"][:, :], [], [B("ct")])
            self.act(cs[:], ct[:], AF.Silu, [B("ct")], [B("cs")])
            wv = I["w_ada"].rearrange("(kc p) n -> p kc n", p=128)
            mq = self.S["modq"][self.layer]
            for cg in range(6):
                wt = wa[cg % 2]
                self.dma("pool", wt[:], wv[:, :, cg * 512:(cg + 1) * 512], [], [B("wa", cg % 2)])
                self.dma("sp", brow[:], I["b_ada"][cg * 512:(cg + 1) * 512].unsqueeze(0), [], [B("brow")])
                pt, pb = self.psum(cg % 2)
                for kc in range(16):
                    self.mm(pt[0:1, :], cs[:, kc:kc + 1], wt[:, kc, :], kc == 0, kc == 15,
                            [B("cs"), B("wa", cg % 2)], [pb])
                self.tt("dve", mrow[:], pt[0:1, :], brow[:], ALU.add, [pb, B("brow")], [B("mrow")])
                self.dma("sp", mq[:, cg * 512:(cg + 1) * 512], mrow[:], [B("mrow")], [B("s_modq")])
            mall = self.S["modall"][self.layer]
            fw.coll("pool", lambda e: e.collective_compute(
                "AllGather", ALU.bypass, replica_groups=[[0, 1, 2, 3], [4, 5, 6, 7]], ins=[mq[:, :]], outs=[mall[:, :]]),
                [B("s_modq")], [B("s_mod")])
            fw.fence(include_cc=True)

    def phase_mod_load(self):
        fw, I, S, B = self.fw, self.I, self.S, self.fw.B
        S["mod"] = S["modall"][self.layer].rearrange("a b -> (a b)")
        with ExitStack() as p:
            mv = S["mod"].rearrange("(a kc p) -> a p kc", a=6, p=128)
            for j, a in enumerate((0, 1, 3, 4)):
                self.dma("sp", self.modT[:, j, :], mv[a], [B("s_mod")], [B("modT")], allow_slow_non_contiguous=True)
            for j in (1, 3):
                self.ts("dve", self.modT[:, j, :], self.modT[:, j, :], 1.0, None, ALU.add, None, [B("modT")], [B("modT")])
            lm = self.sb(p, "lm", [1, 4, 128], F32)
            lt = self.sb(p, "lt", [1, 2, 128], F32)
            l2 = self.sb(p, "l2", [1, 8], F32)
            self.dma("sp", lm[:], I["lam"].unsqueeze(0), [], [B("lm")])
            self.dma("sp", l2[:, 4:5], I["laminit"].unsqueeze(0), [], [B("l2")])
            self.tt("dve", lt[:, 0, :], lm[:, 0, :], lm[:, 1, :], ALU.mult, [B("lm")], [B("lt")])
            self.tt("dve", lt[:, 1, :], lm[:, 2, :], lm[:, 3, :], ALU.mult, [B("lm")], [B("lt")])
            self.fw.op("dve", lambda e: e.reduce_sum(l2[:, 0:1], lt[:, 0, :], mybir.AxisListType.X), [B("lt"), B("l2")], [B("l2")])
            self.fw.op("dve", lambda e: e.reduce_sum(l2[:, 1:2], lt[:, 1, :], mybir.AxisListType.X), [B("lt"), B("l2")], [B("l2")])
            self.act(l2[:, 2:4], l2[:, 0:2], AF.Exp, [B("l2")], [B("l2")])
            self.tt("dve", l2[:, 5:6], l2[:, 3:4], l2[:, 2:3], ALU.subtract, [B("l2")], [B("l2")])
            self.tt("dve", l2[:, 5:6], l2[:, 5:6], l2[:, 4:5], ALU.subtract, [B("l2")], [B("l2")])
            self.ts("dve", l2[:, 6:7], l2[:, 4:5], -1.0, 1.0, ALU.mult, ALU.add, [B("l2")], [B("l2")])
            self.dma("sp", S["lamd"].unsqueeze(0), l2[:, 5:7], [B("l2")], [B("s_lamd")])
            self.dma("sp", self.lamv[:, 0:2], S["lamd"].partition_broadcast(128), [B("s_lamd")], [B("lamv")])
            fw.fence()

    def rope_tables(self, p, pos, npos, tag):
        I, B = self.I, self.fw.B
        res = []
        for (half, inv_name) in ((16, "inv16"), (8, "inv8")):
            inv = self.sb(p, f"inv{half}", [128, half], F32)
            self.dma("sp", inv[:], I[inv_name].partition_broadcast(128), [], [B(tag, "inv", half)])
            ang = self.sb(p, f"ang{half}", [128, npos, half], F32)
            CC = self.sb(p, f"CC{half}", [128, npos, 2 * half], F32)
            SS = self.sb(p, f"SS{half}", [128, npos, 2 * half], F32)
            tmp = self.sb(p, f"rtmp{half}", [128, npos, half], F32)
            ki = self.sb(p, f"rki{half}", [128, npos, half], I32)
            bA, bT, bK, bC, bS = B(tag, "ang", half), B(tag, "tmp", half), B(tag, "ki", half), B(tag, "CC", half), B(tag, "SS", half)
            shp = [128, npos, half]
            self.tt("dve", ang[:], pos.unsqueeze(2).to_broadcast(shp), inv[:].unsqueeze(1).to_broadcast(shp), ALU.mult,
                    [B(tag, "inv", half), B(tag, "pos")], [bA])
            for which, shift in (("sin", 0.0), ("cos", math.pi / 2)):
                self.ts("dve", tmp[:], ang[:], shift, 1.0 / (2 * math.pi), ALU.add, ALU.mult, [bA], [bT])
                self.cp("dve", ki[:], tmp[:], [bT], [bK])
                self.cp("dve", tmp[:], ki[:], [bK], [bT])
                self.stt(tmp[:], tmp[:], -2 * math.pi, ang[:], ALU.mult, ALU.add, [bT, bA], [bT])
                if shift != 0.0:
                    self.ts("dve", tmp[:], tmp[:], shift, None, ALU.add, None, [bT], [bT])
                dst = SS if which == "sin" else CC
                bD = bS if which == "sin" else bC
                w1 = dst[:, :, 0:half]
                self.ts("dve", w1, tmp[:], math.pi, -2 * math.pi, ALU.is_gt, ALU.mult, [bT], [bD])
                self.tt("dve", tmp[:], tmp[:], w1, ALU.add, [bT, bD], [bT])
                self.ts("dve", w1, tmp[:], -math.pi, 2 * math.pi, ALU.is_lt, ALU.mult, [bT], [bD])
                self.tt("dve", tmp[:], tmp[:], w1, ALU.add, [bT, bD], [bT])
                self.act(dst[:, :, half:2 * half], tmp[:], AF.Sin, [bT], [bD])
                if which == "sin":
                    self.ts("dve", dst[:, :, 0:half], dst[:, :, half:2 * half], -1.0, None, ALU.mult, None, [bD], [bD])
                else:
                    self.cp("dve", dst[:, :, 0:half], dst[:, :, half:2 * half], [bD], [bD])
            res += [CC, SS]
        return res

    def rope_apply(self, v, nh, hd, half, CCi, SSi, tmp, rb, tag):
        B = self.fw.B
        vv = v.rearrange("p (h d) -> p h d", d=hd)
        shp = [128, nh, half]
        x1, x2 = vv[:, :, 0:half], vv[:, :, half:2 * half]
        sneg = SSi[:, 0:half].unsqueeze(1).to_broadcast(shp)
        spos = SSi[:, half:2 * half].unsqueeze(1).to_broadcast(shp)
        cc = CCi.unsqueeze(1).to_broadcast([128, nh, 2 * half])
        tb = B("ropetmp", tag)
        self.tt("dve", tmp[:, 0:nh, 0:half], x2, sneg, ALU.mult, rb, [tb])
        self.tt("dve", tmp[:, 0:nh, half:2 * half], x1, spos, ALU.mult, rb, [tb])
        self.tt("dve", vv[:, :, 0:2 * half], vv[:, :, 0:2 * half], cc, ALU.mult, rb, rb)
        self.tt("dve", vv[:, :, 0:2 * half], vv[:, :, 0:2 * half], tmp[:, 0:nh, 0:2 * half], ALU.add, list(rb) + [tb], rb)

    def ln_to_uT(self, p, src_tile_fn, ntiles, uT, ubuf, modj, tag):
        B = self.fw.B
        xt = [self.sb(p, f"lnx{i}", [128, D], F32) for i in range(2)]
        xn = self.sb(p, "lnxn", [128, 4, D], BF16)
        stt_ = self.sb(p, "lnst", [128, 2, 4, 6], F32)
        mv = self.sb(p, "lnmv", [128, 2, 4], F32)
        for i in range(ntiles):
            k = i % 2
            src = src_tile_fn(i)
            self.dma("sp", xt[k][:], src, [B("G")], [B(tag, "x", k)])
            for c in range(4):
                self.fw.op("dve", lambda e, c=c, k=k: e.bn_stats(stt_[:, k, c, :], xt[k][:, c * 512:(c + 1) * 512]),
                           [B(tag, "x", k)], [B(tag, "st", k)])
            self.fw.op("dve", lambda e, k=k: e.bn_aggr(mv[:, k, 0:2], stt_[:, k, :, :].rearrange("p a b -> p (a b)")),
                       [B(tag, "st", k)], [B(tag, "mv", k)])
            self.act(mv[:, k, 2:3], mv[:, k, 1:2], AF.Sqrt, [B(tag, "mv", k)], [B(tag, "mv", k)], bias=self.epsc[:, 0:1])
            self.recip(mv[:, k, 3:4], mv[:, k, 2:3], [B(tag, "mv", k)], [B(tag, "mv", k)])
            self.ts("dve", xn[:, i % 4, :], xt[k][:], mv[:, k, 0:1], mv[:, k, 3:4], ALU.subtract, ALU.mult,
                    [B(tag, "x", k), B(tag, "mv", k)], [B(tag, "xn", i % 4)])
            if i % 4 == 3:
                g0 = (i // 4) * 512
                for kc in range(16):
                    pt, pb = self.psum(kc % 4)
                    ptb = pt[:].bitcast(BF16)
                    for j in range(4):
                        self.tr(ptb[:, j * 128:(j + 1) * 128], xn[:, j, kc * 128:(kc + 1) * 128], [B(tag, "xn", j)], [pb])
                    self.act(uT[:, kc, g0:g0 + 512], ptb[:, 0:512], AF.Identity, [pb, B("modT")], [ubuf(kc, i // 4)],
                             bias=self.modT[:, modj, kc:kc + 1], scale=self.modT[:, modj + 1, kc:kc + 1])

    def phase_proj(self):
        fw, I, S, B = self.fw, self.I, self.S, self.fw.B
        nc = self.nc
        with ExitStack() as p0:
            aup = self.sb(p0, "aup", [128, 4, 512], BF16)
            alg = self.sb(p0, "alg", [128, 4], F32)
            self.dma("pool", aup[:], I["a_up"].rearrange("(kc p) n -> p kc n", p=128), [], [B("aup")])
            self.dma("sp", alg[:], I["a_lat_g"][:, :], [], [B("alg")])
            for kc in range(4):
                self.ts("dve", aup[:, kc, :], aup[:, kc, :], alg[:, kc:kc + 1], None, ALU.mult, None,
                        [B("aup"), B("alg")], [B("aup")])
            for pas in range(3):
                with ExitStack() as p:
                    self.proj_pass(p, pas, aup)
                fw.fence()
            pairs = []
            for i in range(2):
                pairs += [(S["L_akT"][i], S["G_akT"][i]), (S["L_av"][i], S["G_av"][i]),
                          (S["L_ckT"][i], S["G_ckT"][i]), (S["L_cv"][i], S["G_cv"][i])]
            pairs += [(S["L_bK"], S["G_bK"]), (S["L_bV"], S["G_bV"])]
            for (src_, dst_) in pairs:
                fw.coll("pool", lambda e, src_=src_, dst_=dst_: e.collective_compute(
                    "AllGather", ALU.bypass, replica_groups=[[0, 1, 2, 3], [4, 5, 6, 7]],
                    ins=[src_[:, :]], outs=[dst_[:, :]]), [], [B("Gkv")])
        fw.fence()

    def proj_pass(self, p, pas, aup):
        fw, I, S, B = self.fw, self.I, self.S, self.fw.B
        NT = 16 if pas < 2 else 8
        uT = self.sb(p, "uT", [128, 16, 2048], BF16)
        ubuf = lambda kc, grp: B("uT", grp)
        pos = self.sb(p, "pos", [128, NT], F32)
        if pas < 2:
            posi = self.sb(p, "posi", [128, NT], I32)
            fw.op("pool", lambda e: e.iota(posi[:], pattern=[[128, NT]], base=2048 * pas, channel_multiplier=1),
                  [], [B("posi")])
            self.cp("dve", pos[:], posi[:], [B("posi")], [B("rt", "pos")])
            if self.layer == self.layers[0]:
                src = lambda i: I["xf"][2048 * pas + 128 * i: 2048 * pas + 128 * (i + 1), :]
            else:
                def src(i):
                    t0_ = 2048 * pas + 128 * i
                    T = t0_ // 512
                    r_ = T if T <= 3 else 7 - T
                    o0 = (t0_ % 512) + (0 if T <= 3 else 512)
                    return S["G"][o0 // 128, r_ * 128:(r_ + 1) * 128, :]
        else:
            self.cp("dve", pos[:], self.POSq[:, 0:8], [B("POSq")], [B("rt", "pos")])
            if self.layer == self.layers[0]:
                src = lambda i: I["xo"][128 * i:128 * (i + 1), :]
            else:
                src = lambda i: S["xown1"][128 * i:128 * (i + 1), :]
        CC128, SS128, CC64, SS64 = self.rope_tables(p, pos[:], NT, "rt")
        with ExitStack() as pl:
            self.ln_to_uT(pl, src, NT, uT, ubuf, 0, "ln")
            fw.fence()
        if pas == 2:
            for kc in range(16):
                self.dma("sp", S["uT"][kc, :, :], uT[:, kc, 0:1024], [B("uT", 0), B("uT", 1)], [B("s_uT")])

        wts = [self.sb(p, f"wp{i}", [128, 16, 512], BF16) for i in range(2)]
        ev = [self.sb(p, f"ev{i}", [128, 512], BF16) for i in range(3)]
        rtmp = self.sb(p, "rtmp", [128, 8, 32], BF16)
        stg = [self.sb(p, f"stg{i}", [128, 4, 512], BF16) for i in range(2)]
        wv = I["w_in"].rearrange("(kc p) n -> p kc n", p=128)
        self.wslot = 0
        self.evslot = 0
        self.stgslot = 0

        def load_w(c0, n):
            k = self.wslot % 2
            self.wslot += 1
            self.dma("pool", wts[k][:, :, 0:n], wv[:, :, c0:c0 + n], [], [B("wp", k)])
            return wts[k], B("wp", k)

        def project(tt, wt, wb, n, lhs=None, nk=16):
            pt, pb = self.psum(4 + tt % 4)
            for kc in range(nk):
                if lhs is None:
                    l_ap, l_b = uT[:, kc, tt * 128:(tt + 1) * 128], B("uT", tt // 4)
                else:
                    l_ap, l_b = lhs(kc, tt)
                self.mm(pt[:, 0:n], l_ap, wt[:, kc, 0:n], kc == 0, kc == nk - 1, [l_b, wb], [pb])
            return pt, pb

        def evac(pt, pb, n):
            k = self.evslot % 3
            self.evslot += 1
            self.act(ev[k][:, 0:n], pt[:, 0:n], AF.Copy, [pb], [B("ev", k)])
            return ev[k], B("ev", k)

        def rope_T_store(tt, e, eb, n, rope, hd, dst_fn, t0_fn, tiles_per_store=4):
            nch = n // 128
            if rope:
                if hd == 128:
                    self.rope_apply(e[:, 0:n], nch, 128, 16, CC128[:, tt, :], SS128[:, tt, :], rtmp, [eb], "r")
                else:
                    self.rope_apply(e[:, 0:n], n // 64, 64, 8, CC64[:, tt, :], SS64[:, tt, :], rtmp, [eb], "r")
            pt, pb = self.psum(tt % 4)
            ptb = pt[:].bitcast(BF16)
            for c in range(nch):
                self.tr(ptb[:, c * 128:(c + 1) * 128], e[:, c * 128:(c + 1) * 128], [eb], [pb])
            sk = self.stgslot % 2
            sg = stg[sk]
            j = tt % 4
            self.cp("dve", sg[:, 0:nch, j * 128:(j + 1) * 128], ptb[:, 0:nch * 128].rearrange("p (c t) -> p c t", t=128),
                    [pb], [B("stg", sk)])
            if j == 3:
                self.stgslot += 1
                dst = dst_fn(tt // 4)
                if isinstance(dst, list):
                    for c_, d_ in enumerate(dst):
                        self.dma("sp", d_, sg[:, c_, :], [B("stg", sk)], [B("scr")])
                else:
                    self.dma("sp", dst, sg[:, 0:nch, :], [B("stg", sk)], [B("scr")])

        def job_T(c0, n, tiles, rope, hd, dst_fn):
            wt, wb = load_w(c0, n)
            prev = None
            for tt in tiles:
                pt, pb = project(tt, wt, wb, n)
                e, eb = evac(pt, pb, n)
                if prev is not None:
                    rope_T_store(*prev)
                prev = (tt, e, eb, n, rope, hd, dst_fn, None)
            rope_T_store(*prev)

        def job_V(c0, n, tiles, dst_fn):
            wt, wb = load_w(c0, n)
            for tt in tiles:
                pt, pb = project(tt, wt, wb, n)
                e, eb = evac(pt, pb, n)
                self.dma("sp", dst_fn(tt), e[:, 0:n], [eb], [B("scr")])

        if pas < 2:
            t0 = 2048 * pas
            tiles = range(16)
            latT = self.sb(p, "latT", [128, 4, 2048], BF16)
            ss = self.sb(p, "ss", [128, 4], F32)
            junk = self.sb(p, "junk", [128, 512], BF16)
            wt, wb = load_w(O_ALAT, 512)
            for tt in tiles:
                pt, pb = project(tt, wt, wb, 512)
                self.act(junk[:], pt[:, 0:512], AF.Square, [pb], [B("junk"), B("ss")], accum=ss[:, 0:1])
                self.act(ss[:, 1:2], ss[:, 0:1], AF.Sqrt, [B("ss")], [B("ss")], bias=self.epsc[:, 1:2], scale=1.0 / 512)
                self.recip(ss[:, 2:3], ss[:, 1:2], [B("ss")], [B("ss")])
                k = self.evslot % 3
                self.evslot += 1
                self.ts("dve", ev[k][:], pt[:, 0:512], ss[:, 2:3], None, ALU.mult, None, [pb, B("ss")], [B("ev", k)])
                pt2, pb2 = self.psum(tt % 4)
                ptb = pt2[:].bitcast(BF16)
                for c in range(4):
                    self.tr(ptb[:, c * 128:(c + 1) * 128], ev[k][:, c * 128:(c + 1) * 128], [B("ev", k)], [pb2])
                self.cp("dve", latT[:, :, tt * 128:(tt + 1) * 128], ptb[:, 0:512].rearrange("p (c t) -> p c t", t=128),
                        [pb2], [B("latT", tt)])
            lhs_lat = lambda kc, tt: (latT[:, kc, tt * 128:(tt + 1) * 128], B("latT", tt))
            akdst = lambda g4: [S["L_akT"][i][:, t0 + g4 * 512:t0 + (g4 + 1) * 512] for i in range(2)]
            prev = None
            for tt in tiles:
                pt, pb = self.psum(4 + tt % 4)
                for kc in range(4):
                    l_ap, l_b = lhs_lat(kc, tt)
                    self.mm(pt[:, 0:512], l_ap, aup[:, kc, :], kc == 0, kc == 3, [l_b, B("aup")], [pb])
                e, eb = evac(pt, pb, 512)
                self.dma("sp", S["L_av"][pas][tt * 128:(tt + 1) * 128, :], e[:, 256:512], [eb], [B("scr")])
                if prev is not None:
                    rope_T_store(*prev)
                prev = (tt, e, eb, 256, True, 128, akdst, None)
            rope_T_store(*prev)
            wt, wb = load_w(O_IK, 64)
            ikf = self.sb(p, "ikf", [128, 64], F32)
            ikst = self.sb(p, "ikst", [128, 12], F32)
            for tt in tiles:
                pt, pb = project(tt, wt, wb, 64)
                self.cp("act", ikf[:], pt[:, 0:64], [pb], [B("ikf")])
                fw.op("dve", lambda e: e.bn_stats(ikst[:, 0:6], ikf[:]), [B("ikf")], [B("ikst")])
                fw.op("dve", lambda e: e.bn_aggr(ikst[:, 6:8], ikst[:, 0:6]), [B("ikst")], [B("ikst")])
                self.act(ikst[:, 8:9], ikst[:, 7:8], AF.Sqrt, [B("ikst")], [B("ikst")], bias=self.epsc[:, 0:1])
                self.recip(ikst[:, 9:10], ikst[:, 8:9], [B("ikst")], [B("ikst")])
                k = self.evslot % 3
                self.evslot += 1
                self.ts("dve", ev[k][:, 0:64], ikf[:], ikst[:, 6:7], ikst[:, 9:10], ALU.subtract, ALU.mult,
                        [B("ikf"), B("ikst")], [B("ev", k)])
                self.rope_apply(ev[k][:, 0:64], 1, 64, 8, CC64[:, tt, :], SS64[:, tt, :], rtmp, [B("ev", k)], "r")
                self.cp("dve", ev[k][:, 64:128], ev[k][:, 0:64], [B("ev", k)], [B("ev", k)])
                rope_T_store(tt, ev[k], B("ev", k), 128, False, 128,
                             lambda g4: S["ikT"][:, t0 + g4 * 512:t0 + (g4 + 1) * 512].unsqueeze(1), None)
            wkv = I["w_kv"].rearrange("(kc p) n -> p kc n", p=128)

            def load_w2(c0, n):
                k = self.wslot % 2
                self.wslot += 1
                self.dma("pool", wts[k][:, :, 0:n], wkv[:, :, c0:c0 + n], [], [B("wp", k)])
                return wts[k], B("wp", k)

            def job_T2(c0, n, rope, dst_fn):
                wt, wb = load_w2(c0, n)
                prev = None
                for tt in tiles:
                    pt, pb = project(tt, wt, wb, n)
                    e, eb = evac(pt, pb, n)
                    if prev is not None:
                        rope_T_store(*prev)
                    prev = (tt, e, eb, n, rope, 128, dst_fn, None)
                rope_T_store(*prev)

            def job_V2(c0, n, dst_fn):
                wt, wb = load_w2(c0, n)
                for tt in tiles:
                    pt, pb = project(tt, wt, wb, n)
                    e, eb = evac(pt, pb, n)
                    self.dma("sp", dst_fn(tt), e[:, 0:n], [eb], [B("scr")])

            job_T2(0, 128, True, lambda g4: [S["L_bK"][:, t0 + g4 * 512:t0 + (g4 + 1) * 512]])
            job_V2(128, 128, lambda tt: S["L_bV"][t0 + tt * 128:t0 + (tt + 1) * 128, :])
            job_T2(256, 256, True,
                   lambda g4: S["kwT"][0:2, :, t0 + g4 * 512:t0 + (g4 + 1) * 512].rearrange("h d t -> d h t"))
            job_V2(512, 256, lambda tt: S["vw"][t0 + tt * 128:t0 + (tt + 1) * 128, :])
            job_T2(768, 256, True, lambda g4: [S["L_ckT"][i][:, t0 + g4 * 512:t0 + (g4 + 1) * 512] for i in range(2)])
            job_V2(1024, 256, lambda tt: S["L_cv"][pas][tt * 128:(tt + 1) * 128, :])
        else:
            own = range(8)
            for hg in range(2):
                job_T(O_AQ + hg * 512, 512, own, True, 128,
                      lambda g4, hg=hg: S["aqT"][hg * 4:(hg + 1) * 4, :, g4 * 512:(g4 + 1) * 512].rearrange("h d t -> d h t"))
            for hg in range(2):
                job_T(O_IQ + hg * 512, 512, own, True, 64,
                      lambda g4, hg=hg: S["iqT"][hg * 4:(hg + 1) * 4, :, g4 * 512:(g4 + 1) * 512].rearrange("h d t -> d h t"))
            for hg in range(2):
                job_T(O_BQ + hg * 512, 512, own, True, 128,
                      lambda g4, hg=hg: S["bqT"][hg * 4:(hg + 1) * 4, :, g4 * 512:(g4 + 1) * 512].rearrange("h d t -> d h t"))
            for hg in range(2):
                job_T(O_CQ + hg * 512, 512, own, True, 128,
                      lambda g4, hg=hg: S["cqT"][hg * 4:(hg + 1) * 4, :, g4 * 512:(g4 + 1) * 512].rearrange("h d t -> d h t"))
            sm = self.sb(p, "sm", [128, 24], F32)
            wt, wb = load_w(O_IW, 16)
            for tt in own:
                pt, pb = project(tt, wt, wb, 16)
                self.act(sm[:, 0:16], pt[:, 0:16], AF.Copy, [pb], [B("sm")], scale=(16 ** -0.5) * (64 ** -0.5))
                self.dma("sp", S["iw"][tt * 128:(tt + 1) * 128, :], sm[:, 0:16], [B("sm")], [B("scr")])
            wt, wb = load_w(O_BG, 24)
            for tt in own:
                pt, pb = project(tt, wt, wb, 24)
                self.act(sm[:, 0:24], pt[:, 0:24], AF.Sigmoid, [pb], [B("sm")])
                self.dma("sp", S["bg"][tt * 128:(tt + 1) * 128, :], sm[:, 0:24], [B("sm")], [B("scr")])

    def phase_compress(self):
        fw, I, S, B = self.fw, self.I, self.S, self.fw.B
        with ExitStack() as p:
            w1 = self.sb(p, "cw1", [128, 32, 128], BF16)
            w2 = self.sb(p, "cw2", [128, 128], BF16)
            peT = self.sb(p, "cpeT", [128, 32], F32)
            peTb = self.sb(p, "cpeTb", [128, 32], BF16)
            cb = self.sb(p, "ccb", [128, 1], F32)
            xT = self.sb(p, "cxT", [128, L], BF16)
            vtm = self.sb(p, "cvtm", [128, 32, 128], BF16)
            hT = self.sb(p, "chT", [128, 256], BF16)
            cov = self.sb(p, "cov", [128, 2, 64], BF16)
            self.ms("dve", self.cmpR[:].rearrange("p a b c -> p (a b c)"), 0.0, [B("cmpR")])
            self.ms("dve", self.kcT[:].rearrange("p a b -> p (a b)"), 0.0, [B("kcT")])
            self.ms("dve", hT[:], 0.0, [B("chT")])
            for c in range(2):
                self.ms("pool", cov[:, c, :], 1.0, [B("cov")])
                fw.op("pool", lambda e, c=c: e.affine_select(cov[:, c, :], cov[:, c, :], pattern=[[64, 64]], compare_op=ALU.is_gt,
                                                             fill=0.0, base=64 - 2048 * c, channel_multiplier=-16),
                      [B("cov")], [B("cov")])
                fw.op("pool", lambda e, c=c: e.affine_select(cov[:, c, :], cov[:, c, :], pattern=[[-64, 64]], compare_op=ALU.is_gt,
                                                             fill=0.0, base=32 + 2048 * c, channel_multiplier=16),
                      [B("cov")], [B("cov")])
            fw.op("pool", lambda e: e.affine_select(cov[:, 1, :], cov[:, 1, :], pattern=[[0, 64]], compare_op=ALU.is_ge,
                                                    fill=0.0, base=126, channel_multiplier=-1),
                  [B("cov")], [B("cov")])
            for g in range(2):
                for c in range(2):
                    self.cp("dve", self.cmpR[:, g, c, 129:193], cov[:, c, :], [B("cov"), B("cmpR")], [B("cmpR")])
                    self.ms("dve", self.cmpR[:, g, c, 128:129], 1.0, [B("cmpR")])
            for kv in range(2):
                self.dma("pool", w1[:], I["cmp_w1"][kv].rearrange("(j d) f -> d j f", d=128), [], [B("cw1")])
                self.dma("pool", w2[:], I["cmp_w2"][kv], [], [B("cw2")])
                self.dma("sp", peT[:], I["cmp_peT"][kv], [], [B("cpeT")])
                self.cp("dve", peTb[:], peT[:], [B("cpeT")], [B("cpeTb")])
                pt, pb = self.psum(0)
                for j in range(32):
                    self.mm(pt[:, 0:1], w1[:, j, :], peTb[:, j:j + 1], j == 0, j == 31, [B("cw1"), B("cpeTb")], [pb])
                self.cp("dve", cb[:], pt[:, 0:1], [pb], [B("ccb")])
                for g in range(2):
                    if kv == 0:
                        self.dma("sp", xT[:], S["G_bK"][g * 128:(g + 1) * 128, :], [B("scr")], [B("cxT")])
                    else:
                        self.dma("sp", vtm[:], S["G_bV"][g * L:(g + 1) * L, :].rearrange("(c p) d -> p c d", p=128), [B("scr")], [B("cvtm")])
                        for c4 in range(8):
                            ptt, ptb_ = self.psum(4 + c4 % 4)
                            ptv = ptt[:].bitcast(BF16)
                            for j_ in range(4):
                                self.tr(ptv[:, j_ * 128:(j_ + 1) * 128], vtm[:, c4 * 4 + j_, :], [B("cvtm")], [ptb_])
                            self.cp("dve", xT[:, c4 * 512:(c4 + 1) * 512], ptv[:, 0:512], [ptb_], [B("cxT")])
                    pt, pb = self.psum(1)
                    xv = xT[:].rearrange("p (i s) -> p s i", s=16)
                    for j in range(32):
                        s_, i0 = j % 16, j // 16
                        self.mm(pt[:, 0:255], w1[:, j, :], xv[:, s_, i0:i0 + 255], j == 0, j == 31, [B("cw1"), B("cxT")], [pb])
                    self.act(hT[:, 0:255], pt[:, 0:255], AF.Silu, [pb, B("ccb")], [B("chT")], bias=cb[:, 0:1])
                    if kv == 0:
                        pt2, pb2 = self.psum(2)
                        self.mm(pt2[:, 0:256], w2[:], hT[:, 0:256], True, True, [B("cw2"), B("chT")], [pb2])
                        self.cp("dve", self.kcT[:, g, 0:255], pt2[:, 0:255], [pb2], [B("kcT")])
                    else:
                        for c in range(2):
                            pt2, pb2 = self.psum(2 + c)
                            self.mm(pt2[:, 0:128], hT[:, c * 128:(c + 1) * 128], w2[:], True, True, [B("cw2"), B("chT")], [pb2])
                            npart = 128 if c == 0 else 127
                            self.cp("dve", self.cmpR[0:npart, g, c, 0:128], pt2[0:npart, 0:128], [pb2], [B("cmpR")])
            fw.fence()

    def att_head(self, p, kT, kTb, nchunks, qT_ap, qb, V_fn, dv, mask_fn, scale, po_banks, tagbase, exps):
        B = self.fw.B
        LOOK = 4
        pend = []
        for it in range(nchunks + LOOK):
            if it < nchunks:
                kc = it
                pt, pb = self.psum(kc % 4)
                self.mm(pt[:, :], kT[:, kc * 128:(kc + 1) * 128], qT_ap, True, True, [kTb, qb], [pb])
                k = self.expslot % len(exps)
                self.expslot += 1
                e, eb = exps[k], B("exp", k)
                self.act(e[:], pt[:, :], AF.Exp, [pb], [eb], scale=scale)
                m_ = mask_fn(kc)
                if m_ is not None:
                    m_ap, mb = m_
                    self.tt("dve", e[:], e[:], m_ap, ALU.mult, [eb, mb], [eb])
                pend.append((e, eb))
            if it >= LOOK:
                kc = it - LOOK
                e, eb = pend[kc]
                v_ap, vb = V_fn(kc)
                for qt in range(4):
                    po, pob = self.psum(po_banks[qt])
                    self.mm(po[:, 0:dv + 1], e[:, qt * 128:(qt + 1) * 128], v_ap, kc == 0, kc == nchunks - 1, [eb, vb], [pob])

    def phase_att_a(self):
        fw, I, S, B = self.fw, self.I, self.S, self.fw.B
        NIT = 18
        R0 = 64.0
        with ExitStack() as p:
            aqT = self.sb(p, "aqT", [128, 8, NOWN], BF16)
            iqT = self.sb(p, "iqT", [128, 8, NOWN], BF16)
            ikT = self.sb(p, "ikT", [128, L], BF16)
            iw = self.sb(p, "iw", [128, 8, 16], F32)
            kpos = self.kpos512
            qsh = self.sb(p, "qsh", [128, 8, 8], F32)
            acc = self.sb(p, "acc", [128, L], F32)
            acc2 = self.sb(p, "acc2", [128, L], F32)
            rl = [self.sb(p, f"rl{i}", [128, 512], BF16) for i in range(2)]
            bs = self.sb(p, "bs", [128, 8], F32)
            bs2 = self.sb(p, "bs2", [128, 8], F32)
            mk = self.sb(p, "mk", [128, L], BF16)
            mkT = self.sb(p, "mkT", [128, 32, 512], BF16)
            kTs = [self.sb(p, f"akT{i}", [128, L], BF16) for i in range(2)]
            Vs = [self.sb(p, f"aV{i}", [128, 32, 129], BF16) for i in range(2)]
            exps = [self.sb(p, f"aexp{i}", [128, 512], BF16) for i in range(6)]
            fin = self.sb(p, "afin", [128, 4], F32)
            yo = [self.sb(p, f"ayo{i}", [128, 1024], BF16) for i in range(4)]
            self.expslot = 0
            for h in range(8):
                self.dma("sp", aqT[:, h, :], S["aqT"][h], [B("scr")], [B("aqT")])
                self.dma("sp", iqT[:, h, :], S["iqT"][h], [B("scr")], [B("iqT")])
            self.dma("sp", ikT[:], S["ikT"][:, :], [B("scr")], [B("ikT")])
            self.dma("sp", iw[:], S["iw"].rearrange("(t p) h -> p t h", p=128), [B("scr")], [B("iw")])
            for c in range(8):
                self.ts("dve", qsh[:, :, c], self.POSq[:, 0:8], -512.0 * c, None, ALU.add, None, [B("POSq")], [B("qsh")])
            for i in range(2):
                self.ms("dve", Vs[i][:, :, 128:129], 1.0, [B("aV", i)])
            accs = [acc, acc2]
            bss = [bs, bs2]

            def gen_index(qt):
                Sk_ = 2048 if qt < 4 else 4096
                ac = accs[qt % 2]
                ab = lambda c: B("acc", qt % 2, c)
                for c in range(Sk_ // 512):
                    self.ts("dve", ac[:, c * 512:(c + 1) * 512], kpos[:], qsh[:, qt, c:c + 1], NEG, ALU.is_gt, ALU.mult,
                            [B("kpos"), B("qsh")], [ab(c)])
                for h in range(16):
                    hp = (h % 2) * 64
                    for c in range(Sk_ // 512):
                        pt, pb = self.psum((h * 8 + c) % 4)
                        self.mm(pt[:, :], iqT[hp:hp + 64, h // 2, qt * 128:(qt + 1) * 128], ikT[hp:hp + 64, c * 512:(c + 1) * 512],
                                True, True, [B("iqT"), B("ikT")], [pb])
                        k = (h * 8 + c) % 2
                        if (h * 8 + c) % 5 == 4:
                            self.ts("dve", rl[k][:], pt[:, :], 0.0, None, ALU.max, None, [pb], [B("rl", k)])
                        else:
                            self.act(rl[k][:], pt[:, :], AF.Relu, [pb], [B("rl", k)])
                        self.stt(ac[:, c * 512:(c + 1) * 512], rl[k][:], iw[:, qt, h:h + 1], ac[:, c * 512:(c + 1) * 512],
                                 ALU.mult, ALU.add, [B("rl", k), B("iw"), ab(c)], [ab(c)])
                    yield

            def gen_bisect(qt):
                Sk_ = 2048 if qt < 4 else 4096
                nch_ = Sk_ // 128
                qt4 = qt % 4
                ac = accs[qt % 2]
                bsq = bss[qt % 2]
                bb = B("bs", qt % 2)
                accb = [B("acc", qt % 2, c) for c in range(Sk_ // 512)]
                self.ms("dve", bsq[:, 0:1], 0.0, [bb])
                self.ms("dve", bsq[:, 3:4], float(Sk_ - 511), [bb])
                for it in range(NIT):
                    self.act(mk[:, 0:Sk_], ac[:, 0:Sk_], AF.Sign, accb + [bb], [B("mk"), bb],
                             bias=bsq[:, 0:1], scale=1.0, accum=bsq[:, 1:2])
                    self.act(bsq[:, 2:3], bsq[:, 1:2], AF.Sign, [bb], [bb], bias=bsq[:, 3:4], scale=1.0)
                    if it < NIT - 1:
                        w_next = R0 * (0.5 ** (it + 1))
                        self.act(bsq[:, 0:1], bsq[:, 2:3], AF.Identity, [bb], [bb], bias=bsq[:, 0:1], scale=-w_next)
                    else:
                        w_it = R0 * (0.5 ** it)
                        self.ts("dve", bsq[:, 4:5], bsq[:, 2:3], 0.5 * w_it, -0.5 * w_it, ALU.mult, ALU.add, [bb], [bb])
                        self.tt("dve", bsq[:, 0:1], bsq[:, 4:5], bsq[:, 0:1], ALU.subtract, [bb], [bb])
                    yield
                self.ts("dve", mk[:, 0:Sk_], ac[:, 0:Sk_], bsq[:, 0:1], None, ALU.is_ge, None, accb + [bb], [B("mk")])
                for c4 in range(nch_ // 4):
                    pt, pb = self.psum(4 + c4 % 4)
                    ptb = pt[:].bitcast(BF16)
                    for j in range(4):
                        kc = c4 * 4 + j
                        self.tr(ptb[:, j * 128:(j + 1) * 128], mk[:, kc * 128:(kc + 1) * 128], [B("mk")], [pb])
                    self.cp("act", mkT[:, c4 * 4:(c4 + 1) * 4, qt4 * 128:(qt4 + 1) * 128],
                            ptb[:, 0:512].rearrange("p (c t) -> p c t", t=128), [pb], [B("mkT")])
                yield

            for _ in gen_index(0):
                pass
            for slot in range(2):
                nch = 16 if slot == 0 else 32
                Sk = nch * 128
                for qt4 in range(4):
                    qt = slot * 4 + qt4
                    gb = gen_bisect(qt)
                    gn = gen_index(qt + 1) if qt < 7 else None
                    done_b, done_n = False, gn is None
                    while not (done_b and done_n):
                        if not done_b:
                            try:
                                next(gb)
                            except StopIteration:
                                done_b = True
                        if not done_n:
                            try:
                                next(gn)
                            except StopIteration:
                                done_n = True
                if slot == 0:
                    fw._emit_waits("sp", [fw.cc_last])
                for h in range(8):
                    k = h % 2
                    self.dma("sp", kTs[k][:, 0:Sk], S["G_akT"][h % 2][(h // 2) * 128:(h // 2 + 1) * 128, 0:Sk], [B("scr")], [B("akT", k)])
                    for hf in range(Sk // 2048):
                        self.dma("sp", Vs[k][:, hf * 16:(hf + 1) * 16, 0:128],
                                 S["G_av"][hf][(h // 2) * 2048:(h // 2 + 1) * 2048, (h % 2) * 128:(h % 2 + 1) * 128].rearrange("(c p) d -> p c d", p=128),
                                 [B("scr")], [B("aV", k)])
                    self.att_head(p, kTs[k], B("akT", k), nch, aqT[:, h, slot * 512:(slot + 1) * 512], B("aqT"),
                                  lambda kc, k=k: (Vs[k][:, kc, :], B("aV", k)), 128,
                                  lambda kc: (mkT[:, kc, :], B("mkT")), 128 ** -0.5, [4, 5, 6, 7], "a", exps)
                    for qt4 in range(4):
                        po, pob = self.psum(4 + qt4)
                        self.ts("dve", fin[:, 0:1], po[:, 128:129], 1e-30, None, ALU.max, None, [pob], [B("afin")])
                        self.recip(fin[:, 1:2], fin[:, 0:1], [B("afin")], [B("afin")])
                        self.ts("dve", yo[qt4][:, h * 128:(h + 1) * 128], po[:, 0:128], fin[:, 1:2], None, ALU.mult, None,
                                [pob, B("afin")], [B("ayo", qt4)])
                for qt4 in range(4):
                    qt = slot * 4 + qt4
                    self.dma("sp", S["y"][qt * 128:(qt + 1) * 128, 0:1024], yo[qt4][:], [B("ayo", qt4)], [B("s_y")])
            fw.fence()

    def phase_att_b(self):
        fw, I, S, B = self.fw, self.I, self.S, self.fw.B
        sc = 128 ** -0.5
        with ExitStack() as p:
            bqT = self.sb(p, "bqT", [128, 8, NOWN], BF16)
            bg = self.sb(p, "bg", [128, 8, 24], F32)
            E = self.sb(p, "E", [64, L], BF16)
            cend = self.sb(p, "cend", [128, 2], F32)
            cendi = self.sb(p, "cendi", [128, 2], I32)
            m64 = self.sb(p, "m64", [128, 64], F32)
            m64i = self.sb(p, "m64i", [128, 64], I32)
            kposc = self.sb(p, "kposc", [128, 64], F32)
            kposci = self.sb(p, "kposci", [128, 32], I32)
            mc = self.sb(p, "mc", [128, 2, 128], BF16)
            ec = self.sb(p, "ec", [128, 2, 128], BF16)
            imp = self.sb(p, "imp", [128, 2, 64], F32)
            t64 = self.sb(p, "t64", [128, 4, 64], F32)
            m8 = self.sb(p, "m8", [128, 16], F32)
            sel = self.sb(p, "sel", [128, 64], BF16)
            selT = self.sb(p, "selT", [64, 2, 512], BF16)
            fin = self.sb(p, "bfin", [128, 8], F32)
            yo = [self.sb(p, f"byo{i}", [128, 1024], F32) for i in range(4)]
            yob = self.sb(p, "byob", [128, 1024], BF16)
            msk = self.sb(p, "bmsk", [128, 32, 512], BF16)
            mtmp = self.sb(p, "bmtmp", [128, 512], BF16)
            kTs = [self.sb(p, f"bkT{i}", [128, L], BF16) for i in range(2)]
            Vs = [self.sb(p, f"bV{i}", [128, 32, 129], BF16) for i in range(2)]
            exps = [self.sb(p, f"bexp{i}", [128, 512], BF16) for i in range(6)]
            self.expslot = 0
            for h in range(8):
                self.dma("sp", bqT[:, h, :], S["bqT"][h], [B("scr")], [B("bqT")])
            self.dma("sp", bg[:], S["bg"].rearrange("(t p) h -> p t h", p=128), [B("scr")], [B("bg")])
            self.ms("pool", E[:], 1.0, [B("E")])
            fw.op("pool", lambda e: e.affine_select(E[:], E[:], pattern=[[1, L]], compare_op=ALU.is_ge, fill=0.0, base=0,
                                                    channel_multiplier=-64), [B("E")], [B("E")])
            fw.op("pool", lambda e: e.affine_select(E[:], E[:], pattern=[[-1, L]], compare_op=ALU.is_ge, fill=0.0, base=63,
                                                    channel_multiplier=64), [B("E")], [B("E")])
            fw.op("pool", lambda e: e.iota(cendi[:], pattern=[[2048, 2]], base=31, channel_multiplier=16), [], [B("cendi")])
            self.cp("dve", cend[:], cendi[:], [B("cendi")], [B("cend")])
            fw.op("pool", lambda e: e.iota(m64i[:], pattern=[[64, 64]], base=0, channel_multiplier=0), [], [B("m64i")])
            self.cp("dve", m64[:], m64i[:], [B("m64i")], [B("m64")])
            fw.op("pool", lambda e: e.iota(kposci[:], pattern=[[128, 32]], base=0, channel_multiplier=1), [], [B("kposci")])
            self.cp("dve", kposc[:, 0:32], kposci[:], [B("kposci")], [B("kposc")])
            self.ts("dve", kposc[:, 32:64], kposc[:, 0:32], 512.0, None, ALU.add, None, [B("kposc")], [B("kposc")])
            for i in range(2):
                self.ms("dve", Vs[i][:, :, 128:129], 1.0, [B("bV", i)])

            for slot in range(2):
                nch = 16 if slot == 0 else 32
                Sk = nch * 128
                qB = self.qposB[:, slot * 512:(slot + 1) * 512]
                for qt4 in range(4):
                    self.ms("dve", yo[qt4][:], 0.0, [B("byo", qt4)])
                for qt4 in range(4):
                    qt = slot * 4 + qt4
                    qcol = self.POSq[:, qt:qt + 1]
                    qBt = self.qposB[:, qt * 128:(qt + 1) * 128]
                    for c in range(2):
                        self.ts("dve", mc[:, c, :], qBt, cend[:, c:c + 1], None, ALU.is_ge, None, [B("qposB"), B("cend")], [B("mc")])
                    for g in range(2):
                        self.ms("dve", imp[:, g, :], 0.0, [B("imp", g)])
                    for h in range(8):
                        g = h // 4
                        pt, pb = self.psum(h % 3)
                        for c in range(2):
                            self.mm(pt[:, c * 128:(c + 1) * 128], self.kcT[:, g, c * 128:(c + 1) * 128], bqT[:, h, qt * 128:(qt + 1) * 128],
                                    True, True, [B("kcT"), B("bqT")], [pb])
                        self.act(ec[:].rearrange("p a b -> p (a b)"), pt[:, 0:256], AF.Exp, [pb], [B("ec")], scale=sc)
                        self.tt("dve", ec[:], ec[:], mc[:], ALU.mult, [B("ec"), B("mc")], [B("ec")])
                        po, pob = self.psum(3)
                        for c in range(2):
                            self.mm(po[:, 0:193], ec[:, c, :], self.cmpR[:, g, c, :], c == 0, c == 1, [B("ec"), B("cmpR")], [pob])
                        self.ts("dve", fin[:, 0:1], po[:, 128:129], 1e-30, None, ALU.max, None, [pob], [B("bfin")])
                        self.recip(fin[:, 1:2], fin[:, 0:1], [B("bfin")], [B("bfin")])
                        self.tt("dve", fin[:, 2:3], fin[:, 1:2], bg[:, qt, h * 3:h * 3 + 1], ALU.mult, [B("bfin"), B("bg")], [B("bfin")])
                        self.ts("dve", yo[qt4][:, h * 128:(h + 1) * 128], po[:, 0:128], fin[:, 2:3], None, ALU.mult, None,
                                [pob, B("bfin")], [B("byo", qt4)])
                        self.stt(imp[:, g, :], po[:, 129:193], fin[:, 1:2], imp[:, g, :], ALU.mult, ALU.add,
                                 [pob, B("bfin"), B("imp", g)], [B("imp", g)])
                    for g in range(2):
                        bt = B("t64")
                        self.ts("dve", t64[:, 0, :], m64[:], qcol, None, ALU.is_le, None, [B("m64"), B("POSq")], [bt])
                        self.ts("dve", t64[:, 1, :], m64[:], 128.0, qcol, ALU.add, ALU.is_gt, [B("m64"), B("POSq")], [bt])
                        self.tt("dve", t64[:, 1, :], t64[:, 1, :], t64[:, 0, :], ALU.mult, [bt], [bt])
                        self.ts("dve", t64[:, 2, :], m64[:], 0.0, None, ALU.is_equal, None, [B("m64")], [bt])
                        self.tt("dve", t64[:, 1, :], t64[:, 1, :], t64[:, 2, :], ALU.max, [bt], [bt])
                        self.stt(t64[:, 3, :], t64[:, 1, :], 1e6, imp[:, g, :], ALU.mult, ALU.add, [bt, B("imp", g)], [bt])
                        self.ts("dve", t64[:, 2, :], t64[:, 0, :], -1.0, -NEG, ALU.add, ALU.mult, [bt], [bt])
                        self.tt("dve", t64[:, 3, :], t64[:, 3, :], t64[:, 2, :], ALU.add, [bt], [bt])
                        fw.op("dve", lambda e: e.max(m8[:, 0:8], t64[:, 3, :]), [bt], [B("m8")])
                        fw.op("dve", lambda e: e.match_replace(t64[:, 2, :], m8[:, 0:8], t64[:, 3, :], -3e38), [bt, B("m8")], [bt])
                        fw.op("dve", lambda e: e.max(m8[:, 8:16], t64[:, 2, :]), [bt], [B("m8")])
                        self.ts("dve", sel[:], t64[:, 3, :], m8[:, 15:16], None, ALU.is_ge, None, [bt, B("m8")], [B("sel")])
                        pt, pb = self.psum(g)
                        ptb = pt[:].bitcast(BF16)
                        self.tr(ptb[0:64, 0:128], sel[:], [B("sel")], [pb])
                        self.cp("dve", selT[:, g, qt4 * 128:(qt4 + 1) * 128], ptb[0:64, 0:128], [pb], [B("selT", g)])
                for g in range(2):
                    k = g % 2
                    self.dma("sp", kTs[k][:, 0:Sk], S["G_bK"][(2 + g) * 128:(3 + g) * 128, 0:Sk], [B("scr")], [B("bkT", k)])
                    self.dma("sp", Vs[k][:, 0:nch, 0:128], S["G_bV"][(2 + g) * L:(2 + g) * L + Sk, :].rearrange("(c p) d -> p c d", p=128),
                             [B("scr")], [B("bV", k)])
                    for kc in range(nch):
                        pt, pb = self.psum(3)
                        self.mm(pt[:, :], E[:, kc * 128:(kc + 1) * 128], selT[:, g, :], True, True, [B("E"), B("selT", g)], [pb])
                        if slot == 1 and kc < 16:
                            self.cp("act", msk[:, kc, :], pt[:, :], [pb], [B("bmsk", kc)])
                        else:
                            self.ts("dve", mtmp[:], qB, kposc[:, kc:kc + 1], None, ALU.is_ge, None, [B("qposB"), B("kposc")], [B("bmtmp")])
                            self.tt("dve", msk[:, kc, :], pt[:, :], mtmp[:], ALU.mult, [pb, B("bmtmp")], [B("bmsk", kc)])
                    for h4 in range(4):
                        h = g * 4 + h4
                        self.att_head(p, kTs[k], B("bkT", k), nch, bqT[:, h, slot * 512:(slot + 1) * 512], B("bqT"),
                                      lambda kc, k=k: (Vs[k][:, kc, :], B("bV", k)), 128,
                                      lambda kc: (msk[:, kc, :], B("bmsk", kc)), sc, [4, 5, 6, 7], "b", exps)
                        self.b_finish(slot, h, 1, fin, bg, yo)
                c0, c1 = (0, 16) if slot == 0 else (12, 32)
                nw = c1 - c0
                for g in range(2):
                    k = g % 2
                    self.dma("sp", kTs[k][:, 0:nw * 128], S["kwT"][g, :, c0 * 128:c1 * 128], [B("scr")], [B("bkT", k)])
                    self.dma("sp", Vs[k][:, 0:nw, 0:128],
                             S["vw"][c0 * 128:c1 * 128, g * 128:(g + 1) * 128].rearrange("(c p) d -> p c d", p=128),
                             [B("scr")], [B("bV", k)])
                    if g == 0:
                        for kc in range(nw):
                            self.ts("dve", mtmp[:], qB, kposc[:, c0 + kc:c0 + kc + 1], None, ALU.is_ge, None,
                                    [B("qposB"), B("kposc")], [B("bmtmp")])
                            self.stt(msk[:, kc, :], qB, kposc[:, 32 + c0 + kc:33 + c0 + kc], mtmp[:], ALU.is_lt, ALU.mult,
                                     [B("qposB"), B("kposc"), B("bmtmp")], [B("bmsk", kc)])
                    for h4 in range(4):
                        h = g * 4 + h4
                        self.att_head(p, kTs[k], B("bkT", k), nw, bqT[:, h, slot * 512:(slot + 1) * 512], B("bqT"),
                                      lambda kc, k=k: (Vs[k][:, kc, :], B("bV", k)), 128,
                                      lambda kc: (msk[:, kc, :], B("bmsk", kc)), sc, [4, 5, 6, 7], "b", exps)
                        self.b_finish(slot, h, 2, fin, bg, yo)
                for qt4 in range(4):
                    qt = slot * 4 + qt4
                    self.cp("dve", yob[:], yo[qt4][:], [B("byo", qt4)], [B("byob")])
                    self.dma("sp", S["y"][qt * 128:(qt + 1) * 128, 1024:2048], yob[:], [B("byob")], [B("s_y")])
            fw.fence()

    def b_finish(self, slot, h, br, fin, bg, yo):
        B = self.fw.B
        for qt4 in range(4):
            qt = slot * 4 + qt4
            po, pob = self.psum(4 + qt4)
            self.ts("dve", fin[:, 0:1], po[:, 128:129], 1e-30, None, ALU.max, None, [pob], [B("bfin")])
            self.recip(fin[:, 1:2], fin[:, 0:1], [B("bfin")], [B("bfin")])
            self.tt("dve", fin[:, 2:3], fin[:, 1:2], bg[:, qt, h * 3 + br:h * 3 + br + 1], ALU.mult, [B("bfin"), B("bg")], [B("bfin")])
            self.stt(yo[qt4][:, h * 128:(h + 1) * 128], po[:, 0:128], fin[:, 2:3], yo[qt4][:, h * 128:(h + 1) * 128],
                     ALU.mult, ALU.add, [pob, B("bfin"), B("byo", qt4)], [B("byo", qt4)])

    def phase_att_c(self):
        fw, I, S, B = self.fw, self.I, self.S, self.fw.B
        sc = 128 ** -0.5
        with ExitStack() as p:
            cqT = self.sb(p, "cqT", [128, 8, NOWN], BF16)
            kposc = self.sb(p, "ckposc", [128, 32], F32)
            kposci = self.sb(p, "ckposci", [128, 32], I32)
            msk = self.sb(p, "cmsk", [128, 32, 512], BF16)
            kTs = [self.sb(p, f"ckT{i}", [128, L], BF16) for i in range(2)]
            Vs = [self.sb(p, f"cV{i}", [128, 32, 257], BF16) for i in range(2)]
            exps = [self.sb(p, f"cexp{i}", [128, 512], BF16) for i in range(6)]
            o0 = [self.sb(p, f"co0{i}", [128, 256], F32) for i in range(4)]
            dd = self.sb(p, "cdd", [128, 256], F32)
            junk = self.sb(p, "cjunk", [128, 256], F32)
            gB = self.sb(p, "cgB", [128, 256], F32)
            fin = self.sb(p, "cfin", [128, 8], F32)
            yo = [self.sb(p, f"cyo{i}", [128, 1024], BF16) for i in range(4)]
            self.expslot = 0
            for h in range(8):
                self.dma("sp", cqT[:, h, :], S["cqT"][h], [B("scr")], [B("cqT")])
            self.dma("sp", gB[:], I["c_subln_g"].partition_broadcast(128), [], [B("cgB")])
            fw.op("pool", lambda e: e.iota(kposci[:], pattern=[[128, 32]], base=0, channel_multiplier=1), [], [B("ckposci")])
            self.cp("dve", kposc[:], kposci[:], [B("ckposci")], [B("ckposc")])
            for i in range(2):
                self.ms("dve", Vs[i][:, :, 256:257], 1.0, [B("cV", i)])
            for slot in range(2):
                nch = 16 if slot == 0 else 32
                Sk = nch * 128
                qB = self.qposB[:, slot * 512:(slot + 1) * 512]
                vis = 0 if slot == 0 else 16
                for kc in range(vis, nch):
                    self.ts("dve", msk[:, kc, :], qB, kposc[:, kc:kc + 1], None, ALU.is_ge, None, [B("qposB"), B("ckposc")], [B("cmsk", kc)])
                for h in range(4):
                    vk = h % 2
                    for hf in range(Sk // 2048):
                        self.dma("sp", Vs[vk][:, hf * 16:(hf + 1) * 16, 0:256],
                                 S["G_cv"][hf][h * 2048:(h + 1) * 2048, :].rearrange("(c p) d -> p c d", p=128),
                                 [B("scr")], [B("cV", vk)])
                    for m in range(2):
                        hm = h * 2 + m
                        k = hm % 2
                        self.dma("sp", kTs[k][:, 0:Sk], S["G_ckT"][m][h * 128:(h + 1) * 128, 0:Sk], [B("scr")], [B("ckT", k)])
                        self.att_head(p, kTs[k], B("ckT", k), nch, cqT[:, hm, slot * 512:(slot + 1) * 512], B("cqT"),
                                      lambda kc, vk=vk: (Vs[vk][:, kc, :], B("cV", vk)), 256,
                                      lambda kc, vis=vis: ((msk[:, kc, :], B("cmsk", kc)) if kc >= vis else None), sc, [4, 5, 6, 7], "c", exps)
                        for qt4 in range(4):
                            po, pob = self.psum(4 + qt4)
                            self.ts("dve", fin[:, 0:1], po[:, 256:257], 1e-30, None, ALU.max, None, [pob], [B("cfin")])
                            self.recip(fin[:, 1:2], fin[:, 0:1], [B("cfin")], [B("cfin")])
                            if m == 0:
                                self.ts("dve", o0[qt4][:], po[:, 0:256], fin[:, 1:2], None, ALU.mult, None,
                                        [pob, B("cfin")], [B("co0", qt4)])
                            else:
                                self.tt("dve", fin[:, 2:3], fin[:, 1:2], self.lamv[:, 0:1], ALU.mult, [B("cfin"), B("lamv")], [B("cfin")])
                                self.stt(dd[:], po[:, 0:256], fin[:, 2:3], o0[qt4][:], ALU.mult, ALU.add,
                                         [pob, B("cfin"), B("co0", qt4)], [B("cdd")])
                                self.act(junk[:], dd[:], AF.Square, [B("cdd")], [B("cjunk"), B("cfin")], accum=fin[:, 3:4])
                                self.act(fin[:, 4:5], fin[:, 3:4], AF.Sqrt, [B("cfin")], [B("cfin")], bias=self.epsc[:, 1:2], scale=1.0 / 256)
                                self.recip(fin[:, 5:6], fin[:, 4:5], [B("cfin")], [B("cfin")])
                                self.tt("dve", fin[:, 6:7], fin[:, 5:6], self.lamv[:, 1:2], ALU.mult, [B("cfin"), B("lamv")], [B("cfin")])
                                self.stt(yo[qt4][:, h * 256:(h + 1) * 256], dd[:], fin[:, 6:7], gB[:], ALU.mult, ALU.mult,
                                         [B("cdd"), B("cfin"), B("cgB")], [B("cyo", qt4)])
                for qt4 in range(4):
                    qt = slot * 4 + qt4
                    self.dma("sp", S["y"][qt * 128:(qt + 1) * 128, 2048:3072], yo[qt4][:], [B("cyo", qt4)], [B("s_y")])
            fw.fence()

    def phase_merge(self):
        fw, I, S, B = self.fw, self.I, self.S, self.fw.B
        with ExitStack() as p:
          mg = self.sb(p, "mg", [128, 16, NOWN], BF16)
          with ExitStack() as p:
            yT = self.sb(p, "yT", [128, 24, NOWN], BF16)
            uT = self.sb(p, "muT", [128, 16, NOWN], BF16)
            for kc in range(16):
                self.dma("sp", uT[:, kc, :], S["uT"][kc], [B("s_uT")], [B("muT")])
            with ExitStack() as p1:
                yt = [self.sb(p1, f"yt{i}", [128, 3072], BF16) for i in range(2)]
                for tt in range(8):
                    k = tt % 2
                    self.dma("sp", yt[k][:], S["y"][tt * 128:(tt + 1) * 128, :], [B("s_y")], [B("yt", k)])
                    for c4 in range(6):
                        pt, pb = self.psum(c4 % 4)
                        ptb = pt[:].bitcast(BF16)
                        for j in range(4):
                            c = c4 * 4 + j
                            self.tr(ptb[:, j * 128:(j + 1) * 128], yt[k][:, c * 128:(c + 1) * 128], [B("yt", k)], [pb])
                        self.cp("act" if c4 % 2 else "dve", yT[:, c4 * 4:(c4 + 1) * 4, tt * 128:(tt + 1) * 128],
                                ptb[:, 0:512].rearrange("p (c t) -> p c t", t=128), [pb], [B("yT", tt)])
                fw.fence()
            yTb = [B("yT", tt) for tt in range(8)]
            with ExitStack() as p2:
                wbr = [self.sb(p2, f"wbr{i}", [128, 8, 512], BF16) for i in range(2)]
                wg = [self.sb(p2, f"wg{i}", [128, 16, 512], BF16) for i in range(2)]
                gs = [self.sb(p2, f"gs{i}", [128, 512], F32) for i in range(2)]
                ma = self.sb(p2, "ma", [128, 512], F32)
                mt = self.sb(p2, "mt", [128, 512], F32)
                wiv = I["w_in"].rearrange("(kc p) n -> p kc n", p=128)
                n = 0
                for dg in range(4):
                    for r in range(3):
                        k = n % 2
                        n += 1
                        self.dma("pool", wbr[k][:], I["w_br"][r].rearrange("(kc p) n -> p kc n", p=128)[:, :, dg * 512:(dg + 1) * 512],
                                 [], [B("wbr", k)])
                        self.dma("pool", wg[k][:], wiv[:, :, O_GL + r * 2048 + dg * 512:O_GL + r * 2048 + (dg + 1) * 512],
                                 [], [B("wg", k)])
                        for dc in range(4):
                            for half in range(2):
                                tsl = slice(half * 512, (half + 1) * 512)
                                pg, pgb = self.psum(0 + (dc * 2 + half) % 2)
                                for kc in range(16):
                                    self.mm(pg[:, :], wg[k][:, kc, dc * 128:(dc + 1) * 128], uT[:, kc, tsl], kc == 0, kc == 15,
                                            [B("wg", k), B("muT")], [pgb])
                                gk = (dc * 2 + half) % 2
                                self.act(gs[gk][:], pg[:, :], AF.Sigmoid, [pgb], [B("gs", gk)])
                                pbr, pbrb = self.psum(2 + (dc * 2 + half) % 2)
                                for kc in range(8):
                                    self.mm(pbr[:, :], wbr[k][:, kc, dc * 128:(dc + 1) * 128], yT[:, r * 8 + kc, tsl], kc == 0, kc == 7,
                                            [B("wbr", k)] + yTb[half * 4:(half + 1) * 4], [pbrb])
                                dst = mg[:, dg * 4 + dc, tsl]
                                db = B("mg", dg * 4 + dc, half)
                                if r == 0:
                                    self.tt("dve", dst, pbr[:, :], gs[gk][:], ALU.mult, [pbrb, B("gs", gk)], [db])
                                else:
                                    self.tt("dve", mt[:], pbr[:, :], gs[gk][:], ALU.mult, [pbrb, B("gs", gk)], [B("mt")])
                                    self.tt("dve", dst, dst, mt[:], ALU.add, [db, B("mt")], [db])
                fw.fence()
          fw.fence()
          mgb = lambda tt: [B("mg", c, tt // 4) for c in range(16)]
          with ExitStack() as p3:
              self.dense_out_ln(p3, mg, mgb, 16, I["w_o"], 2,
                                (lambda tt: I["xo"][tt * 128:(tt + 1) * 128, :]) if self.layer == self.layers[0] else
                                (lambda tt: S["xown1"][tt * 128:(tt + 1) * 128, :]), 0,
                                lambda tt: S["x1"][tt * 128:(tt + 1) * 128, :], B("s_x1"), "mo")
          fw.fence()

    def dense_out_ln(self, p, actT, actb, nk, w_dram, g_part, x_src, ln_idx, dst_fn, dstb, tag, halves=1, exchange=False):
        fw, I, S, B = self.fw, self.I, self.S, self.fw.B
        gB = self.sb(p, "gB", [128, D], F32)
        lg = self.sb(p, "lg", [128, D], F32)
        lb = self.sb(p, "lb", [128, D], F32)
        self.dma("sp", gB[:], S["mod"][g_part * D:(g_part + 1) * D].partition_broadcast(128), [B("s_mod")], [B(tag, "gB")])
        self.dma("sp", lg[:], I["ln_g"][ln_idx].partition_broadcast(128), [], [B(tag, "lg")])
        self.dma("sp", lb[:], I["ln_b"][ln_idx].partition_broadcast(128), [], [B(tag, "lb")])
        ntt = 8 // halves
        vb = self.sb(p, "vb", [128, ntt, D], F32)
        NW = 128 if nk > 16 else 512
        ws = [self.sb(p, f"wo{i}", [128, nk, NW], BF16) for i in range(2)]
        zt = self.sb(p, "zt", [128, 512], F32)
        stt_ = self.sb(p, "ost", [128, 4, 6], F32)
        mv = self.sb(p, "omv", [128, 4], F32)
        wv = w_dram.rearrange("(kc p) n -> p kc n", p=128)
        n = 0
        for hf in range(halves):
            for j in range(ntt):
                tt = hf * ntt + j
                self.dma("sp", vb[:, j, :], x_src(tt), [], [B(tag, "vb", j)])
            for ng in range(D // NW):
                k = n % 2
                n += 1
                self.dma("pool", ws[k][:], wv[:, :, ng * NW:(ng + 1) * NW], [], [B(tag, "w", k)])
                for j in range(ntt):
                    tt = hf * ntt + j
                    pt, pb = self.psum(4 + j % 4)
                    for kc in range(nk):
                        self.mm(pt[:, 0:NW], actT[:, kc, tt * 128:(tt + 1) * 128], ws[k][:, kc, :], kc == 0, kc == nk - 1,
                                [B(tag, "w", k)] + actb(tt), [pb])
                    csl = slice(ng * NW, (ng + 1) * NW)
                    self.tt("dve", zt[:, 0:NW], pt[:, 0:NW], gB[:, csl], ALU.mult, [pb, B(tag, "gB")], [B(tag, "zt")])
                    self.stt(vb[:, j, csl], vb[:, j, csl], ALPHA, zt[:, 0:NW], ALU.mult, ALU.add,
                             [B(tag, "vb", j), B(tag, "zt")], [B(tag, "vb", j)])
            for j in range(ntt):
                tt = hf * ntt + j
                vbj = B(tag, "vb", j)
                for c in range(4):
                    fw.op("dve", lambda e, c=c, j=j: e.bn_stats(stt_[:, c, :], vb[:, j, c * 512:(c + 1) * 512]), [vbj], [B(tag, "st")])
                fw.op("dve", lambda e: e.bn_aggr(mv[:, 0:2], stt_[:].rearrange("p a b -> p (a b)")), [B(tag, "st")], [B(tag, "mv")])
                self.act(mv[:, 2:3], mv[:, 1:2], AF.Sqrt, [B(tag, "mv")], [B(tag, "mv")], bias=self.epsc[:, 0:1])
                self.recip(mv[:, 3:4], mv[:, 2:3], [B(tag, "mv")], [B(tag, "mv")])
                self.ts("dve", vb[:, j, :], vb[:, j, :], mv[:, 0:1], mv[:, 3:4], ALU.subtract, ALU.mult, [vbj, B(tag, "mv")], [vbj])
                self.tt("dve", vb[:, j, :], vb[:, j, :], lg[:], ALU.mult, [vbj, B(tag, "lg")], [vbj])
                self.tt("dve", vb[:, j, :], vb[:, j, :], lb[:], ALU.add, [vbj, B(tag, "lb")], [vbj])
                if exchange:
                    xb_ = B("xown1", tt)
                    self.fw.dma("sp", dst_fn(tt), vb[:, j, :], [vbj], [xb_])
                    xo1, G = self.S["xown1"], self.S["G"]
                    self.fw.coll("pool", lambda e, tt=tt: e.collective_compute(
                        "AllGather", ALU.bypass, replica_groups=[[0, 1, 2, 3], [4, 5, 6, 7]],
                        ins=[xo1[tt * 128:(tt + 1) * 128, :]], outs=[G[tt]]), [xb_], [B("G")])
                else:
                    self.dma("sp", dst_fn(tt), vb[:, j, :], [vbj], [dstb])

    def phase_ffn(self, xout, exchange=False):
        fw, I, S, B = self.fw, self.I, self.S, self.fw.B
        with ExitStack() as p:
            hT = self.sb(p, "hT", [128, 44, NOWN], BF16)
            with ExitStack() as p1:
                u2 = self.sb(p1, "u2T", [128, 16, NOWN], BF16)
                with ExitStack() as pl:
                    self.ln_to_uT(pl, lambda i: S["x1"][i * 128:(i + 1) * 128, :], 8, u2, lambda kc, grp: B("u2T", grp), 2, "ln2")
                    fw.fence()
                wgt = [self.sb(p1, f"fwg{i}", [128, 16, 256], BF16) for i in range(2)]
                wup = [self.sb(p1, f"fwu{i}", [128, 16, 256], BF16) for i in range(2)]
                sg = [self.sb(p1, f"fsg{i}", [128, 512], F32) for i in range(2)]
                wv = I["w_ffn_in"].rearrange("(kc p) n -> p kc n", p=128)
                for fg in range(22):
                    k = fg % 2
                    self.dma("pool", wgt[k][:], wv[:, :, fg * 256:(fg + 1) * 256], [], [B("fwg", k)])
                    self.dma("pool", wup[k][:], wv[:, :, DFF + fg * 256:DFF + (fg + 1) * 256], [], [B("fwu", k)])
                    for dc in range(2):
                        for half in range(2):
                            tsl = slice(half * 512, (half + 1) * 512)
                            ub = [B("u2T", half)]
                            pg, pgb = self.psum((dc * 2 + half) % 2)
                            for kc in range(16):
                                self.mm(pg[:, :], wgt[k][:, kc, dc * 128:(dc + 1) * 128], u2[:, kc, tsl], kc == 0, kc == 15,
                                        [B("fwg", k)] + ub, [pgb])
                            pu, pub = self.psum(2 + (dc * 2 + half) % 2)
                            for kc in range(16):
                                self.mm(pu[:, :], wup[k][:, kc, dc * 128:(dc + 1) * 128], u2[:, kc, tsl], kc == 0, kc == 15,
                                        [B("fwu", k)] + ub, [pub])
                            sk = (dc * 2 + half) % 2
                            self.act(sg[sk][:], pg[:, :], AF.Silu, [pgb], [B("fsg", sk)])
                            self.tt("dve", hT[:, fg * 2 + dc, tsl], pu[:, :], sg[sk][:], ALU.mult, [pub, B("fsg", sk)],
                                    [B("hT", fg * 2 + dc, half)])
                fw.fence()
            hb = lambda tt: [B("hT", c, tt // 4) for c in range(44)]
            with ExitStack() as p3:
                self.dense_out_ln(p3, hT, hb, 44, I["w_ffn_out"], 5, lambda tt: S["x1"][tt * 128:(tt + 1) * 128, :], 1,
                                  lambda tt: xout[tt * 128:(tt + 1) * 128, :], B("xout"), "fo", halves=2, exchange=exchange)
            fw.fence()


_PROG_CACHE = {}


def get_prog(debug=False, phases=None, layers=(0, 1)):
    key = (debug, tuple(phases) if phases else None, tuple(layers))
    if key not in _PROG_CACHE:
        pr = Prog(debug=debug, phases=phases, layers=layers)
        _PROG_CACHE[key] = (pr, pr.build())
    return _PROG_CACHE[key]


def core_tokens(j):
    a = np.arange(512 * j, 512 * j + 512)
    b = np.arange(512 * (7 - j), 512 * (7 - j) + 512)
    return np.concatenate([a, b])


def _grow(T):
    return T * 1024 if T <= 3 else (7 - T) * 1024 + 512


def make_maps(inputs, layers=(0, 1)):
    f = np.float32
    x = np.asarray(inputs["x"], dtype=f)
    inv16 = (ROPE_THETA ** (-(np.arange(16, dtype=np.float32) * 2.0) / 32)).astype(f)
    inv8 = (ROPE_THETA ** (-(np.arange(8, dtype=np.float32) * 2.0) / 16)).astype(f)
    shared = {"inv16": inv16, "inv8": inv8}
    for l in layers:
        sfx = str(l)
        lam_init = 0.8 - 0.6 * math.exp(-0.3 * l)
        shared.update({
            "laminit" + sfx: np.array([lam_init], f),
            "w_in" + sfx: inputs["w_in"][l],
            "a_lat_g" + sfx: np.ascontiguousarray(inputs["a_lat_g"][l].reshape(4, 128).T),
            "cmp_w1" + sfx: inputs["cmp_w1"][l], "cmp_w2" + sfx: inputs["cmp_w2"][l],
            "cmp_peT" + sfx: np.ascontiguousarray(inputs["cmp_pe"][l].transpose(0, 2, 1)),
            "lam" + sfx: inputs["lam"][l], "c_subln_g" + sfx: inputs["c_subln_g"][l], "w_br" + sfx: inputs["w_br"][l],
            "w_o" + sfx: inputs["w_o"][l], "w_ffn_in" + sfx: inputs["w_ffn_in"][l], "w_ffn_out" + sfx: inputs["w_ffn_out"][l],
            "ln_g" + sfx: inputs["ln_g"][l], "ln_b" + sfx: inputs["ln_b"][l],
        })
    maps = []
    for i in range(8):
        b, j = i // 4, i % 4
        tok = core_tokens(j)
        qpos = tok.astype(f)
        m = dict(shared)
        for l in layers:
            m["w_ada" + str(l)] = np.ascontiguousarray(inputs["w_ada"][l][:, j * 3072:(j + 1) * 3072])
            m["b_ada" + str(l)] = np.ascontiguousarray(inputs["b_ada"][l][j * 3072:(j + 1) * 3072])
            wi, au = inputs["w_in"][l], inputs["a_up"][l]
            br_, g_ = j // 2, j % 2
            ck_ = O_BKV + ((br_ * 2 + 0) * 2 + g_) * 128
            cv_ = O_BKV + ((br_ * 2 + 1) * 2 + g_) * 128
            m["w_kv" + str(l)] = np.ascontiguousarray(np.concatenate([
                wi[:, ck_:ck_ + 128], wi[:, cv_:cv_ + 128], wi[:, O_BKV + 1024:O_BKV + 1536],
                wi[:, O_CK + j * 256:O_CK + (j + 1) * 256], wi[:, O_CV + j * 256:O_CV + (j + 1) * 256]], axis=1))
            m["a_up" + str(l)] = np.ascontiguousarray(np.concatenate([au[:, j * 256:(j + 1) * 256], au[:, 1024 + j * 256:1024 + (j + 1) * 256]], axis=1))
        m.update({
            "xf": x[b], "xo": np.ascontiguousarray(x[b][tok]),
            "qpos": qpos, "qposc": np.ascontiguousarray(qpos.reshape(8, 128).T),
            "ct": np.ascontiguousarray(np.asarray(inputs["c"], dtype=f)[b].reshape(16, 128).T),
        })
        maps.append(m)
    return maps


def kernel(**inputs):
    inputs = {k: np.asarray(v) for k, v in inputs.items()}
    pr, nc = get_prog()
    maps = make_maps(inputs)
    res = run_bass_kernel_spmd(nc, maps, core_ids=list(range(8)))
    out = np.empty((2, L, D), np.float32)
    for i in range(8):
        b, j = i // 4, i % 4
        out[b][core_tokens(j)] = res.results[i]["xout"]
    return out
```

```python
import math
from collections import defaultdict
from contextlib import ExitStack

import numpy as np
import concourse.bass as bass
import concourse.mybir as mybir
from concourse.bass_utils import run_bass_kernel_spmd

F32 = mybir.dt.float32
BF16 = mybir.dt.bfloat16
I32 = mybir.dt.int32
ALU = mybir.AluOpType
AF = mybir.ActivationFunctionType

D = 2048
L = 4096
NOWN = 1024
DFF = 5632
NIN = 14440
ROPE_THETA = 500000.0
ALPHA = 4 ** 0.25
NEG = -1e30

O_AQ, O_ALAT, O_IQ, O_IK, O_IW, O_BQ, O_BKV, O_BG, O_CQ, O_CK, O_CV, O_GL = (
    0, 1024, 1536, 2560, 2624, 2640, 3664, 5200, 5224, 6248, 7272, 8296)

EPOCH = 30000
DEBUG_TB = False
NDMASEM = 24


class Buf:
    __slots__ = ("w", "r")

    def __init__(self):
        self.w = None
        self.r = {}


class FW:
    def __init__(self, nc, stack):
        self.nc = nc
        self.stack = stack
        self.engs = {"pe": nc.tensor, "act": nc.scalar, "dve": nc.vector, "pool": nc.gpsimd, "sp": nc.sync}
        self.ops = {k: [] for k in self.engs}
        self.cnt = {k: 0 for k in self.engs}
        self.sems = {}
        self.waited = {k: {} for k in self.engs}
        self.dma_i = 0
        self.dma_last = [None] * NDMASEM
        self.dma_val = [0] * NDMASEM
        for i in range(NDMASEM):
            self.sems[("dma", i)] = stack.enter_context(nc.semaphore(f"dsem{i}"))
        self.n_inst = 0
        self.bufs = defaultdict(Buf)

    def B(self, *key):
        return self.bufs[key]

    def _engsem(self, eng, epoch):
        key = (eng, epoch)
        if key not in self.sems:
            self.sems[key] = self.stack.enter_context(self.nc.semaphore(f"s_{eng}_{epoch}"))
        return key

    def _emit_waits(self, eng, toks):
        need = {}
        for (k, v) in toks:
            if need.get(k, 0) < v:
                need[k] = v
        for k, v in need.items():
            if self.waited[eng].get(k, 0) >= v:
                continue
            self.waited[eng][k] = v
            h = self.sems[k]
            self.ops[eng].append(lambda e, h=h, v=v: e.wait_ge(h, v))
            self.n_inst += 1

    def _deps(self, eng, reads, writes):
        toks = []
        for b in reads:
            if b.w is not None:
                toks.append(b.w)
        for b in writes:
            if b.w is not None:
                toks.append(b.w)
            for k, v in b.r.items():
                toks.append((k, v))
        if eng == "pe":
            toks = [t for t in toks if t[0][0] != "pe"]
        return toks

    def _mark(self, tok, reads, writes):
        key, val = tok
        for b in reads:
            if b.r.get(key, 0) < val:
                b.r[key] = val
        for b in writes:
            b.w = tok
            b.r = {}

    def _last_tok(self, eng):
        c = self.cnt[eng]
        if not c:
            return None
        return ((eng, (c - 1) // EPOCH), (c - 1) % EPOCH + 1)

    def op(self, eng, fn, reads=(), writes=()):
        self._emit_waits(eng, self._deps(eng, reads, writes))
        self.cnt[eng] += 1
        c = self.cnt[eng]
        epoch, val = (c - 1) // EPOCH, (c - 1) % EPOCH + 1
        key = self._engsem(eng, epoch)
        if val == 1 and epoch > 0:
            pass
        h = self.sems[key]
        if DEBUG_TB:
            import traceback
            tb = traceback.extract_stack(limit=6)

            def run(e, fn=fn, h=h, tb=tb):
                try:
                    return fn(e).then_inc(h, 1)
                except Exception:
                    print("".join(traceback.format_list(tb)))
                    raise
            self.ops[eng].append(run)
        else:
            self.ops[eng].append(lambda e, fn=fn, h=h: fn(e).then_inc(h, 1))
        self.n_inst += 1
        tok = (key, val)
        self._mark(tok, reads, writes)
        return tok

    def dma(self, eng, out, in_, reads=(), writes=(), **kw):
        i = self.dma_i % NDMASEM
        self.dma_i += 1
        toks = self._deps(eng, reads, writes)
        if self.dma_last[i] is not None:
            toks.append(self.dma_last[i])
        self._emit_waits(eng, toks)
        self.dma_val[i] += 16
        key = ("dma", i)
        val = self.dma_val[i]
        h = self.sems[key]
        self.ops[eng].append(lambda e, h=h: e.dma_start(out=out, in_=in_, **kw).then_inc(h, 16))
        self.n_inst += 1
        tok = (key, val)
        self.dma_last[i] = tok
        self._mark(tok, reads, writes)
        return tok

    def coll(self, eng, fn, reads=(), writes=()):
        key = ("cc", 0)
        if key not in self.sems:
            self.sems[key] = self.stack.enter_context(self.nc.semaphore("ccsem"))
            self.cc_val = 0
        self._emit_waits(eng, self._deps(eng, reads, writes))
        self.cc_val += 1
        val = self.cc_val
        h = self.sems[key]
        self.ops[eng].append(lambda e, h=h: fn(e).then_inc(h, 1))
        self.n_inst += 1
        tok = (key, val)
        self._mark(tok, reads, writes)
        self.cc_last = tok
        return tok

    def dma_like(self, eng, fn, reads=(), writes=()):
        i = self.dma_i % NDMASEM
        self.dma_i += 1
        toks = self._deps(eng, reads, writes)
        if self.dma_last[i] is not None:
            toks.append(self.dma_last[i])
        self._emit_waits(eng, toks)
        self.dma_val[i] += 16
        key = ("dma", i)
        val = self.dma_val[i]
        h = self.sems[key]
        self.ops[eng].append(lambda e, h=h: fn(e).then_inc(h, 16))
        self.n_inst += 1
        tok = (key, val)
        self.dma_last[i] = tok
        self._mark(tok, reads, writes)
        return tok

    def fence(self, include_cc=False):
        toks = []
        for eng in self.engs:
            t = self._last_tok(eng)
            if t:
                toks.append(t)
        for i in range(NDMASEM):
            if self.dma_last[i] is not None:
                toks.append(self.dma_last[i])
        if include_cc and getattr(self, "cc_last", None) is not None:
            toks.append(self.cc_last)
        for eng in self.engs:
            self._emit_waits(eng, [t for t in toks if not (t[0][0] == eng)])

    def finish(self):
        toks = []
        for eng in self.engs:
            t = self._last_tok(eng)
            if t:
                toks.append(t)
        for i in range(NDMASEM):
            if self.dma_last[i] is not None:
                toks.append(self.dma_last[i])
        if getattr(self, "cc_last", None) is not None:
            toks.append(self.cc_last)
        self._emit_waits("sp", toks)

    def replay(self):
        with self.nc.Block() as block:
            @block.sync
            def _(e):
                for f in self.ops["sp"]:
                    f(e)

            @block.tensor
            def _(e):
                for f in self.ops["pe"]:
                    f(e)

            @block.scalar
            def _(e):
                for f in self.ops["act"]:
                    f(e)

            @block.vector
            def _(e):
                for f in self.ops["dve"]:
                    f(e)

            @block.gpsimd
            def _(e):
                for f in self.ops["pool"]:
                    f(e)


class Prog:
    def __init__(self, debug=False, phases=None, layers=(0, 1)):
        self.debug = debug
        self.layers = tuple(layers)
        self.phases = phases
        self.nc = bass.Bass("TRN2", target_bir_lowering=False)
        self.st = ExitStack()
        self.fw = FW(self.nc, self.st)
        self.uid = 0

    def din(self, name, shape, dt=F32):
        return self.nc.dram_tensor(name, list(shape), dt, kind="ExternalInput").ap()

    def dout(self, name, shape, dt=F32):
        return self.nc.dram_tensor(name, list(shape), dt, kind="ExternalOutput").ap()

    def dscr(self, name, shape, dt=BF16):
        kind = "ExternalOutput" if self.debug else "Internal"
        return self.nc.dram_tensor(name, list(shape), dt, kind=kind).ap()

    def sb(self, stack, name, shape, dt):
        self.uid += 1
        return stack.enter_context(self.nc.sbuf_tensor(f"{name}_{self.uid}", list(shape), dt))

    def mm(self, out, lhsT, rhs, start, stop, r, w):
        self.fw.op("pe", lambda e: e.matmul(out, lhsT, rhs, start=start, stop=stop), r, w)

    def tr(self, out, in_, r, w):
        ident = self.ident[0:in_.shape[0], 0:in_.shape[0]]
        self.fw.op("pe", lambda e: e.transpose(out, in_, ident), list(r) + [self.fw.B("ident")], w)

    def act(self, out, in_, func, r, w, bias=0.0, scale=1.0, accum=None):
        if accum is None:
            self.fw.op("act", lambda e: e.activation(out, in_, func, bias=bias, scale=scale), r, w)
        else:
            self.fw.op("act", lambda e: e.activation(out, in_, func, bias=bias, scale=scale, accum_out=accum), r, w)

    def ts(self, eng, out, in0, s1, s2, op0, op1, r, w, accum=None):
        if accum is None:
            if op1 is None:
                self.fw.op(eng, lambda e: e.tensor_scalar(out, in0, s1, None, op0), r, w)
            else:
                self.fw.op(eng, lambda e: e.tensor_scalar(out, in0, s1, s2, op0, op1), r, w)
        else:
            self.fw.op(eng, lambda e: e.tensor_scalar(out, in0, s1, s2, op0, op1, accum_out=accum), r, w)

    def tt(self, eng, out, in0, in1, op, r, w):
        self.fw.op(eng, lambda e: e.tensor_tensor(out, in0, in1, op), r, w)

    def stt(self, out, in0, scalar, in1, op0, op1, r, w, accum=None):
        if accum is None:
            self.fw.op("dve", lambda e: e.scalar_tensor_tensor(out, in0, scalar, in1, op0, op1), r, w)
        else:
            self.fw.op("dve", lambda e: e.scalar_tensor_tensor(out, in0, scalar, in1, op0, op1, accum_out=accum), r, w)

    def cp(self, eng, out, in_, r, w):
        if eng == "act":
            self.fw.op("act", lambda e: e.copy(out, in_), r, w)
        else:
            self.fw.op(eng, lambda e: e.tensor_copy(out, in_), r, w)

    def ms(self, eng, ap, val, w):
        self.fw.op(eng, lambda e: e.memset(ap, val), (), w)

    def recip(self, out, in_, r, w):
        self.fw.op("dve", lambda e: e.reciprocal(out, in_), r, w)

    UNTRACKED = (("scr",), ("s_y",), ("s_uT",), ("s_x1",), ("xout",))

    def dma(self, q, out, in_, r, w, **kw):
        un = [self.fw.bufs[k] for k in self.UNTRACKED]
        r = [b for b in r if not any(b is u for u in un)]
        w = [b for b in w if not any(b is u for u in un)]
        self.fw.dma(q, out, in_, r, w, **kw)

    def psum(self, i):
        return self.ps[i], self.fw.B("ps", i)

    def build(self):
        nc, fw, st = self.nc, self.fw, self.st
        B = fw.B
        C = {}
        C["xf"] = self.din("xf", [L, D])
        C["xo"] = self.din("xo", [NOWN, D])
        C["qpos"] = self.din("qpos", [NOWN])
        C["qposc"] = self.din("qposc", [128, 8])
        C["ct"] = self.din("ct", [128, 16])
        C["inv16"] = self.din("inv16", [16])
        C["inv8"] = self.din("inv8", [8])
        self.Il = {}
        for l in self.layers:
            W = dict(C)
            sfx = str(l)
            W["laminit"] = self.din("laminit" + sfx, [1])
            W["w_ada"] = self.din("w_ada" + sfx, [D, 3072])
            W["b_ada"] = self.din("b_ada" + sfx, [3072])
            W["w_in"] = self.din("w_in" + sfx, [D, NIN])
            W["a_lat_g"] = self.din("a_lat_g" + sfx, [128, 4])
            W["a_up"] = self.din("a_up" + sfx, [512, 512])
            W["w_kv"] = self.din("w_kv" + sfx, [D, 1280])
            W["cmp_w1"] = self.din("cmp_w1" + sfx, [2, 4096, 128])
            W["cmp_w2"] = self.din("cmp_w2" + sfx, [2, 128, 128])
            W["cmp_peT"] = self.din("cmp_peT" + sfx, [2, 128, 32])
            W["lam"] = self.din("lam" + sfx, [4, 128])
            W["c_subln_g"] = self.din("c_subln_g" + sfx, [256])
            W["w_br"] = self.din("w_br" + sfx, [3, 1024, D])
            W["w_o"] = self.din("w_o" + sfx, [D, D])
            W["w_ffn_in"] = self.din("w_ffn_in" + sfx, [D, 2 * DFF])
            W["w_ffn_out"] = self.din("w_ffn_out" + sfx, [DFF, D])
            W["ln_g"] = self.din("ln_g" + sfx, [2, D])
            W["ln_b"] = self.din("ln_b" + sfx, [2, D])
            self.Il[l] = W
        I = self.Il[self.layers[0]]
        self.I = I
        xout = self.dout("xout", [NOWN, D])

        S = {}
        S["modq"] = [self.nc.dram_tensor(f"s_modq{l}", [1, 3072], F32, kind="Internal").ap() for l in range(2)]
        S["modall"] = [self.nc.dram_tensor(f"s_modall{l}", [4, 3072], F32, kind="Internal").ap() for l in range(2)]
        S["lamd"] = self.dscr("s_lamd", [2], F32)
        idr = lambda n, shp: self.nc.dram_tensor(n, list(shp), BF16, kind="Internal").ap()
        S["L_akT"] = [idr(f"l_akT{i}", [128, L]) for i in range(2)]
        S["G_akT"] = [idr(f"g_akT{i}", [512, L]) for i in range(2)]
        S["L_av"] = [idr(f"l_av{i}", [2048, 256]) for i in range(2)]
        S["G_av"] = [idr(f"g_av{i}", [4 * 2048, 256]) for i in range(2)]
        S["L_bK"] = idr("l_bK", [128, L])
        S["G_bK"] = idr("g_bK", [512, L])
        S["L_bV"] = idr("l_bV", [L, 128])
        S["G_bV"] = idr("g_bV", [4 * L, 128])
        S["L_ckT"] = [idr(f"l_ckT{i}", [128, L]) for i in range(2)]
        S["G_ckT"] = [idr(f"g_ckT{i}", [512, L]) for i in range(2)]
        S["L_cv"] = [idr(f"l_cv{i}", [2048, 256]) for i in range(2)]
        S["G_cv"] = [idr(f"g_cv{i}", [4 * 2048, 256]) for i in range(2)]
        S["ikT"] = self.dscr("s_ikT", [128, L])
        S["aqT"] = self.dscr("s_aqT", [8, 128, NOWN])
        S["iqT"] = self.dscr("s_iqT", [8, 128, NOWN])
        S["iw"] = self.dscr("s_iw", [NOWN, 16], F32)
        S["bqT"] = self.dscr("s_bqT", [8, 128, NOWN])
        S["bg"] = self.dscr("s_bg", [NOWN, 24], F32)
        S["cqT"] = self.dscr("s_cqT", [8, 128, NOWN])
        S["kwT"] = self.dscr("s_kwT", [2, 128, L])
        S["vw"] = self.dscr("s_vw", [L, 256])
        S["uT"] = self.dscr("s_uT", [16, 128, NOWN])
        S["y"] = self.dscr("s_y", [NOWN, 3072])
        S["x1"] = self.dscr("s_x1", [NOWN, D], F32)
        S["xown1"] = self.nc.dram_tensor("s_xown1", [NOWN, D], F32, kind="Internal").ap()
        S["G"] = self.nc.dram_tensor("s_G", [8, 512, D], F32, kind="Internal").ap()
        self.S = S

        self.ps = [st.enter_context(nc.psum_tensor(f"ps{i}", [128, 512], F32)) for i in range(8)]
        g = st
        self.ident = self.sb(g, "ident", [128, 128], BF16)
        self.ones_bf = self.sb(g, "ones_bf", [128, 128], BF16)
        self.modT = self.sb(g, "modT", [128, 4, 16], F32)
        self.POSq = self.sb(g, "POSq", [128, 16], F32)
        self.qposB = self.sb(g, "qposB", [128, NOWN], F32)
        self.cmpR = self.sb(g, "cmpR", [128, 2, 2, 193], BF16)
        self.kcT = self.sb(g, "kcT", [128, 2, 256], BF16)
        self.lamv = self.sb(g, "lamv", [128, 4], F32)
        self.epsc = self.sb(g, "epsc", [128, 2], F32)
        self.kpos512 = self.sb(g, "kpos512", [128, 512], F32)
        kpi = self.sb(g, "kpos512i", [128, 512], I32)
        fw.op("pool", lambda e: e.iota(kpi[:], pattern=[[1, 512]], base=0, channel_multiplier=0), [], [B("kposi")])
        self.cp("dve", self.kpos512[:], kpi[:], [B("kposi")], [B("kpos")])
        self.ms("dve", self.epsc[:, 0:1], 1e-5, [B("epsc")])
        self.ms("dve", self.epsc[:, 1:2], 1e-6, [B("epsc")])

        ident, ones_bf = self.ident, self.ones_bf
        self.ms("pool", ident[:], 1.0, [B("ident")])
        fw.op("pool", lambda e: e.affine_select(ident[:], ident[:], pattern=[[-1, 128]], compare_op=ALU.is_equal,
                                                fill=0.0, base=0, channel_multiplier=1),
              [B("ident")], [B("ident")])
        self.ms("pool", ones_bf[:], 1.0, [B("ones")])
        self.dma("sp", self.POSq[:, 0:8], I["qposc"][:, :], [], [B("POSq")])
        self.dma("sp", self.qposB[:], I["qpos"].partition_broadcast(128), [], [B("qposB")])

        ph = self.phases
        if ph is None or "mod" in ph:
            for l in self.layers:
                self.layer = l
                self.I = self.Il[l]
                self.phase_mod()
        for li, l in enumerate(self.layers):
            self.layer = l
            self.I = self.Il[l]
            last = (li == len(self.layers) - 1)
            if ph is None or "mod" in ph:
                self.phase_mod_load()
            if ph is None or "proj" in ph:
                self.phase_proj()
            if ph is None or "attA" in ph:
                self.phase_att_a()
            if ph is None or "cmp" in ph:
                self.phase_compress()
            if ph is None or "attB" in ph:
                self.phase_att_b()
            if ph is None or "attC" in ph:
                self.phase_att_c()
            if ph is None or "merge" in ph:
                self.phase_merge()
            if ph is None or "ffn" in ph:
                self.phase_ffn(xout if last else S["xown1"], exchange=not last)
            if not last:
                fw.fence(include_cc=True)
        fw.finish()
        fw.replay()
        st.close()
        return nc

    def phase_mod(self):
        fw, I, S, B = self.fw, self.I, self.S, self.fw.B
        with ExitStack() as p:
            ct = self.sb(p, "ct", [128, 16], F32)
            cs = self.sb(p, "cs", [128, 16], BF16)
            wa = [self.sb(p, f"wa{i}", [128, 16, 512], BF16) for i in range(2)]
            brow = self.sb(p, "brow", [1, 512], F32)
            mrow = self.sb(p, "mrow", [1, 512], F32)
            self.dma("sp", ct[:], I["ct"][:, :], [], [B("ct")])
            self.act(cs[:], ct[:], AF.Silu, [B("ct")], [B("cs")])
            wv = I["w_ada"].rearrange("(kc p) n -> p kc n", p=128)
            mq = self.S["modq"][self.layer]
            for cg in range(6):
                wt = wa[cg % 2]
                self.dma("pool", wt[:], wv[:, :, cg * 512:(cg + 1) * 512], [], [B("wa", cg % 2)])
                self.dma("sp", brow[:], I["b_ada"][cg * 512:(cg + 1) * 512].unsqueeze(0), [], [B("brow")])
                pt, pb = self.psum(cg % 2)
                for kc in range(16):
                    self.mm(pt[0:1, :], cs[:, kc:kc + 1], wt[:, kc, :], kc == 0, kc == 15,
                            [B("cs"), B("wa", cg % 2)], [pb])
                self.tt("dve", mrow[:], pt[0:1, :], brow[:], ALU.add, [pb, B("brow")], [B("mrow")])
                self.dma("sp", mq[:, cg * 512:(cg + 1) * 512], mrow[:], [B("mrow")], [B("s_modq")])
            mall = self.S["modall"][self.layer]
            fw.coll("pool", lambda e: e.collective_compute(
                "AllGather", ALU.bypass, replica_groups=[[0, 1, 2, 3], [4, 5, 6, 7]], ins=[mq[:, :]], outs=[mall[:, :]]),
                [B("s_modq")], [B("s_mod")])
            fw.fence(include_cc=True)

    def phase_mod_load(self):
        fw, I, S, B = self.fw, self.I, self.S, self.fw.B
        S["mod"] = S["modall"][self.layer].rearrange("a b -> (a b)")
        with ExitStack() as p:
            mv = S["mod"].rearrange("(a kc p) -> a p kc", a=6, p=128)
            for j, a in enumerate((0, 1, 3, 4)):
                self.dma("sp", self.modT[:, j, :], mv[a], [B("s_mod")], [B("modT")], allow_slow_non_contiguous=True)
            for j in (1, 3):
                self.ts("dve", self.modT[:, j, :], self.modT[:, j, :], 1.0, None, ALU.add, None, [B("modT")], [B("modT")])
            lm = self.sb(p, "lm", [1, 4, 128], F32)
            lt = self.sb(p, "lt", [1, 2, 128], F32)
            l2 = self.sb(p, "l2", [1, 8], F32)
            self.dma("sp", lm[:], I["lam"].unsqueeze(0), [], [B("lm")])
            self.dma("sp", l2[:, 4:5], I["laminit"].unsqueeze(0), [], [B("l2")])
            self.tt("dve", lt[:, 0, :], lm[:, 0, :], lm[:, 1, :], ALU.mult, [B("lm")], [B("lt")])
            self.tt("dve", lt[:, 1, :], lm[:, 2, :], lm[:, 3, :], ALU.mult, [B("lm")], [B("lt")])
            self.fw.op("dve", lambda e: e.reduce_sum(l2[:, 0:1], lt[:, 0, :], mybir.AxisListType.X), [B("lt"), B("l2")], [B("l2")])
            self.fw.op("dve", lambda e: e.reduce_sum(l2[:, 1:2], lt[:, 1, :], mybir.AxisListType.X), [B("lt"), B("l2")], [B("l2")])
            self.act(l2[:, 2:4], l2[:, 0:2], AF.Exp, [B("l2")], [B("l2")])
            self.tt("dve", l2[:, 5:6], l2[:, 3:4], l2[:, 2:3], ALU.subtract, [B("l2")], [B("l2")])
            self.tt("dve", l2[:, 5:6], l2[:, 5:6], l2[:, 4:5], ALU.subtract, [B("l2")], [B("l2")])
            self.ts("dve", l2[:, 6:7], l2[:, 4:5], -1.0, 1.0, ALU.mult, ALU.add, [B("l2")], [B("l2")])
            self.dma("sp", S["lamd"].unsqueeze(0), l2[:, 5:7], [B("l2")], [B("s_lamd")])
            self.dma("sp", self.lamv[:, 0:2], S["lamd"].partition_broadcast(128), [B("s_lamd")], [B("lamv")])
            fw.fence()

    def rope_tables(self, p, pos, npos, tag):
        I, B = self.I, self.fw.B
        res = []
        for (half, inv_name) in ((16, "inv16"), (8, "inv8")):
            inv = self.sb(p, f"inv{half}", [128, half], F32)
            self.dma("sp", inv[:], I[inv_name].partition_broadcast(128), [], [B(tag, "inv", half)])
            ang = self.sb(p, f"ang{half}", [128, npos, half], F32)
            CC = self.sb(p, f"CC{half}", [128, npos, 2 * half], F32)
            SS = self.sb(p, f"SS{half}", [128, npos, 2 * half], F32)
            tmp = self.sb(p, f"rtmp{half}", [128, npos, half], F32)
            ki = self.sb(p, f"rki{half}", [128, npos, half], I32)
            bA, bT, bK, bC, bS = B(tag, "ang", half), B(tag, "tmp", half), B(tag, "ki", half), B(tag, "CC", half), B(tag, "SS", half)
            shp = [128, npos, half]
            self.tt("dve", ang[:], pos.unsqueeze(2).to_broadcast(shp), inv[:].unsqueeze(1).to_broadcast(shp), ALU.mult,
                    [B(tag, "inv", half), B(tag, "pos")], [bA])
            for which, shift in (("sin", 0.0), ("cos", math.pi / 2)):
                self.ts("dve", tmp[:], ang[:], shift, 1.0 / (2 * math.pi), ALU.add, ALU.mult, [bA], [bT])
                self.cp("dve", ki[:], tmp[:], [bT], [bK])
                self.cp("dve", tmp[:], ki[:], [bK], [bT])
                self.stt(tmp[:], tmp[:], -2 * math.pi, ang[:], ALU.mult, ALU.add, [bT, bA], [bT])
                if shift != 0.0:
                    self.ts("dve", tmp[:], tmp[:], shift, None, ALU.add, None, [bT], [bT])
                dst = SS if which == "sin" else CC
                bD = bS if which == "sin" else bC
                w1 = dst[:, :, 0:half]
                self.ts("dve", w1, tmp[:], math.pi, -2 * math.pi, ALU.is_gt, ALU.mult, [bT], [bD])
                self.tt("dve", tmp[:], tmp[:], w1, ALU.add, [bT, bD], [bT])
                self.ts("dve", w1, tmp[:], -math.pi, 2 * math.pi, ALU.is_lt, ALU.mult, [bT], [bD])
                self.tt("dve", tmp[:], tmp[:], w1, ALU.add, [bT, bD], [bT])
                self.act(dst[:, :, half:2 * half], tmp[:], AF.Sin, [bT], [bD])
                if which == "sin":
                    self.ts("dve", dst[:, :, 0:half], dst[:, :, half:2 * half], -1.0, None, ALU.mult, None, [bD], [bD])
                else:
                    self.cp("dve", dst[:, :, 0:half], dst[:, :, half:2 * half], [bD], [bD])
            res += [CC, SS]
        return res

    def rope_apply(self, v, nh, hd, half, CCi, SSi, tmp, rb, tag):
        B = self.fw.B
        vv = v.rearrange("p (h d) -> p h d", d=hd)
        shp = [128, nh, half]
        x1, x2 = vv[:, :, 0:half], vv[:, :, half:2 * half]
        sneg = SSi[:, 0:half].unsqueeze(1).to_broadcast(shp)
        spos = SSi[:, half:2 * half].unsqueeze(1).to_broadcast(shp)
        cc = CCi.unsqueeze(1).to_broadcast([128, nh, 2 * half])
        tb = B("ropetmp", tag)
        self.tt("dve", tmp[:, 0:nh, 0:half], x2, sneg, ALU.mult, rb, [tb])
        self.tt("dve", tmp[:, 0:nh, half:2 * half], x1, spos, ALU.mult, rb, [tb])
        self.tt("dve", vv[:, :, 0:2 * half], vv[:, :, 0:2 * half], cc, ALU.mult, rb, rb)
        self.tt("dve", vv[:, :, 0:2 * half], vv[:, :, 0:2 * half], tmp[:, 0:nh, 0:2 * half], ALU.add, list(rb) + [tb], rb)

    def ln_to_uT(self, p, src_tile_fn, ntiles, uT, ubuf, modj, tag):
        B = self.fw.B
        xt = [self.sb(p, f"lnx{i}", [128, D], F32) for i in range(2)]
        xn = self.sb(p, "lnxn", [128, 4, D], BF16)
        stt_ = self.sb(p, "lnst", [128, 2, 4, 6], F32)
        mv = self.sb(p, "lnmv", [128, 2, 4], F32)
        for i in range(ntiles):
            k = i % 2
            src = src_tile_fn(i)
            self.dma("sp", xt[k][:], src, [B("G")], [B(tag, "x", k)])
            for c in range(4):
                self.fw.op("dve", lambda e, c=c, k=k: e.bn_stats(stt_[:, k, c, :], xt[k][:, c * 512:(c + 1) * 512]),
                           [B(tag, "x", k)], [B(tag, "st", k)])
            self.fw.op("dve", lambda e, k=k: e.bn_aggr(mv[:, k, 0:2], stt_[:, k, :, :].rearrange("p a b -> p (a b)")),
                       [B(tag, "st", k)], [B(tag, "mv", k)])
            self.act(mv[:, k, 2:3], mv[:, k, 1:2], AF.Sqrt, [B(tag, "mv", k)], [B(tag, "mv", k)], bias=self.epsc[:, 0:1])
            self.recip(mv[:, k, 3:4], mv[:, k, 2:3], [B(tag, "mv", k)], [B(tag, "mv", k)])
            self.ts("dve", xn[:, i % 4, :], xt[k][:], mv[:, k, 0:1], mv[:, k, 3:4], ALU.subtract, ALU.mult,
                    [B(tag, "x", k), B(tag, "mv", k)], [B(tag, "xn", i % 4)])
            if i % 4 == 3:
                g0 = (i // 4) * 512
                for kc in range(16):
                    pt, pb = self.psum(kc % 4)
                    ptb = pt[:].bitcast(BF16)
                    for j in range(4):
                        self.tr(ptb[:, j * 128:(j + 1) * 128], xn[:, j, kc * 128:(kc + 1) * 128], [B(tag, "xn", j)], [pb])
                    self.act(uT[:, kc, g0:g0 + 512], ptb[:, 0:512], AF.Identity, [pb, B("modT")], [ubuf(kc, i // 4)],
                             bias=self.modT[:, modj, kc:kc + 1], scale=self.modT[:, modj + 1, kc:kc + 1])

    def phase_proj(self):
        fw, I, S, B = self.fw, self.I, self.S, self.fw.B
        nc = self.nc
        with ExitStack() as p0:
            aup = self.sb(p0, "aup", [128, 4, 512], BF16)
            alg = self.sb(p0, "alg", [128, 4], F32)
            self.dma("pool", aup[:], I["a_up"].rearrange("(kc p) n -> p kc n", p=128), [], [B("aup")])
            self.dma("sp", alg[:], I["a_lat_g"][:, :], [], [B("alg")])
            for kc in range(4):
                self.ts("dve", aup[:, kc, :], aup[:, kc, :], alg[:, kc:kc + 1], None, ALU.mult, None,
                        [B("aup"), B("alg")], [B("aup")])
            for pas in range(3):
                with ExitStack() as p:
                    self.proj_pass(p, pas, aup)
                fw.fence()
            pairs = []
            for i in range(2):
                pairs += [(S["L_akT"][i], S["G_akT"][i]), (S["L_av"][i], S["G_av"][i]),
                          (S["L_ckT"][i], S["G_ckT"][i]), (S["L_cv"][i], S["G_cv"][i])]
            pairs += [(S["L_bK"], S["G_bK"]), (S["L_bV"], S["G_bV"])]
            for (src_, dst_) in pairs:
                fw.coll("pool", lambda e, src_=src_, dst_=dst_: e.collective_compute(
                    "AllGather", ALU.bypass, replica_groups=[[0, 1, 2, 3], [4, 5, 6, 7]],
                    ins=[src_[:, :]], outs=[dst_[:, :]]), [], [B("Gkv")])
        fw.fence()

    def proj_pass(self, p, pas, aup):
        fw, I, S, B = self.fw, self.I, self.S, self.fw.B
        NT = 16 if pas < 2 else 8
        uT = self.sb(p, "uT", [128, 16, 2048], BF16)
        ubuf = lambda kc, grp: B("uT", grp)
        pos = self.sb(p, "pos", [128, NT], F32)
        if pas < 2:
            posi = self.sb(p, "posi", [128, NT], I32)
            fw.op("pool", lambda e: e.iota(posi[:], pattern=[[128, NT]], base=2048 * pas, channel_multiplier=1),
                  [], [B("posi")])
            self.cp("dve", pos[:], posi[:], [B("posi")], [B("rt", "pos")])
            if self.layer == self.layers[0]:
                src = lambda i: I["xf"][2048 * pas + 128 * i: 2048 * pas + 128 * (i + 1), :]
            else:
                def src(i):
                    t0_ = 2048 * pas + 128 * i
                    T = t0_ // 512
                    r_ = T if T <= 3 else 7 - T
                    o0 = (t0_ % 512) + (0 if T <= 3 else 512)
                    return S["G"][o0 // 128, r_ * 128:(r_ + 1) * 128, :]
        else:
            self.cp("dve", pos[:], self.POSq[:, 0:8], [B("POSq")], [B("rt", "pos")])
            if self.layer == self.layers[0]:
                src = lambda i: I["xo"][128 * i:128 * (i + 1), :]
            else:
                src = lambda i: S["xown1"][128 * i:128 * (i + 1), :]
        CC128, SS128, CC64, SS64 = self.rope_tables(p, pos[:], NT, "rt")
        with ExitStack() as pl:
            self.ln_to_uT(pl, src, NT, uT, ubuf, 0, "ln")
            fw.fence()
        if pas == 2:
            for kc in range(16):
                self.dma("sp", S["uT"][kc, :, :], uT[:, kc, 0:1024], [B("uT", 0), B("uT", 1)], [B("s_uT")])

        wts = [self.sb(p, f"wp{i}", [128, 16, 512], BF16) for i in range(2)]
        ev = [self.sb(p, f"ev{i}", [128, 512], BF16) for i in range(3)]
        rtmp = self.sb(p, "rtmp", [128, 8, 32], BF16)
        stg = [self.sb(p, f"stg{i}", [128, 4, 512], BF16) for i in range(2)]
        wv = I["w_in"].rearrange("(kc p) n -> p kc n", p=128)
        self.wslot = 0
        self.evslot = 0
        self.stgslot = 0

        def load_w(c0, n):
            k = self.wslot % 2
            self.wslot += 1
            self.dma("pool", wts[k][:, :, 0:n], wv[:, :, c0:c0 + n], [], [B("wp", k)])
            return wts[k], B("wp", k)

        def project(tt, wt, wb, n, lhs=None, nk=16):
            pt, pb = self.psum(4 + tt % 4)
            for kc in range(nk):
                if lhs is None:
                    l_ap, l_b = uT[:, kc, tt * 128:(tt + 1) * 128], B("uT", tt // 4)
                else:
                    l_ap, l_b = lhs(kc, tt)
                self.mm(pt[:, 0:n], l_ap, wt[:, kc, 0:n], kc == 0, kc == nk - 1, [l_b, wb], [pb])
            return pt, pb

        def evac(pt, pb, n):
            k = self.evslot % 3
            self.evslot += 1
            self.act(ev[k][:, 0:n], pt[:, 0:n], AF.Copy, [pb], [B("ev", k)])
            return ev[k], B("ev", k)

        def rope_T_store(tt, e, eb, n, rope, hd, dst_fn, t0_fn, tiles_per_store=4):
            nch = n // 128
            if rope:
                if hd == 128:
                    self.rope_apply(e[:, 0:n], nch, 128, 16, CC128[:, tt, :], SS128[:, tt, :], rtmp, [eb], "r")
                else:
                    self.rope_apply(e[:, 0:n], n // 64, 64, 8, CC64[:, tt, :], SS64[:, tt, :], rtmp, [eb], "r")
            pt, pb = self.psum(tt % 4)
            ptb = pt[:].bitcast(BF16)
            for c in range(nch):
                self.tr(ptb[:, c * 128:(c + 1) * 128], e[:, c * 128:(c + 1) * 128], [eb], [pb])
            sk = self.stgslot % 2
            sg = stg[sk]
            j = tt % 4
            self.cp("dve", sg[:, 0:nch, j * 128:(j + 1) * 128], ptb[:, 0:nch * 128].rearrange("p (c t) -> p c t", t=128),
                    [pb], [B("stg", sk)])
            if j == 3:
                self.stgslot += 1
                dst = dst_fn(tt // 4)
                if isinstance(dst, list):
                    for c_, d_ in enumerate(dst):
                        self.dma("sp", d_, sg[:, c_, :], [B("stg", sk)], [B("scr")])
                else:
                    self.dma("sp", dst, sg[:, 0:nch, :], [B("stg", sk)], [B("scr")])

        def job_T(c0, n, tiles, rope, hd, dst_fn):
            wt, wb = load_w(c0, n)
            prev = None
            for tt in tiles:
                pt, pb = project(tt, wt, wb, n)
                e, eb = evac(pt, pb, n)
                if prev is not None:
                    rope_T_store(*prev)
                prev = (tt, e, eb, n, rope, hd, dst_fn, None)
            rope_T_store(*prev)

        def job_V(c0, n, tiles, dst_fn):
            wt, wb = load_w(c0, n)
            for tt in tiles:
                pt, pb = project(tt, wt, wb, n)
                e, eb = evac(pt, pb, n)
                self.dma("sp", dst_fn(tt), e[:, 0:n], [eb], [B("scr")])

        if pas < 2:
            t0 = 2048 * pas
            tiles = range(16)
            latT = self.sb(p, "latT", [128, 4, 2048], BF16)
            ss = self.sb(p, "ss", [128, 4], F32)
            junk = self.sb(p, "junk", [128, 512], BF16)
            wt, wb = load_w(O_ALAT, 512)
            for tt in tiles:
                pt, pb = project(tt, wt, wb, 512)
                self.act(junk[:], pt[:, 0:512], AF.Square, [pb], [B("junk"), B("ss")], accum=ss[:, 0:1])
                self.act(ss[:, 1:2], ss[:, 0:1], AF.Sqrt, [B("ss")], [B("ss")], bias=self.epsc[:, 1:2], scale=1.0 / 512)
                self.recip(ss[:, 2:3], ss[:, 1:2], [B("ss")], [B("ss")])
                k = self.evslot % 3
                self.evslot += 1
                self.ts("dve", ev[k][:], pt[:, 0:512], ss[:, 2:3], None, ALU.mult, None, [pb, B("ss")], [B("ev", k)])
                pt2, pb2 = self.psum(tt % 4)
                ptb = pt2[:].bitcast(BF16)
                for c in range(4):
                    self.tr(ptb[:, c * 128:(c + 1) * 128], ev[k][:, c * 128:(c + 1) * 128], [B("ev", k)], [pb2])
                self.cp("dve", latT[:, :, tt * 128:(tt + 1) * 128], ptb[:, 0:512].rearrange("p (c t) -> p c t", t=128),
                        [pb2], [B("latT", tt)])
            lhs_lat = lambda kc, tt: (latT[:, kc, tt * 128:(tt + 1) * 128], B("latT", tt))
            akdst = lambda g4: [S["L_akT"][i][:, t0 + g4 * 512:t0 + (g4 + 1) * 512] for i in range(2)]
            prev = None
            for tt in tiles:
                pt, pb = self.psum(4 + tt % 4)
                for kc in range(4):
                    l_ap, l_b = lhs_lat(kc, tt)
                    self.mm(pt[:, 0:512], l_ap, aup[:, kc, :], kc == 0, kc == 3, [l_b, B("aup")], [pb])
                e, eb = evac(pt, pb, 512)
                self.dma("sp", S["L_av"][pas][tt * 128:(tt + 1) * 128, :], e[:, 256:512], [eb], [B("scr")])
                if prev is not None:
                    rope_T_store(*prev)
                prev = (tt, e, eb, 256, True, 128, akdst, None)
            rope_T_store(*prev)
            wt, wb = load_w(O_IK, 64)
            ikf = self.sb(p, "ikf", [128, 64], F32)
            ikst = self.sb(p, "ikst", [128, 12], F32)
            for tt in tiles:
                pt, pb = project(tt, wt, wb, 64)
                self.cp("act", ikf[:], pt[:, 0:64], [pb], [B("ikf")])
                fw.op("dve", lambda e: e.bn_stats(ikst[:, 0:6], ikf[:]), [B("ikf")], [B("ikst")])
                fw.op("dve", lambda e: e.bn_aggr(ikst[:, 6:8], ikst[:, 0:6]), [B("ikst")], [B("ikst")])
                self.act(ikst[:, 8:9], ikst[:, 7:8], AF.Sqrt, [B("ikst")], [B("ikst")], bias=self.epsc[:, 0:1])
                self.recip(ikst[:, 9:10], ikst[:, 8:9], [B("ikst")], [B("ikst")])
                k = self.evslot % 3
                self.evslot += 1
                self.ts("dve", ev[k][:, 0:64], ikf[:], ikst[:, 6:7], ikst[:, 9:10], ALU.subtract, ALU.mult,
                        [B("ikf"), B("ikst")], [B("ev", k)])
                self.rope_apply(ev[k][:, 0:64], 1, 64, 8, CC64[:, tt, :], SS64[:, tt, :], rtmp, [B("ev", k)], "r")
                self.cp("dve", ev[k][:, 64:128], ev[k][:, 0:64], [B("ev", k)], [B("ev", k)])
                rope_T_store(tt, ev[k], B("ev", k), 128, False, 128,
                             lambda g4: S["ikT"][:, t0 + g4 * 512:t0 + (g4 + 1) * 512].unsqueeze(1), None)
            wkv = I["w_kv"].rearrange("(kc p) n -> p kc n", p=128)

            def load_w2(c0, n):
                k = self.wslot % 2
                self.wslot += 1
                self.dma("pool", wts[k][:, :, 0:n], wkv[:, :, c0:c0 + n], [], [B("wp", k)])
                return wts[k], B("wp", k)

            def job_T2(c0, n, rope, dst_fn):
                wt, wb = load_w2(c0, n)
                prev = None
                for tt in tiles:
                    pt, pb = project(tt, wt, wb, n)
                    e, eb = evac(pt, pb, n)
                    if prev is not None:
                        rope_T_store(*prev)
                    prev = (tt, e, eb, n, rope, 128, dst_fn, None)
                rope_T_store(*prev)

            def job_V2(c0, n, dst_fn):
                wt, wb = load_w2(c0, n)
                for tt in tiles:
                    pt, pb = project(tt, wt, wb, n)
                    e, eb = evac(pt, pb, n)
                    self.dma("sp", dst_fn(tt), e[:, 0:n], [eb], [B("scr")])

            job_T2(0, 128, True, lambda g4: [S["L_bK"][:, t0 + g4 * 512:t0 + (g4 + 1) * 512]])
            job_V2(128, 128, lambda tt: S["L_bV"][t0 + tt * 128:t0 + (tt + 1) * 128, :])
            job_T2(256, 256, True,
                   lambda g4: S["kwT"][0:2, :, t0 + g4 * 512:t0 + (g4 + 1) * 512].rearrange("h d t -> d h t"))
            job_V2(512, 256, lambda tt: S["vw"][t0 + tt * 128:t0 + (tt + 1) * 128, :])
            job_T2(768, 256, True, lambda g4: [S["L_ckT"][i][:, t0 + g4 * 512:t0 + (g4 + 1) * 512] for i in range(2)])
            job_V2(1024, 256, lambda tt: S["L_cv"][pas][tt * 128:(tt + 1) * 128, :])
        else:
            own = range(8)
            for hg in range(2):
                job_T(O_AQ + hg * 512, 512, own, True, 128,
                      lambda g4, hg=hg: S["aqT"][hg * 4:(hg + 1) * 4, :, g4 * 512:(g4 + 1) * 512].rearrange("h d t -> d h t"))
            for hg in range(2):
                job_T(O_IQ + hg * 512, 512, own, True, 64,
                      lambda g4, hg=hg: S["iqT"][hg * 4:(hg + 1) * 4, :, g4 * 512:(g4 + 1) * 512].rearrange("h d t -> d h t"))
            for hg in range(2):
                job_T(O_BQ + hg * 512, 512, own, True, 128,
                      lambda g4, hg=hg: S["bqT"][hg * 4:(hg + 1) * 4, :, g4 * 512:(g4 + 1) * 512].rearrange("h d t -> d h t"))
            for hg in range(2):
                job_T(O_CQ + hg * 512, 512, own, True, 128,
                      lambda g4, hg=hg: S["cqT"][hg * 4:(hg + 1) * 4, :, g4 * 512:(g4 + 1) * 512].rearrange("h d t -> d h t"))
            sm = self.sb(p, "sm", [128, 24], F32)
            wt, wb = load_w(O_IW, 16)
            for tt in own:
                pt, pb = project(tt, wt, wb, 16)
                self.act(sm[:, 0:16], pt[:, 0:16], AF.Copy, [pb], [B("sm")], scale=(16 ** -0.5) * (64 ** -0.5))
                self.dma("sp", S["iw"][tt * 128:(tt + 1) * 128, :], sm[:, 0:16], [B("sm")], [B("scr")])
            wt, wb = load_w(O_BG, 24)
            for tt in own:
                pt, pb = project(tt, wt, wb, 24)
                self.act(sm[:, 0:24], pt[:, 0:24], AF.Sigmoid, [pb], [B("sm")])
                self.dma("sp", S["bg"][tt * 128:(tt + 1) * 128, :], sm[:, 0:24], [B("sm")], [B("scr")])

    def phase_compress(self):
        fw, I, S, B = self.fw, self.I, self.S, self.fw.B
        with ExitStack() as p:
            w1 = self.sb(p, "cw1", [128, 32, 128], BF16)
            w2 = self.sb(p, "cw2", [128, 128], BF16)
            peT = self.sb(p, "cpeT", [128, 32], F32)
            peTb = self.sb(p, "cpeTb", [128, 32], BF16)
            cb = self.sb(p, "ccb", [128, 1], F32)
            xT = self.sb(p, "cxT", [128, L], BF16)
            vtm = self.sb(p, "cvtm", [128, 32, 128], BF16)
            hT = self.sb(p, "chT", [128, 256], BF16)
            cov = self.sb(p, "cov", [128, 2, 64], BF16)
            self.ms("dve", self.cmpR[:].rearrange("p a b c -> p (a b c)"), 0.0, [B("cmpR")])
            self.ms("dve", self.kcT[:].rearrange("p a b -> p (a b)"), 0.0, [B("kcT")])
            self.ms("dve", hT[:], 0.0, [B("chT")])
            for c in range(2):
                self.ms("pool", cov[:, c, :], 1.0, [B("cov")])
                fw.op("pool", lambda e, c=c: e.affine_select(cov[:, c, :], cov[:, c, :], pattern=[[64, 64]], compare_op=ALU.is_gt,
                                                             fill=0.0, base=64 - 2048 * c, channel_multiplier=-16),
                      [B("cov")], [B("cov")])
                fw.op("pool", lambda e, c=c: e.affine_select(cov[:, c, :], cov[:, c, :], pattern=[[-64, 64]], compare_op=ALU.is_gt,
                                                             fill=0.0, base=32 + 2048 * c, channel_multiplier=16),
                      [B("cov")], [B("cov")])
            fw.op("pool", lambda e: e.affine_select(cov[:, 1, :], cov[:, 1, :], pattern=[[0, 64]], compare_op=ALU.is_ge,
                                                    fill=0.0, base=126, channel_multiplier=-1),
                  [B("cov")], [B("cov")])
            for g in range(2):
                for c in range(2):
                    self.cp("dve", self.cmpR[:, g, c, 129:193], cov[:, c, :], [B("cov"), B("cmpR")], [B("cmpR")])
                    self.ms("dve", self.cmpR[:, g, c, 128:129], 1.0, [B("cmpR")])
            for kv in range(2):
                self.dma("pool", w1[:], I["cmp_w1"][kv].rearrange("(j d) f -> d j f", d=128), [], [B("cw1")])
                self.dma("pool", w2[:], I["cmp_w2"][kv], [], [B("cw2")])
                self.dma("sp", peT[:], I["cmp_peT"][kv], [], [B("cpeT")])
                self.cp("dve", peTb[:], peT[:], [B("cpeT")], [B("cpeTb")])
                pt, pb = self.psum(0)
                for j in range(32):
                    self.mm(pt[:, 0:1], w1[:, j, :], peTb[:, j:j + 1], j == 0, j == 31, [B("cw1"), B("cpeTb")], [pb])
                self.cp("dve", cb[:], pt[:, 0:1], [pb], [B("ccb")])
                for g in range(2):
                    if kv == 0:
                        self.dma("sp", xT[:], S["G_bK"][g * 128:(g + 1) * 128, :], [B("scr")], [B("cxT")])
                    else:
                        self.dma("sp", vtm[:], S["G_bV"][g * L:(g + 1) * L, :].rearrange("(c p) d -> p c d", p=128), [B("scr")], [B("cvtm")])
                        for c4 in range(8):
                            ptt, ptb_ = self.psum(4 + c4 % 4)
                            ptv = ptt[:].bitcast(BF16)
                            for j_ in range(4):
                                self.tr(ptv[:, j_ * 128:(j_ + 1) * 128], vtm[:, c4 * 4 + j_, :], [B("cvtm")], [ptb_])
                            self.cp("dve", xT[:, c4 * 512:(c4 + 1) * 512], ptv[:, 0:512], [ptb_], [B("cxT")])
                    pt, pb = self.psum(1)
                    xv = xT[:].rearrange("p (i s) -> p s i", s=16)
                    for j in range(32):
                        s_, i0 = j % 16, j // 16
                        self.mm(pt[:, 0:255], w1[:, j, :], xv[:, s_, i0:i0 + 255], j == 0, j == 31, [B("cw1"), B("cxT")], [pb])
                    self.act(hT[:, 0:255], pt[:, 0:255], AF.Silu, [pb, B("ccb")], [B("chT")], bias=cb[:, 0:1])
                    if kv == 0:
                        pt2, pb2 = self.psum(2)
                        self.mm(pt2[:, 0:256], w2[:], hT[:, 0:256], True, True, [B("cw2"), B("chT")], [pb2])
                        self.cp("dve", self.kcT[:, g, 0:255], pt2[:, 0:255], [pb2], [B("kcT")])
                    else:
                        for c in range(2):
                            pt2, pb2 = self.psum(2 + c)
                            self.mm(pt2[:, 0:128], hT[:, c * 128:(c + 1) * 128], w2[:], True, True, [B("cw2"), B("chT")], [pb2])
                            npart = 128 if c == 0 else 127
                            self.cp("dve", self.cmpR[0:npart, g, c, 0:128], pt2[0:npart, 0:128], [pb2], [B("cmpR")])
            fw.fence()

    def att_head(self, p, kT, kTb, nchunks, qT_ap, qb, V_fn, dv, mask_fn, scale, po_banks, tagbase, exps):
        B = self.fw.B
        LOOK = 4
        pend = []
        for it in range(nchunks + LOOK):
            if it < nchunks:
                kc = it
                pt, pb = self.psum(kc % 4)
                self.mm(pt[:, :], kT[:, kc * 128:(kc + 1) * 128], qT_ap, True, True, [kTb, qb], [pb])
                k = self.expslot % len(exps)
                self.expslot += 1
                e, eb = exps[k], B("exp", k)
                self.act(e[:], pt[:, :], AF.Exp, [pb], [eb], scale=scale)
                m_ = mask_fn(kc)
                if m_ is not None:
                    m_ap, mb = m_
                    self.tt("dve", e[:], e[:], m_ap, ALU.mult, [eb, mb], [eb])
                pend.append((e, eb))
            if it >= LOOK:
                kc = it - LOOK
                e, eb = pend[kc]
                v_ap, vb = V_fn(kc)
                for qt in range(4):
                    po, pob = self.psum(po_banks[qt])
                    self.mm(po[:, 0:dv + 1], e[:, qt * 128:(qt + 1) * 128], v_ap, kc == 0, kc == nchunks - 1, [eb, vb], [pob])

    def phase_att_a(self):
        fw, I, S, B = self.fw, self.I, self.S, self.fw.B
        NIT = 18
        R0 = 64.0
        with ExitStack() as p:
            aqT = self.sb(p, "aqT", [128, 8, NOWN], BF16)
            iqT = self.sb(p, "iqT", [128, 8, NOWN], BF16)
            ikT = self.sb(p, "ikT", [128, L], BF16)
            iw = self.sb(p, "iw", [128, 8, 16], F32)
            kpos = self.kpos512
            qsh = self.sb(p, "qsh", [128, 8, 8], F32)
            acc = self.sb(p, "acc", [128, L], F32)
            acc2 = self.sb(p, "acc2", [128, L], F32)
            rl = [self.sb(p, f"rl{i}", [128, 512], BF16) for i in range(2)]
            bs = self.sb(p, "bs", [128, 8], F32)
            bs2 = self.sb(p, "bs2", [128, 8], F32)
            mk = self.sb(p, "mk", [128, L], BF16)
            mkT = self.sb(p, "mkT", [128, 32, 512], BF16)
            kTs = [self.sb(p, f"akT{i}", [128, L], BF16) for i in range(2)]
            Vs = [self.sb(p, f"aV{i}", [128, 32, 129], BF16) for i in range(2)]
            exps = [self.sb(p, f"aexp{i}", [128, 512], BF16) for i in range(6)]
            fin = self.sb(p, "afin", [128, 4], F32)
            yo = [self.sb(p, f"ayo{i}", [128, 1024], BF16) for i in range(4)]
            self.expslot = 0
            for h in range(8):
                self.dma("sp", aqT[:, h, :], S["aqT"][h], [B("scr")], [B("aqT")])
                self.dma("sp", iqT[:, h, :], S["iqT"][h], [B("scr")], [B("iqT")])
            self.dma("sp", ikT[:], S["ikT"][:, :], [B("scr")], [B("ikT")])
            self.dma("sp", iw[:], S["iw"].rearrange("(t p) h -> p t h", p=128), [B("scr")], [B("iw")])
            for c in range(8):
                self.ts("dve", qsh[:, :, c], self.POSq[:, 0:8], -512.0 * c, None, ALU.add, None, [B("POSq")], [B("qsh")])
            for i in range(2):
                self.ms("dve", Vs[i][:, :, 128:129], 1.0, [B("aV", i)])
            accs = [acc, acc2]
            bss = [bs, bs2]

            def gen_index(qt):
                Sk_ = 2048 if qt < 4 else 4096
                ac = accs[qt % 2]
                ab = lambda c: B("acc", qt % 2, c)
                for c in range(Sk_ // 512):
                    self.ts("dve", ac[:, c * 512:(c + 1) * 512], kpos[:], qsh[:, qt, c:c + 1], NEG, ALU.is_gt, ALU.mult,
                            [B("kpos"), B("qsh")], [ab(c)])
                for h in range(16):
                    hp = (h % 2) * 64
                    for c in range(Sk_ // 512):
                        pt, pb = self.psum((h * 8 + c) % 4)
                        self.mm(pt[:, :], iqT[hp:hp + 64, h // 2, qt * 128:(qt + 1) * 128], ikT[hp:hp + 64, c * 512:(c + 1) * 512],
                                True, True, [B("iqT"), B("ikT")], [pb])
                        k = (h * 8 + c) % 2
                        if (h * 8 + c) % 5 == 4:
                            self.ts("dve", rl[k][:], pt[:, :], 0.0, None, ALU.max, None, [pb], [B("rl", k)])
                        else:
                            self.act(rl[k][:], pt[:, :], AF.Relu, [pb], [B("rl", k)])
                        self.stt(ac[:, c * 512:(c + 1) * 512], rl[k][:], iw[:, qt, h:h + 1], ac[:, c * 512:(c + 1) * 512],
                                 ALU.mult, ALU.add, [B("rl", k), B("iw"), ab(c)], [ab(c)])
                    yield

            def gen_bisect(qt):
                Sk_ = 2048 if qt < 4 else 4096
                nch_ = Sk_ // 128
                qt4 = qt % 4
                ac = accs[qt % 2]
                bsq = bss[qt % 2]
                bb = B("bs", qt % 2)
                accb = [B("acc", qt % 2, c) for c in range(Sk_ // 512)]
                self.ms("dve", bsq[:, 0:1], 0.0, [bb])
                self.ms("dve", bsq[:, 3:4], float(Sk_ - 511), [bb])
                for it in range(NIT):
                    self.act(mk[:, 0:Sk_], ac[:, 0:Sk_], AF.Sign, accb + [bb], [B("mk"), bb],
                             bias=bsq[:, 0:1], scale=1.0, accum=bsq[:, 1:2])
                    self.act(bsq[:, 2:3], bsq[:, 1:2], AF.Sign, [bb], [bb], bias=bsq[:, 3:4], scale=1.0)
                    if it < NIT - 1:
                        w_next = R0 * (0.5 ** (it + 1))
                        self.act(bsq[:, 0:1], bsq[:, 2:3], AF.Identity, [bb], [bb], bias=bsq[:, 0:1], scale=-w_next)
                    else:
                        w_it = R0 * (0.5 ** it)
                        self.ts("dve", bsq[:, 4:5], bsq[:, 2:3], 0.5 * w_it, -0.5 * w_it, ALU.mult, ALU.add, [bb], [bb])
                        self.tt("dve", bsq[:, 0:1], bsq[:, 4:5], bsq[:, 0:1], ALU.subtract, [bb], [bb])
                    yield
                self.ts("dve", mk[:, 0:Sk_], ac[:, 0:Sk_], bsq[:, 0:1], None, ALU.is_ge, None, accb + [bb], [B("mk")])
                for c4 in range(nch_ // 4):
                    pt, pb = self.psum(4 + c4 % 4)
                    ptb = pt[:].bitcast(BF16)
                    for j in range(4):
                        kc = c4 * 4 + j
                        self.tr(ptb[:, j * 128:(j + 1) * 128], mk[:, kc * 128:(kc + 1) * 128], [B("mk")], [pb])
                    self.cp("act", mkT[:, c4 * 4:(c4 + 1) * 4, qt4 * 128:(qt4 + 1) * 128],
                            ptb[:, 0:512].rearrange("p (c t) -> p c t", t=128), [pb], [B("mkT")])
                yield

            for _ in gen_index(0):
                pass
            for slot in range(2):
                nch = 16 if slot == 0 else 32
                Sk = nch * 128
                for qt4 in range(4):
                    qt = slot * 4 + qt4
                    gb = gen_bisect(qt)
                    gn = gen_index(qt + 1) if qt < 7 else None
                    done_b, done_n = False, gn is None
                    while not (done_b and done_n):
                        if not done_b:
                            try:
                                next(gb)
                            except StopIteration:
                                done_b = True
                        if not done_n:
                            try:
                                next(gn)
                            except StopIteration:
                                done_n = True
                if slot == 0:
                    fw._emit_waits("sp", [fw.cc_last])
                for h in range(8):
                    k = h % 2
                    self.dma("sp", kTs[k][:, 0:Sk], S["G_akT"][h % 2][(h // 2) * 128:(h // 2 + 1) * 128, 0:Sk], [B("scr")], [B("akT", k)])
                    for hf in range(Sk // 2048):
                        self.dma("sp", Vs[k][:, hf * 16:(hf + 1) * 16, 0:128],
                                 S["G_av"][hf][(h // 2) * 2048:(h // 2 + 1) * 2048, (h % 2) * 128:(h % 2 + 1) * 128].rearrange("(c p) d -> p c d", p=128),
                                 [B("scr")], [B("aV", k)])
                    self.att_head(p, kTs[k], B("akT", k), nch, aqT[:, h, slot * 512:(slot + 1) * 512], B("aqT"),
                                  lambda kc, k=k: (Vs[k][:, kc, :], B("aV", k)), 128,
                                  lambda kc: (mkT[:, kc, :], B("mkT")), 128 ** -0.5, [4, 5, 6, 7], "a", exps)
                    for qt4 in range(4):
                        po, pob = self.psum(4 + qt4)
                        self.ts("dve", fin[:, 0:1], po[:, 128:129], 1e-30, None, ALU.max, None, [pob], [B("afin")])
                        self.recip(fin[:, 1:2], fin[:, 0:1], [B("afin")], [B("afin")])
                        self.ts("dve", yo[qt4][:, h * 128:(h + 1) * 128], po[:, 0:128], fin[:, 1:2], None, ALU.mult, None,
                                [pob, B("afin")], [B("ayo", qt4)])
                for qt4 in range(4):
                    qt = slot * 4 + qt4
                    self.dma("sp", S["y"][qt * 128:(qt + 1) * 128, 0:1024], yo[qt4][:], [B("ayo", qt4)], [B("s_y")])
            fw.fence()

    def phase_att_b(self):
        fw, I, S, B = self.fw, self.I, self.S, self.fw.B
        sc = 128 ** -0.5
        with ExitStack() as p:
            bqT = self.sb(p, "bqT", [128, 8, NOWN], BF16)
            bg = self.sb(p, "bg", [128, 8, 24], F32)
            E = self.sb(p, "E", [64, L], BF16)
            cend = self.sb(p, "cend", [128, 2], F32)
            cendi = self.sb(p, "cendi", [128, 2], I32)
            m64 = self.sb(p, "m64", [128, 64], F32)
            m64i = self.sb(p, "m64i", [128, 64], I32)
            kposc = self.sb(p, "kposc", [128, 64], F32)
            kposci = self.sb(p, "kposci", [128, 32], I32)
            mc = self.sb(p, "mc", [128, 2, 128], BF16)
            ec = self.sb(p, "ec", [128, 2, 128], BF16)
            imp = self.sb(p, "imp", [128, 2, 64], F32)
            t64 = self.sb(p, "t64", [128, 4, 64], F32)
            m8 = self.sb(p, "m8", [128, 16], F32)
            sel = self.sb(p, "sel", [128, 64], BF16)
            selT = self.sb(p, "selT", [64, 2, 512], BF16)
            fin = self.sb(p, "bfin", [128, 8], F32)
            yo = [self.sb(p, f"byo{i}", [128, 1024], F32) for i in range(4)]
            yob = self.sb(p, "byob", [128, 1024], BF16)
            msk = self.sb(p, "bmsk", [128, 32, 512], BF16)
            mtmp = self.sb(p, "bmtmp", [128, 512], BF16)
            kTs = [self.sb(p, f"bkT{i}", [128, L], BF16) for i in range(2)]
            Vs = [self.sb(p, f"bV{i}", [128, 32, 129], BF16) for i in range(2)]
            exps = [self.sb(p, f"bexp{i}", [128, 512], BF16) for i in range(6)]
            self.expslot = 0
            for h in range(8):
                self.dma("sp", bqT[:, h, :], S["bqT"][h], [B("scr")], [B("bqT")])
            self.dma("sp", bg[:], S["bg"].rearrange("(t p) h -> p t h", p=128), [B("scr")], [B("bg")])
            self.ms("pool", E[:], 1.0, [B("E")])
            fw.op("pool", lambda e: e.affine_select(E[:], E[:], pattern=[[1, L]], compare_op=ALU.is_ge, fill=0.0, base=0,
                                                    channel_multiplier=-64), [B("E")], [B("E")])
            fw.op("pool", lambda e: e.affine_select(E[:], E[:], pattern=[[-1, L]], compare_op=ALU.is_ge, fill=0.0, base=63,
                                                    channel_multiplier=64), [B("E")], [B("E")])
            fw.op("pool", lambda e: e.iota(cendi[:], pattern=[[2048, 2]], base=31, channel_multiplier=16), [], [B("cendi")])
            self.cp("dve", cend[:], cendi[:], [B("cendi")], [B("cend")])
            fw.op("pool", lambda e: e.iota(m64i[:], pattern=[[64, 64]], base=0, channel_multiplier=0), [], [B("m64i")])
            self.cp("dve", m64[:], m64i[:], [B("m64i")], [B("m64")])
            fw.op("pool", lambda e: e.iota(kposci[:], pattern=[[128, 32]], base=0, channel_multiplier=1), [], [B("kposci")])
            self.cp("dve", kposc[:, 0:32], kposci[:], [B("kposci")], [B("kposc")])
            self.ts("dve", kposc[:, 32:64], kposc[:, 0:32], 512.0, None, ALU.add, None, [B("kposc")], [B("kposc")])
            for i in range(2):
                self.ms("dve", Vs[i][:, :, 128:129], 1.0, [B("bV", i)])

            for slot in range(2):
                nch = 16 if slot == 0 else 32
                Sk = nch * 128
                qB = self.qposB[:, slot * 512:(slot + 1) * 512]
                for qt4 in range(4):
                    self.ms("dve", yo[qt4][:], 0.0, [B("byo", qt4)])
                for qt4 in range(4):
                    qt = slot * 4 + qt4
                    qcol = self.POSq[:, qt:qt + 1]
                    qBt = self.qposB[:, qt * 128:(qt + 1) * 128]
                    for c in range(2):
                        self.ts("dve", mc[:, c, :], qBt, cend[:, c:c + 1], None, ALU.is_ge, None, [B("qposB"), B("cend")], [B("mc")])
                    for g in range(2):
                        self.ms("dve", imp[:, g, :], 0.0, [B("imp", g)])
                    for h in range(8):
                        g = h // 4
                        pt, pb = self.psum(h % 3)
                        for c in range(2):
                            self.mm(pt[:, c * 128:(c + 1) * 128], self.kcT[:, g, c * 128:(c + 1) * 128], bqT[:, h, qt * 128:(qt + 1) * 128],
                                    True, True, [B("kcT"), B("bqT")], [pb])
                        self.act(ec[:].rearrange("p a b -> p (a b)"), pt[:, 0:256], AF.Exp, [pb], [B("ec")], scale=sc)
                        self.tt("dve", ec[:], ec[:], mc[:], ALU.mult, [B("ec"), B("mc")], [B("ec")])
                        po, pob = self.psum(3)
                        for c in range(2):
                            self.mm(po[:, 0:193], ec[:, c, :], self.cmpR[:, g, c, :], c == 0, c == 1, [B("ec"), B("cmpR")], [pob])
                        self.ts("dve", fin[:, 0:1], po[:, 128:129], 1e-30, None, ALU.max, None, [pob], [B("bfin")])
                        self.recip(fin[:, 1:2], fin[:, 0:1], [B("bfin")], [B("bfin")])
                        self.tt("dve", fin[:, 2:3], fin[:, 1:2], bg[:, qt, h * 3:h * 3 + 1], ALU.mult, [B("bfin"), B("bg")], [B("bfin")])
                        self.ts("dve", yo[qt4][:, h * 128:(h + 1) * 128], po[:, 0:128], fin[:, 2:3], None, ALU.mult, None,
                                [pob, B("bfin")], [B("byo", qt4)])
                        self.stt(imp[:, g, :], po[:, 129:193], fin[:, 1:2], imp[:, g, :], ALU.mult, ALU.add,
                                 [pob, B("bfin"), B("imp", g)], [B("imp", g)])
                    for g in range(2):
                        bt = B("t64")
                        self.ts("dve", t64[:, 0, :], m64[:], qcol, None, ALU.is_le, None, [B("m64"), B("POSq")], [bt])
                        self.ts("dve", t64[:, 1, :], m64[:], 128.0, qcol, ALU.add, ALU.is_gt, [B("m64"), B("POSq")], [bt])
                        self.tt("dve", t64[:, 1, :], t64[:, 1, :], t64[:, 0, :], ALU.mult, [bt], [bt])
                        self.ts("dve", t64[:, 2, :], m64[:], 0.0, None, ALU.is_equal, None, [B("m64")], [bt])
                        self.tt("dve", t64[:, 1, :], t64[:, 1, :], t64[:, 2, :], ALU.max, [bt], [bt])
                        self.stt(t64[:, 3, :], t64[:, 1, :], 1e6, imp[:, g, :], ALU.mult, ALU.add, [bt, B("imp", g)], [bt])
                        self.ts("dve", t64[:, 2, :], t64[:, 0, :], -1.0, -NEG, ALU.add, ALU.mult, [bt], [bt])
                        self.tt("dve", t64[:, 3, :], t64[:, 3, :], t64[:, 2, :], ALU.add, [bt], [bt])
                        fw.op("dve", lambda e: e.max(m8[:, 0:8], t64[:, 3, :]), [bt], [B("m8")])
                        fw.op("dve", lambda e: e.match_replace(t64[:, 2, :], m8[:, 0:8], t64[:, 3, :], -3e38), [bt, B("m8")], [bt])
                        fw.op("dve", lambda e: e.max(m8[:, 8:16], t64[:, 2, :]), [bt], [B("m8")])
                        self.ts("dve", sel[:], t64[:, 3, :], m8[:, 15:16], None, ALU.is_ge, None, [bt, B("m8")], [B("sel")])
                        pt, pb = self.psum(g)
                        ptb = pt[:].bitcast(BF16)
                        self.tr(ptb[0:64, 0:128], sel[:], [B("sel")], [pb])
                        self.cp("dve", selT[:, g, qt4 * 128:(qt4 + 1) * 128], ptb[0:64, 0:128], [pb], [B("selT", g)])
                for g in range(2):
                    k = g % 2
                    self.dma("sp", kTs[k][:, 0:Sk], S["G_bK"][(2 + g) * 128:(3 + g) * 128, 0:Sk], [B("scr")], [B("bkT", k)])
                    self.dma("sp", Vs[k][:, 0:nch, 0:128], S["G_bV"][(2 + g) * L:(2 + g) * L + Sk, :].rearrange("(c p) d -> p c d", p=128),
                             [B("scr")], [B("bV", k)])
                    for kc in range(nch):
                        pt, pb = self.psum(3)
                        self.mm(pt[:, :], E[:, kc * 128:(kc + 1) * 128], selT[:, g, :], True, True, [B("E"), B("selT", g)], [pb])
                        if slot == 1 and kc < 16:
                            self.cp("act", msk[:, kc, :], pt[:, :], [pb], [B("bmsk", kc)])
                        else:
                            self.ts("dve", mtmp[:], qB, kposc[:, kc:kc + 1], None, ALU.is_ge, None, [B("qposB"), B("kposc")], [B("bmtmp")])
                            self.tt("dve", msk[:, kc, :], pt[:, :], mtmp[:], ALU.mult, [pb, B("bmtmp")], [B("bmsk", kc)])
                    for h4 in range(4):
                        h = g * 4 + h4
                        self.att_head(p, kTs[k], B("bkT", k), nch, bqT[:, h, slot * 512:(slot + 1) * 512], B("bqT"),
                                      lambda kc, k=k: (Vs[k][:, kc, :], B("bV", k)), 128,
                                      lambda kc: (msk[:, kc, :], B("bmsk", kc)), sc, [4, 5, 6, 7], "b", exps)
                        self.b_finish(slot, h, 1, fin, bg, yo)
                c0, c1 = (0, 16) if slot == 0 else (12, 32)
                nw = c1 - c0
                for g in range(2):
                    k = g % 2
                    self.dma("sp", kTs[k][:, 0:nw * 128], S["kwT"][g, :, c0 * 128:c1 * 128], [B("scr")], [B("bkT", k)])
                    self.dma("sp", Vs[k][:, 0:nw, 0:128],
                             S["vw"][c0 * 128:c1 * 128, g * 128:(g + 1) * 128].rearrange("(c p) d -> p c d", p=128),
                             [B("scr")], [B("bV", k)])
                    if g == 0:
                        for kc in range(nw):
                            self.ts("dve", mtmp[:], qB, kposc[:, c0 + kc:c0 + kc + 1], None, ALU.is_ge, None,
                                    [B("qposB"), B("kposc")], [B("bmtmp")])
                            self.stt(msk[:, kc, :], qB, kposc[:, 32 + c0 + kc:33 + c0 + kc], mtmp[:], ALU.is_lt, ALU.mult,
                                     [B("qposB"), B("kposc"), B("bmtmp")], [B("bmsk", kc)])
                    for h4 in range(4):
                        h = g * 4 + h4
                        self.att_head(p, kTs[k], B("bkT", k), nw, bqT[:, h, slot * 512:(slot + 1) * 512], B("bqT"),
                                      lambda kc, k=k: (Vs[k][:, kc, :], B("bV", k)), 128,
                                      lambda kc: (msk[:, kc, :], B("bmsk", kc)), sc, [4, 5, 6, 7], "b", exps)
                        self.b_finish(slot, h, 2, fin, bg, yo)
                for qt4 in range(4):
                    qt = slot * 4 + qt4
                    self.cp("dve", yob[:], yo[qt4][:], [B("byo", qt4)], [B("byob")])
                    self.dma("sp", S["y"][qt * 128:(qt + 1) * 128, 1024:2048], yob[:], [B("byob")], [B("s_y")])
            fw.fence()

    def b_finish(self, slot, h, br, fin, bg, yo):
        B = self.fw.B
        for qt4 in range(4):
            qt = slot * 4 + qt4
            po, pob = self.psum(4 + qt4)
            self.ts("dve", fin[:, 0:1], po[:, 128:129], 1e-30, None, ALU.max, None, [pob], [B("bfin")])
            self.recip(fin[:, 1:2], fin[:, 0:1], [B("bfin")], [B("bfin")])
            self.tt("dve", fin[:, 2:3], fin[:, 1:2], bg[:, qt, h * 3 + br:h * 3 + br + 1], ALU.mult, [B("bfin"), B("bg")], [B("bfin")])
            self.stt(yo[qt4][:, h * 128:(h + 1) * 128], po[:, 0:128], fin[:, 2:3], yo[qt4][:, h * 128:(h + 1) * 128],
                     ALU.mult, ALU.add, [pob, B("bfin"), B("byo", qt4)], [B("byo", qt4)])

    def phase_att_c(self):
        fw, I, S, B = self.fw, self.I, self.S, self.fw.B
        sc = 128 ** -0.5
        with ExitStack() as p:
            cqT = self.sb(p, "cqT", [128, 8, NOWN], BF16)
            kposc = self.sb(p, "ckposc", [128, 32], F32)
            kposci = self.sb(p, "ckposci", [128, 32], I32)
            msk = self.sb(p, "cmsk", [128, 32, 512], BF16)
            kTs = [self.sb(p, f"ckT{i}", [128, L], BF16) for i in range(2)]
            Vs = [self.sb(p, f"cV{i}", [128, 32, 257], BF16) for i in range(2)]
            exps = [self.sb(p, f"cexp{i}", [128, 512], BF16) for i in range(6)]
            o0 = [self.sb(p, f"co0{i}", [128, 256], F32) for i in range(4)]
            dd = self.sb(p, "cdd", [128, 256], F32)
            junk = self.sb(p, "cjunk", [128, 256], F32)
            gB = self.sb(p, "cgB", [128, 256], F32)
            fin = self.sb(p, "cfin", [128, 8], F32)
            yo = [self.sb(p, f"cyo{i}", [128, 1024], BF16) for i in range(4)]
            self.expslot = 0
            for h in range(8):
                self.dma("sp", cqT[:, h, :], S["cqT"][h], [B("scr")], [B("cqT")])
            self.dma("sp", gB[:], I["c_subln_g"].partition_broadcast(128), [], [B("cgB")])
            fw.op("pool", lambda e: e.iota(kposci[:], pattern=[[128, 32]], base=0, channel_multiplier=1), [], [B("ckposci")])
            self.cp("dve", kposc[:], kposci[:], [B("ckposci")], [B("ckposc")])
            for i in range(2):
                self.ms("dve", Vs[i][:, :, 256:257], 1.0, [B("cV", i)])
            for slot in range(2):
                nch = 16 if slot == 0 else 32
                Sk = nch * 128
                qB = self.qposB[:, slot * 512:(slot + 1) * 512]
                vis = 0 if slot == 0 else 16
                for kc in range(vis, nch):
                    self.ts("dve", msk[:, kc, :], qB, kposc[:, kc:kc + 1], None, ALU.is_ge, None, [B("qposB"), B("ckposc")], [B("cmsk", kc)])
                for h in range(4):
                    vk = h % 2
                    for hf in range(Sk // 2048):
                        self.dma("sp", Vs[vk][:, hf * 16:(hf + 1) * 16, 0:256],
                                 S["G_cv"][hf][h * 2048:(h + 1) * 2048, :].rearrange("(c p) d -> p c d", p=128),
                                 [B("scr")], [B("cV", vk)])
                    for m in range(2):
                        hm = h * 2 + m
                        k = hm % 2
                        self.dma("sp", kTs[k][:, 0:Sk], S["G_ckT"][m][h * 128:(h + 1) * 128, 0:Sk], [B("scr")], [B("ckT", k)])
                        self.att_head(p, kTs[k], B("ckT", k), nch, cqT[:, hm, slot * 512:(slot + 1) * 512], B("cqT"),
                                      lambda kc, vk=vk: (Vs[vk][:, kc, :], B("cV", vk)), 256,
                                      lambda kc, vis=vis: ((msk[:, kc, :], B("cmsk", kc)) if kc >= vis else None), sc, [4, 5, 6, 7], "c", exps)
                        for qt4 in range(4):
                            po, pob = self.psum(4 + qt4)
                            self.ts("dve", fin[:, 0:1], po[:, 256:257], 1e-30, None, ALU.max, None, [pob], [B("cfin")])
                            self.recip(fin[:, 1:2], fin[:, 0:1], [B("cfin")], [B("cfin")])
                            if m == 0:
                                self.ts("dve", o0[qt4][:], po[:, 0:256], fin[:, 1:2], None, ALU.mult, None,
                                        [pob, B("cfin")], [B("co0", qt4)])
                            else:
                                self.tt("dve", fin[:, 2:3], fin[:, 1:2], self.lamv[:, 0:1], ALU.mult, [B("cfin"), B("lamv")], [B("cfin")])
                                self.stt(dd[:], po[:, 0:256], fin[:, 2:3], o0[qt4][:], ALU.mult, ALU.add,
                                         [pob, B("cfin"), B("co0", qt4)], [B("cdd")])
                                self.act(junk[:], dd[:], AF.Square, [B("cdd")], [B("cjunk"), B("cfin")], accum=fin[:, 3:4])
                                self.act(fin[:, 4:5], fin[:, 3:4], AF.Sqrt, [B("cfin")], [B("cfin")], bias=self.epsc[:, 1:2], scale=1.0 / 256)
                                self.recip(fin[:, 5:6], fin[:, 4:5], [B("cfin")], [B("cfin")])
                                self.tt("dve", fin[:, 6:7], fin[:, 5:6], self.lamv[:, 1:2], ALU.mult, [B("cfin"), B("lamv")], [B("cfin")])
                                self.stt(yo[qt4][:, h * 256:(h + 1) * 256], dd[:], fin[:, 6:7], gB[:], ALU.mult, ALU.mult,
                                         [B("cdd"), B("cfin"), B("cgB")], [B("cyo", qt4)])
                for qt4 in range(4):
                    qt = slot * 4 + qt4
                    self.dma("sp", S["y"][qt * 128:(qt + 1) * 128, 2048:3072], yo[qt4][:], [B("cyo", qt4)], [B("s_y")])
            fw.fence()

    def phase_merge(self):
        fw, I, S, B = self.fw, self.I, self.S, self.fw.B
        with ExitStack() as p:
          mg = self.sb(p, "mg", [128, 16, NOWN], BF16)
          with ExitStack() as p:
            yT = self.sb(p, "yT", [128, 24, NOWN], BF16)
            uT = self.sb(p, "muT", [128, 16, NOWN], BF16)
            for kc in range(16):
                self.dma("sp", uT[:, kc, :], S["uT"][kc], [B("s_uT")], [B("muT")])
            with ExitStack() as p1:
                yt = [self.sb(p1, f"yt{i}", [128, 3072], BF16) for i in range(2)]
                for tt in range(8):
                    k = tt % 2
                    self.dma("sp", yt[k][:], S["y"][tt * 128:(tt + 1) * 128, :], [B("s_y")], [B("yt", k)])
                    for c4 in range(6):
                        pt, pb = self.psum(c4 % 4)
                        ptb = pt[:].bitcast(BF16)
                        for j in range(4):
                            c = c4 * 4 + j
                            self.tr(ptb[:, j * 128:(j + 1) * 128], yt[k][:, c * 128:(c + 1) * 128], [B("yt", k)], [pb])
                        self.cp("act" if c4 % 2 else "dve", yT[:, c4 * 4:(c4 + 1) * 4, tt * 128:(tt + 1) * 128],
                                ptb[:, 0:512].rearrange("p (c t) -> p c t", t=128), [pb], [B("yT", tt)])
                fw.fence()
            yTb = [B("yT", tt) for tt in range(8)]
            with ExitStack() as p2:
                wbr = [self.sb(p2, f"wbr{i}", [128, 8, 512], BF16) for i in range(2)]
                wg = [self.sb(p2, f"wg{i}", [128, 16, 512], BF16) for i in range(2)]
                gs = [self.sb(p2, f"gs{i}", [128, 512], F32) for i in range(2)]
                ma = self.sb(p2, "ma", [128, 512], F32)
                mt = self.sb(p2, "mt", [128, 512], F32)
                wiv = I["w_in"].rearrange("(kc p) n -> p kc n", p=128)
                n = 0
                for dg in range(4):
                    for r in range(3):
                        k = n % 2
                        n += 1
                        self.dma("pool", wbr[k][:], I["w_br"][r].rearrange("(kc p) n -> p kc n", p=128)[:, :, dg * 512:(dg + 1) * 512],
                                 [], [B("wbr", k)])
                        self.dma("pool", wg[k][:], wiv[:, :, O_GL + r * 2048 + dg * 512:O_GL + r * 2048 + (dg + 1) * 512],
                                 [], [B("wg", k)])
                        for dc in range(4):
                            for half in range(2):
                                tsl = slice(half * 512, (half + 1) * 512)
                                pg, pgb = self.psum(0 + (dc * 2 + half) % 2)
                                for kc in range(16):
                                    self.mm(pg[:, :], wg[k][:, kc, dc * 128:(dc + 1) * 128], uT[:, kc, tsl], kc == 0, kc == 15,
                                            [B("wg", k), B("muT")], [pgb])
                                gk = (dc * 2 + half) % 2
                                self.act(gs[gk][:], pg[:, :], AF.Sigmoid, [pgb], [B("gs", gk)])
                                pbr, pbrb = self.psum(2 + (dc * 2 + half) % 2)
                                for kc in range(8):
                                    self.mm(pbr[:, :], wbr[k][:, kc, dc * 128:(dc + 1) * 128], yT[:, r * 8 + kc, tsl], kc == 0, kc == 7,
                                            [B("wbr", k)] + yTb[half * 4:(half + 1) * 4], [pbrb])
                                dst = mg[:, dg * 4 + dc, tsl]
                                db = B("mg", dg * 4 + dc, half)
                                if r == 0:
                                    self.tt("dve", dst, pbr[:, :], gs[gk][:], ALU.mult, [pbrb, B("gs", gk)], [db])
                                else:
                                    self.tt("dve", mt[:], pbr[:, :], gs[gk][:], ALU.mult, [pbrb, B("gs", gk)], [B("mt")])
                                    self.tt("dve", dst, dst, mt[:], ALU.add, [db, B("mt")], [db])
                fw.fence()
          fw.fence()
          mgb = lambda tt: [B("mg", c, tt // 4) for c in range(16)]
          with ExitStack() as p3:
              self.dense_out_ln(p3, mg, mgb, 16, I["w_o"], 2,
                                (lambda tt: I["xo"][tt * 128:(tt + 1) * 128, :]) if self.layer == self.layers[0] else
                                (lambda tt: S["xown1"][tt * 128:(tt + 1) * 128, :]), 0,
                                lambda tt: S["x1"][tt * 128:(tt + 1) * 128, :], B("s_x1"), "mo")
          fw.fence()

    def dense_out_ln(self, p, actT, actb, nk, w_dram, g_part, x_src, ln_idx, dst_fn, dstb, tag, halves=1, exchange=False):
        fw, I, S, B = self.fw, self.I, self.S, self.fw.B
        gB = self.sb(p, "gB", [128, D], F32)
        lg = self.sb(p, "lg", [128, D], F32)
        lb = self.sb(p, "lb", [128, D], F32)
        self.dma("sp", gB[:], S["mod"][g_part * D:(g_part + 1) * D].partition_broadcast(128), [B("s_mod")], [B(tag, "gB")])
        self.dma("sp", lg[:], I["ln_g"][ln_idx].partition_broadcast(128), [], [B(tag, "lg")])
        self.dma("sp", lb[:], I["ln_b"][ln_idx].partition_broadcast(128), [], [B(tag, "lb")])
        ntt = 8 // halves
        vb = self.sb(p, "vb", [128, ntt, D], F32)
        NW = 256 if nk > 16 else 512
        ws = [self.sb(p, f"wo{i}", [128, nk, NW], BF16) for i in range(2)]
        zt = self.sb(p, "zt", [128, 512], F32)
        stt_ = self.sb(p, "ost", [128, 4, 6], F32)
        mv = self.sb(p, "omv", [128, 4], F32)
        wv = w_dram.rearrange("(kc p) n -> p kc n", p=128)
        n = 0
        for hf in range(halves):
            for j in range(ntt):
                tt = hf * ntt + j
                self.dma("sp", vb[:, j, :], x_src(tt), [], [B(tag, "vb", j)])
            for ng in range(D // NW):
                k = n % 2
                n += 1
                self.dma("pool", ws[k][:], wv[:, :, ng * NW:(ng + 1) * NW], [], [B(tag, "w", k)])
                for j in range(ntt):
                    tt = hf * ntt + j
                    pt, pb = self.psum(4 + j % 4)
                    for kc in range(nk):
                        self.mm(pt[:, 0:NW], actT[:, kc, tt * 128:(tt + 1) * 128], ws[k][:, kc, :], kc == 0, kc == nk - 1,
                                [B(tag, "w", k)] + actb(tt), [pb])
                    csl = slice(ng * NW, (ng + 1) * NW)
                    self.tt("dve", zt[:, 0:NW], pt[:, 0:NW], gB[:, csl], ALU.mult, [pb, B(tag, "gB")], [B(tag, "zt")])
                    self.stt(vb[:, j, csl], vb[:, j, csl], ALPHA, zt[:, 0:NW], ALU.mult, ALU.add,
                             [B(tag, "vb", j), B(tag, "zt")], [B(tag, "vb", j)])
            for j in range(ntt):
                tt = hf * ntt + j
                vbj = B(tag, "vb", j)
                for c in range(4):
                    fw.op("dve", lambda e, c=c, j=j: e.bn_stats(stt_[:, c, :], vb[:, j, c * 512:(c + 1) * 512]), [vbj], [B(tag, "st")])
                fw.op("dve", lambda e: e.bn_aggr(mv[:, 0:2], stt_[:].rearrange("p a b -> p (a b)")), [B(tag, "st")], [B(tag, "mv")])
                self.act(mv[:, 2:3], mv[:, 1:2], AF.Sqrt, [B(tag, "mv")], [B(tag, "mv")], bias=self.epsc[:, 0:1])
                self.recip(mv[:, 3:4], mv[:, 2:3], [B(tag, "mv")], [B(tag, "mv")])
                self.ts("dve", vb[:, j, :], vb[:, j, :], mv[:, 0:1], mv[:, 3:4], ALU.subtract, ALU.mult, [vbj, B(tag, "mv")], [vbj])
                self.tt("dve", vb[:, j, :], vb[:, j, :], lg[:], ALU.mult, [vbj, B(tag, "lg")], [vbj])
                self.tt("dve", vb[:, j, :], vb[:, j, :], lb[:], ALU.add, [vbj, B(tag, "lb")], [vbj])
                if exchange:
                    xb_ = B("xown1", tt)
                    self.fw.dma("sp", dst_fn(tt), vb[:, j, :], [vbj], [xb_])
                    xo1, G = self.S["xown1"], self.S["G"]
                    self.fw.coll("pool", lambda e, tt=tt: e.collective_compute(
                        "AllGather", ALU.bypass, replica_groups=[[0, 1, 2, 3], [4, 5, 6, 7]],
                        ins=[xo1[tt * 128:(tt + 1) * 128, :]], outs=[G[tt]]), [xb_], [B("G")])
                else:
                    self.dma("sp", dst_fn(tt), vb[:, j, :], [vbj], [dstb])

    def phase_ffn(self, xout, exchange=False):
        fw, I, S, B = self.fw, self.I, self.S, self.fw.B
        with ExitStack() as p:
            hT = self.sb(p, "hT", [128, 44, NOWN], BF16)
            with ExitStack() as p1:
                u2 = self.sb(p1, "u2T", [128, 16, NOWN], BF16)
                with ExitStack() as pl:
                    self.ln_to_uT(pl, lambda i: S["x1"][i * 128:(i + 1) * 128, :], 8, u2, lambda kc, grp: B("u2T", grp), 2, "ln2")
                    fw.fence()
                wgt = [self.sb(p1, f"fwg{i}", [128, 16, 256], BF16) for i in range(2)]
                wup = [self.sb(p1, f"fwu{i}", [128, 16, 256], BF16) for i in range(2)]
                sg = [self.sb(p1, f"fsg{i}", [128, 512], F32) for i in range(2)]
                wv = I["w_ffn_in"].rearrange("(kc p) n -> p kc n", p=128)
                for fg in range(22):
                    k = fg % 2
                    self.dma("pool", wgt[k][:], wv[:, :, fg * 256:(fg + 1) * 256], [], [B("fwg", k)])
                    self.dma("pool", wup[k][:], wv[:, :, DFF + fg * 256:DFF + (fg + 1) * 256], [], [B("fwu", k)])
                    for dc in range(2):
                        for half in range(2):
                            tsl = slice(half * 512, (half + 1) * 512)
                            ub = [B("u2T", half)]
                            pg, pgb = self.psum((dc * 2 + half) % 2)
                            for kc in range(16):
                                self.mm(pg[:, :], wgt[k][:, kc, dc * 128:(dc + 1) * 128], u2[:, kc, tsl], kc == 0, kc == 15,
                                        [B("fwg", k)] + ub, [pgb])
                            pu, pub = self.psum(2 + (dc * 2 + half) % 2)
                            for kc in range(16):
                                self.mm(pu[:, :], wup[k][:, kc, dc * 128:(dc + 1) * 128], u2[:, kc, tsl], kc == 0, kc == 15,
                                        [B("fwu", k)] + ub, [pub])
                            sk = (dc * 2 + half) % 2
                            self.act(sg[sk][:], pg[:, :], AF.Silu, [pgb], [B("fsg", sk)])
                            self.tt("dve", hT[:, fg * 2 + dc, tsl], pu[:, :], sg[sk][:], ALU.mult, [pub, B("fsg", sk)],
                                    [B("hT", fg * 2 + dc, half)])
                fw.fence()
            hb = lambda tt: [B("hT", c, tt // 4) for c in range(44)]
            with ExitStack() as p3:
                self.dense_out_ln(p3, hT, hb, 44, I["w_ffn_out"], 5, lambda tt: S["x1"][tt * 128:(tt + 1) * 128, :], 1,
                                  lambda tt: xout[tt * 128:(tt + 1) * 128, :], B("xout"), "fo", halves=2, exchange=exchange)
            fw.fence()


_PROG_CACHE = {}


def get_prog(debug=False, phases=None, layers=(0, 1)):
    key = (debug, tuple(phases) if phases else None, tuple(layers))
    if key not in _PROG_CACHE:
        pr = Prog(debug=debug, phases=phases, layers=layers)
        _PROG_CACHE[key] = (pr, pr.build())
    return _PROG_CACHE[key]


def core_tokens(j):
    a = np.arange(512 * j, 512 * j + 512)
    b = np.arange(512 * (7 - j), 512 * (7 - j) + 512)
    return np.concatenate([a, b])


def _grow(T):
    return T * 1024 if T <= 3 else (7 - T) * 1024 + 512


def make_maps(inputs, layers=(0, 1)):
    f = np.float32
    x = np.asarray(inputs["x"], dtype=f)
    inv16 = (ROPE_THETA ** (-(np.arange(16, dtype=np.float32) * 2.0) / 32)).astype(f)
    inv8 = (ROPE_THETA ** (-(np.arange(8, dtype=np.float32) * 2.0) / 16)).astype(f)
    shared = {"inv16": inv16, "inv8": inv8}
    for l in layers:
        sfx = str(l)
        lam_init = 0.8 - 0.6 * math.exp(-0.3 * l)
        shared.update({
            "laminit" + sfx: np.array([lam_init], f),
            "w_in" + sfx: inputs["w_in"][l],
            "a_lat_g" + sfx: np.ascontiguousarray(inputs["a_lat_g"][l].reshape(4, 128).T),
            "cmp_w1" + sfx: inputs["cmp_w1"][l], "cmp_w2" + sfx: inputs["cmp_w2"][l],
            "cmp_peT" + sfx: np.ascontiguousarray(inputs["cmp_pe"][l].transpose(0, 2, 1)),
            "lam" + sfx: inputs["lam"][l], "c_subln_g" + sfx: inputs["c_subln_g"][l], "w_br" + sfx: inputs["w_br"][l],
            "w_o" + sfx: inputs["w_o"][l], "w_ffn_in" + sfx: inputs["w_ffn_in"][l], "w_ffn_out" + sfx: inputs["w_ffn_out"][l],
            "ln_g" + sfx: inputs["ln_g"][l], "ln_b" + sfx: inputs["ln_b"][l],
        })
    maps = []
    for i in range(8):
        b, j = i // 4, i % 4
        tok = core_tokens(j)
        qpos = tok.astype(f)
        m = dict(shared)
        for l in layers:
            m["w_ada" + str(l)] = np.ascontiguousarray(inputs["w_ada"][l][:, j * 3072:(j + 1) * 3072])
            m["b_ada" + str(l)] = np.ascontiguousarray(inputs["b_ada"][l][j * 3072:(j + 1) * 3072])
            wi, au = inputs["w_in"][l], inputs["a_up"][l]
            br_, g_ = j // 2, j % 2
            ck_ = O_BKV + ((br_ * 2 + 0) * 2 + g_) * 128
            cv_ = O_BKV + ((br_ * 2 + 1) * 2 + g_) * 128
            m["w_kv" + str(l)] = np.ascontiguousarray(np.concatenate([
                wi[:, ck_:ck_ + 128], wi[:, cv_:cv_ + 128], wi[:, O_BKV + 1024:O_BKV + 1536],
                wi[:, O_CK + j * 256:O_CK + (j + 1) * 256], wi[:, O_CV + j * 256:O_CV + (j + 1) * 256]], axis=1))
            m["a_up" + str(l)] = np.ascontiguousarray(np.concatenate([au[:, j * 256:(j + 1) * 256], au[:, 1024 + j * 256:1024 + (j + 1) * 256]], axis=1))
        m.update({
            "xf": x[b], "xo": np.ascontiguousarray(x[b][tok]),
            "qpos": qpos, "qposc": np.ascontiguousarray(qpos.reshape(8, 128).T),
            "ct": np.ascontiguousarray(np.asarray(inputs["c"], dtype=f)[b].reshape(16, 128).T),
        })
        maps.append(m)
    return maps


def kernel(**inputs):
    inputs = {k: np.asarray(v) for k, v in inputs.items()}
    pr, nc = get_prog()
    maps = make_maps(inputs)
    res = run_bass_kernel_spmd(nc, maps, core_ids=list(range(8)))
    out = np.empty((2, L, D), np.float32)
    for i in range(8):
        b, j = i // 4, i % 4
        out[b][core_tokens(j)] = res.results[i]["xout"]
    return out
```
